# Optimizing a Trainium2 kernel written in Bass

```python
import jax, jax.numpy as jnp
from jax import lax
import numpy as np

D_MODEL = 1024
BATCH = 8
SEQ = 2048
DEPTH = 2

CTX_LEN = 256
GRID_W = 64
EXPAND = 2
D_INNER = EXPAND * D_MODEL
HG_KEY_DIM = 128
HG_HEADS = D_INNER // HG_KEY_DIM
HG_VAL_DIM = D_INNER // HG_HEADS
HG_CHUNK = 64
RG_BLOCK = 256
RG_HEADS = D_INNER // RG_BLOCK
RG_CONV_W = 4
RG_C = 8.0
N_HG = (DEPTH + 1) // 2
N_RG = DEPTH // 2
EPS = 1e-6

kernel_name = 'hybrid_hgrn2_rglru_prefix_dit'

F32 = jnp.float32


def _rms(x, g):
    xf = x.astype(F32)
    return xf * lax.rsqrt(jnp.mean(xf * xf, axis=-1, keepdims=True) + EPS) * g.astype(F32)


def _heads(a):
    B, T, _ = a.shape
    return a.reshape(B, T, HG_HEADS, -1).transpose(0, 2, 1, 3)


def _hgrn2_chunk_scan(q, k, logf, v, s0):
    B, H, T, K = q.shape
    V = v.shape[-1]
    n = T // HG_CHUNK

    def chunks(a):
        return a.reshape(B, H, n, HG_CHUNK, a.shape[-1]).transpose(2, 0, 1, 3, 4)

    lower_tri = jnp.tril(jnp.ones((HG_CHUNK, HG_CHUNK), dtype=bool))[:, :, None]

    def step(S, inp):
        qc, kc, gc, vc = inp
        b = jnp.cumsum(gc, axis=2)
        diff = b[:, :, :, None, :] - b[:, :, None, :, :]
        decay = jnp.exp(jnp.where(lower_tri, diff, -jnp.inf))
        scores = jnp.einsum('bhtk,bhsk,bhtsk->bhts', qc, kc, decay)
        o = (jnp.einsum('bhts,bhsv->bhtv', scores, vc)
             + jnp.einsum('bhtk,bhkv->bhtv', qc * jnp.exp(b), S))
        b_last = b[:, :, -1:, :]
        S = (jnp.exp(b_last[:, :, 0, :])[..., None] * S
             + jnp.einsum('bhsk,bhsv->bhkv', kc * jnp.exp(b_last - b), vc))
        return S, o

    S, o = lax.scan(step, s0, (chunks(q), chunks(k), chunks(logf), chunks(v)))
    o = o.transpose(1, 2, 0, 3, 4).reshape(B, H, T, V)
    return o, S


def hgrn2_mixer(h_ctx, h_lat, w_in, lb, norm_g, w_out, need_ctx):
    def prep(hh):
        q, v, zf, zb, g = jnp.split(hh @ w_in, 5, axis=-1)
        q = jax.nn.silu(q.astype(F32))

        def forget(z):
            f = lb + (1.0 - lb) * jax.nn.sigmoid(z.astype(F32))
            return _heads(jnp.log(f)), _heads(1.0 - f)

        return _heads(q), _heads(v.astype(F32)), forget(zf), forget(zb), g

    def run(q, k, logf, v, s0, reverse):
        if reverse:
            q, k, logf, v = (jnp.flip(a, axis=2) for a in (q, k, logf, v))
        o, s = _hgrn2_chunk_scan(q, k, logf, v, s0)
        if reverse:
            o = jnp.flip(o, axis=2)
        return o, s

    def readout(o, g):
        B, H, T, V = o.shape
        o = _rms(o.transpose(0, 2, 1, 3), norm_g.reshape(H, V)).reshape(B, T, H * V)
        o = o * jax.nn.silu(g.astype(F32))
        return o.astype(w_out.dtype) @ w_out

    B = h_lat.shape[0]
    s0 = jnp.zeros((B, HG_HEADS, HG_KEY_DIM, HG_VAL_DIM), F32)
    qc, vc, (gcf, kcf), (gcb, kcb), g_c = prep(h_ctx)
    ocf, sf = run(qc, kcf, gcf, vc, s0, False)
    ocb, sb = run(qc, kcb, gcb, vc, s0, True)
    ql, vl, (glf, klf), (glb, klb), g_l = prep(h_lat)
    olf, _ = run(ql, klf, glf, vl, sf, False)
    olb, _ = run(ql, klb, glb, vl, sb, True)
    y = readout(olf + olb, g_l)
    yc = readout(ocf + ocb, g_c) if need_ctx else None
    return yc, y


def _conv_centred(xb, conv_w, conv_b):
    E = xb.shape[-1]
    y = lax.conv_general_dilated(xb, conv_w[:, None, :].astype(xb.dtype), window_strides=(1,),
                                 padding=[(2, 1)], dimension_numbers=('NWC', 'WIO', 'NWC'),
                                 feature_group_count=E)
    return y + conv_b


def _linear_scan(a, u, h0):
    def combine(l, r):
        return (l[0] * r[0], r[0] * l[1] + r[1])
    A, Bv = lax.associative_scan(combine, (a, u), axis=1)
    return A * h0[:, None, :] + Bv


def rglru_mixer(h_ctx, h_lat, w_in, conv_w, conv_b, w_a, b_a, w_x, b_x, lam, w_out, need_ctx):
    B, T, _ = h_lat.shape
    rows = T // GRID_W
    h_lat = h_lat.reshape(B, rows, GRID_W, -1).transpose(0, 2, 1, 3).reshape(B, T, -1)

    def prep(hh):
        xb, g = jnp.split(hh @ w_in, 2, axis=-1)
        return _conv_centred(xb, conv_w, conv_b).astype(F32), g

    def coeffs(xb, d):
        Bq, Tq, _ = xb.shape
        xh = xb.reshape(Bq, Tq, RG_HEADS, RG_BLOCK)
        r = jax.nn.sigmoid(jnp.einsum('bthi,hij->bthj', xh, w_a[d].astype(F32)).reshape(Bq, Tq, D_INNER)
                           + b_a[d].astype(F32))
        ig = jax.nn.sigmoid(jnp.einsum('bthi,hij->bthj', xh, w_x[d].astype(F32)).reshape(Bq, Tq, D_INNER)
                            + b_x[d].astype(F32))
        log_a = -RG_C * r * jax.nn.softplus(-lam[d].astype(F32))
        a = jnp.exp(log_a)
        u = jnp.sqrt(-jnp.expm1(2.0 * log_a)) * (ig * xb)
        return a, u

    def run(a, u, h0, reverse):
        if reverse:
            a, u = jnp.flip(a, axis=1), jnp.flip(u, axis=1)
        h = _linear_scan(a, u, h0)
        last = h[:, -1]
        if reverse:
            h = jnp.flip(h, axis=1)
        return h, last

    def readout(hsum, g):
        return (hsum * jax.nn.silu(g.astype(F32))).astype(w_out.dtype) @ w_out

    h0 = jnp.zeros((B, D_INNER), F32)
    xc, g_c = prep(h_ctx)
    ycf, hf = run(*coeffs(xc, 0), h0, False)
    ycb, hb = run(*coeffs(xc, 1), h0, True)
    xl, g_l = prep(h_lat)
    ylf, _ = run(*coeffs(xl, 0), hf, False)
    ylb, _ = run(*coeffs(xl, 1), hb, True)
    y = readout(ylf + ylb, g_l)
    y = y.reshape(B, GRID_W, rows, -1).transpose(0, 2, 1, 3).reshape(B, T, -1)
    yc = readout(ycf + ycb, g_c) if need_ctx else None
    return yc, y


def setup_inputs(seed: int = 0) -> dict:
    key = jax.random.key(seed)
    ks = jax.random.split(key, 24)
    n = jax.random.normal
    D, E = D_MODEL, D_INNER
    u = jax.random.uniform(ks[21], (N_RG, 2, E), minval=0.9, maxval=0.999)
    s = u ** (1.0 / RG_C)
    return {
        'x': n(ks[0], (BATCH, SEQ, D), F32),
        'c': n(ks[1], (BATCH, D), F32),
        'ctx': n(ks[2], (BATCH, CTX_LEN, D), F32),
        'c_ctx': n(ks[3], (D,), F32),
        'ada_w': n(ks[4], (DEPTH, D, 3 * D), F32) * (0.5 * D ** -0.5),
        'ada_b': n(ks[5], (DEPTH, 3 * D), F32) * 0.02,
        'norm_g': 1.0 + 0.02 * n(ks[6], (DEPTH, D), F32),
        'final_norm_g': 1.0 + 0.02 * n(ks[7], (D,), F32),
        'hg_w_in': n(ks[8], (N_HG, D, 5 * E), F32) * D ** -0.5,
        'hg_lower_bounds': n(ks[9], (DEPTH + 1, E), F32) * 0.5,
        'hg_norm_g': 1.0 + 0.02 * n(ks[10], (N_HG, E), F32),
        'hg_w_out': n(ks[11], (N_HG, E, D), F32) * E ** -0.5,
        'rg_w_in': n(ks[12], (N_RG, D, 2 * E), F32) * D ** -0.5,
        'rg_conv_w': n(ks[13], (N_RG, RG_CONV_W, E), F32) * RG_CONV_W ** -0.5,
        'rg_conv_b': n(ks[14], (N_RG, E), F32) * 0.02,
        'rg_w_a': n(ks[15], (N_RG, 2, RG_HEADS, RG_BLOCK, RG_BLOCK), F32) * RG_BLOCK ** -0.5,
        'rg_b_a': n(ks[16], (N_RG, 2, E), F32) * 0.02,
        'rg_w_x': n(ks[17], (N_RG, 2, RG_HEADS, RG_BLOCK, RG_BLOCK), F32) * RG_BLOCK ** -0.5,
        'rg_b_x': n(ks[18], (N_RG, 2, E), F32) * 0.02,
        'rg_lambda': jnp.log(s) - jnp.log1p(-s),
        'rg_w_out': n(ks[19], (N_RG, E, D), F32) * E ** -0.5,
    }


def reference(x, c, ctx, c_ctx, ada_w, ada_b, norm_g, final_norm_g, hg_w_in, hg_lower_bounds,
              hg_norm_g, hg_w_out, rg_w_in, rg_conv_w, rg_conv_b, rg_w_a, rg_b_a, rg_w_x, rg_b_x,
              rg_lambda, rg_w_out):
    sc = jax.nn.silu(c)
    scc = jax.nn.silu(c_ctx)
    lb_all = jnp.cumsum(jax.nn.softmax(hg_lower_bounds.astype(F32), axis=0), axis=0)
    for i in range(DEPTH):
        shift, scale, gate = jnp.split(sc @ ada_w[i] + ada_b[i], 3, axis=-1)
        shift_c, scale_c, gate_c = jnp.split(scc @ ada_w[i] + ada_b[i], 3, axis=-1)
        h = (_rms(x, norm_g[i]) * (1.0 + scale[:, None]) + shift[:, None]).astype(x.dtype)
        hc = (_rms(ctx, norm_g[i]) * (1.0 + scale_c) + shift_c).astype(ctx.dtype)
        need_ctx = i < DEPTH - 1
        j = i // 2
        if i % 2 == 0:
            yc, y = hgrn2_mixer(hc, h, hg_w_in[j], lb_all[i], hg_norm_g[j], hg_w_out[j], need_ctx)
        else:
            yc, y = rglru_mixer(hc, h, rg_w_in[j], rg_conv_w[j], rg_conv_b[j], rg_w_a[j], rg_b_a[j],
                                rg_w_x[j], rg_b_x[j], rg_lambda[j], rg_w_out[j], need_ctx)
        x = (x + gate[:, None] * y).astype(x.dtype)
        if need_ctx:
            ctx = (ctx + gate_c * yc).astype(ctx.dtype)
    return _rms(x, final_norm_g).astype(x.dtype)
```

```python
import contextlib
import numpy as np
import concourse.bass as bass
import concourse.mybir as mybir
from concourse.bass_utils import run_bass_kernel_spmd

F32 = mybir.dt.float32
BF16 = mybir.dt.bfloat16
AF = mybir.ActivationFunctionType
ALU = mybir.AluOpType

import os
HEAD_BARRIER = os.environ.get("HEAD_BARRIER", "0") == "1"
NO_PREFETCH = os.environ.get("NO_PREFETCH", "0") == "1"
D = 1024
E = 2048
NT = 2304
NCH = 18
EPS = 1e-6
MID_F = 63
MID_B = 64


class Prog:
    ENGS = ("pe", "act", "dve", "pool", "sp")
    N_DMA_SEMS = {"sp": 12, "pool": 6, "act": 4}

    def __init__(self, nc):
        self.nc = nc
        self.ops = []

    def op(self, eng, fn, reads=(), writes=(), dma=False, barrier=False):
        self.ops.append(dict(eng=eng, fn=fn, reads=tuple(reads), writes=tuple(writes), dma=dma, barrier=barrier))
        return len(self.ops) - 1

    def pe(self, fn, r=(), w=()): return self.op("pe", fn, r, w)
    def act(self, fn, r=(), w=()): return self.op("act", fn, r, w)
    def dve(self, fn, r=(), w=()): return self.op("dve", fn, r, w)
    def pool(self, fn, r=(), w=()): return self.op("pool", fn, r, w)
    def dma(self, fn, r=(), w=(), q="sp"): return self.op(q, fn, r, w, dma=True)

    def barrier(self):
        for e in self.ENGS:
            self.op(e, None, barrier=True)

    def finalize(self, final_keys):
        nc = self.nc
        ops = self.ops
        self.op("sp", None, reads=final_keys)
        n = len(ops)
        last_writer, readers = {}, {}
        deps = [None] * n
        dma_cnt = {q: 0 for q in self.N_DMA_SEMS}
        dma_last_on_sem, dma_sem_of, dma_val_of, dma_semval = {}, {}, {}, {}
        last_on_eng = {}
        for i, o in enumerate(ops):
            d = set()
            if o["barrier"]:
                d.update(last_on_eng.values())
                d.update(dma_last_on_sem.values())
            for k in o["reads"]:
                if k in last_writer:
                    d.add(last_writer[k])
            for k in o["writes"]:
                if k in last_writer:
                    d.add(last_writer[k])
                d.update(readers.get(k, ()))
            if o["dma"]:
                q = o["eng"]
                s = (q, dma_cnt[q] % self.N_DMA_SEMS[q])
                dma_cnt[q] += 1
                if s in dma_last_on_sem:
                    d.add(dma_last_on_sem[s])
                dma_last_on_sem[s] = i
                dma_sem_of[i] = s
                dma_semval[s] = dma_semval.get(s, 0) + 16
                dma_val_of[i] = dma_semval[s]
            elif o["fn"] is not None:
                last_on_eng[o["eng"]] = i
            for k in o["reads"]:
                readers.setdefault(k, []).append(i)
            for k in o["writes"]:
                last_writer[k] = i
                readers[k] = []
            d.discard(i)
            if o["eng"] == "pe":
                d = {j for j in d if ops[j]["eng"] != "pe"}
            deps[i] = d
        needed = set()
        for i in range(n):
            needed.update(deps[i])
        sig = {}
        cnt = {e: 0 for e in self.ENGS}
        for i, o in enumerate(ops):
            if o["dma"] or o["fn"] is None:
                continue
            if i in needed:
                cnt[o["eng"]] += 1
                sig[i] = (("eng", o["eng"]), cnt[o["eng"]])
        for i in dma_sem_of:
            sig[i] = (("dma",) + dma_sem_of[i], dma_val_of[i])
        known = {e: {} for e in self.ENGS}
        clock = [None] * n
        waits = [None] * n
        for i, o in enumerate(ops):
            kn = known[o["eng"]]
            wm = {}
            for j in sorted(deps[i]):
                if j not in sig:
                    continue
                s, v = sig[j]
                if kn.get(s, 0) >= v:
                    continue
                wm[s] = max(wm.get(s, 0), v)
                for s2, v2 in clock[j].items():
                    if kn.get(s2, 0) < v2:
                        kn[s2] = v2
                kn[s] = v
            waits[i] = list(wm.items())
            clock[i] = dict(kn)
        with contextlib.ExitStack() as st:
            sems = {}
            for e in self.ENGS:
                sems[("eng", e)] = st.enter_context(nc.semaphore("s_" + e))
            for q, k in self.N_DMA_SEMS.items():
                for t in range(k):
                    sems[("dma", q, t)] = st.enter_context(nc.semaphore("d_%s%d" % (q, t)))
            block = st.enter_context(nc.Block())

            def make(ename):
                def body(eng):
                    for i, o in enumerate(ops):
                        if o["eng"] != ename:
                            continue
                        for s, v in waits[i]:
                            eng.wait_ge(sems[s], v)
                        if o["fn"] is None:
                            continue
                        ins = o["fn"](eng)
                        if i in sig:
                            ins.then_inc(sems[sig[i][0]], 16 if o["dma"] else 1)
                return body

            block.tensor(make("pe"))
            block.scalar(make("act"))
            block.vector(make("dve"))
            block.gpsimd(make("pool"))
            block.sync(make("sp"))
        self.n_ops = n


class Arena:
    def __init__(self, ap_all, words):
        self.ap = ap_all
        self.words = words
        self.off = 0

    def at(self, off):
        self.off = off

    def f32(self, n):
        v = self.ap[:, self.off:self.off + n]
        self.off += n
        assert self.off <= self.words, ("arena overflow", self.off, self.words)
        return v

    def bf16(self, n):
        w = (n + 1) // 2
        v = self.ap[:, self.off:self.off + w].bitcast(BF16)
        self.off += w
        assert self.off <= self.words, ("arena overflow", self.off, self.words)
        return v


ROW_ADAB = (0, 3072)
ROW_NG = (6144, 7168)
ROW_FNG = 8192
NROWS = 9216


def build_nc(dbg=False, stop_after=None):
    nc = bass.Bass("TRN2", target_bir_lowering=False)
    dt = nc.dram_tensor
    x_d = dt("x", [2048, D], F32, kind="ExternalInput").ap()
    ctx_d = dt("ctx", [256, D], F32, kind="ExternalInput").ap()
    c2_d = dt("c2", [128, 16], F32, kind="ExternalInput").ap()
    adaw_d = dt("adaw", [2, 6, 128, 8, 512], F32, kind="ExternalInput").ap()
    rows_d = dt("rows", [1, NROWS], F32, kind="ExternalInput").ap()
    hgw_d = dt("hgw", [16, 128, 8, 640], F32, kind="ExternalInput").ap()
    hgwo_d = dt("hgwo", [128, 16, D], F32, kind="ExternalInput").ap()
    hgvec_d = dt("hgvec", [128, 4, 16], F32, kind="ExternalInput").ap()
    rgw_d = dt("rgw", [8, 128, 2, 8, 256], F32, kind="ExternalInput").ap()
    rgax_d = dt("rgax", [8, 128, 8, 256], F32, kind="ExternalInput").ap()
    rgwo_d = dt("rgwo", [128, 16, D], F32, kind="ExternalInput").ap()
    rgvec_d = dt("rgvec", [128, 16, 11], F32, kind="ExternalInput").ap()
    out_d = dt("out", [2048, D], F32, kind="ExternalOutput").ap()
    kind1 = "ExternalOutput" if dbg else "Internal"
    x1_d = dt("x1", [2048, D], F32, kind=kind1).ap()
    ctx1_d = dt("ctx1", [256, D], F32, kind=kind1).ap()

    AW = 48800
    with contextlib.ExitStack() as st:
        T = lambda name, shape, dty: st.enter_context(nc.sbuf_tensor(name, shape, dty))
        arena_t = T("arena", [128, AW], F32)
        top_t = T("top", [128, 2048], F32)
        ident = T("ident", [128, 128], BF16)
        identf = T("identf", [128, 128], F32)
        ones_bf = T("ones_bf", [128, 128], BF16)
        ones_f = T("ones_f", [1, 128], F32)
        smask = T("smask", [128, 512], F32)
        m01L = T("m01L", [128, 128], F32)
        m01U = T("m01U", [128, 128], F32)
        small = T("small", [128, 640], F32)
        pbanks = [st.enter_context(nc.psum_tensor("pb%d" % i, [128, 512], F32)) for i in range(8)]

        P = Prog(nc)
        A = Arena(arena_t[:], AW)
        bank_ctr = [0]

        def newbank():
            i = bank_ctr[0] % 8
            bank_ctr[0] += 1
            return pbanks[i], ["pb%ds%d" % (i, q) for q in range(4)]

        pool_ctr = {"proj": 0, "aux": 0}

        def bank_proj():
            i = pool_ctr["proj"] % 5
            pool_ctr["proj"] += 1
            return pbanks[i], ["pb%ds%d" % (i, q) for q in range(4)]

        def bank_aux():
            i = 5 + pool_ctr["aux"] % 3
            pool_ctr["aux"] += 1
            return pbanks[i], ["pb%ds%d" % (i, q) for q in range(4)]

        gate_l = top_t[:, 0:1024]
        gate_c = top_t[:, 1024:2048]

        P.pool(lambda e: e.memset(identf[:], 1.0), w=["identf"])
        P.pool(lambda e: e.affine_select(out=identf[:], in_=identf[:], pattern=[[-1, 128]], compare_op=ALU.is_equal,
                                         fill=0.0, base=0, channel_multiplier=1), r=["identf"], w=["identf"])
        P.dve(lambda e: e.tensor_copy(out=ident[:], in_=identf[:]), r=["identf"], w=["ident"])
        P.pool(lambda e: e.memset(ones_bf[:], 1.0), w=["ones_bf"])
        P.pool(lambda e: e.memset(ones_f[:], 1.0), w=["ones_f"])
        P.pool(lambda e: e.memset(smask[:], 1.0), w=["smask"])
        smv = smask[:].rearrange("p (c j) -> p c j", j=128)
        P.pool(lambda e: e.memset(smv[:, :, 0:1], 0.0), r=["smask"], w=["smask"])
        for (m, cm, pat, key) in ((m01L, -1, 1, "maskL"), (m01U, 1, -1, "maskU")):
            P.pool(lambda e, m=m: e.memset(m[:], 1.0), w=[key])
            P.pool(lambda e, m=m, cm=cm, pat=pat: e.affine_select(
                out=m[:], in_=m[:], pattern=[[pat, 128]], compare_op=ALU.is_ge, fill=0.0, base=0,
                channel_multiplier=cm), r=[key], w=[key])
        MASKS = {"f": (m01L, "maskL"), "b": (m01U, "maskU")}
        for i in range(8):
            P.dve(lambda e, i=i: e.memset(pbanks[i][:], 0.0), w=["pb%ds%d" % (i, q) for q in range(4)])

        c2 = small[:, 0:16]
        sc2 = small[:, 16:32]
        hgv = small[:, 32:96].rearrange("p (a h) -> p a h", h=16)
        lbv = small[:, 96:112]
        omlv = small[:, 112:128]
        tmpv = small[:, 128:176].rearrange("p (a h) -> p a h", h=16)
        ss_t = small[:, 176:184]
        rstd_t = small[:, 184:192]
        AFv = small[:, 192:210]
        BFv = small[:, 210:228]
        ABv = small[:, 228:246]
        BBv = small[:, 246:264]
        RFv = small[:, 264:282]
        RBv = small[:, 282:300]
        TRv = small[:, 300:318]
        tot4 = small[:, 318:322]
        clv = small[:, 322:354].rearrange("p (c d) -> p c d", d=2)
        cl2v = small[:, 354:386].rearrange("p (c d) -> p c d", d=2)
        clhv = small[:, 386:418].rearrange("p (c d) -> p c d", d=2)
        carry = small[:, 418:426]
        nhalf = small[:, 426:427]
        halfc = small[:, 427:428]
        c0v = small[:, 428:444]
        c1v = small[:, 444:460]
        chhv = small[:, 460:492].rearrange("p (c d) -> p c d", d=2)
        clqv = small[:, 492:524].rearrange("p (c d) -> p c d", d=2)
        rgvh = T("rgvh", [128, 16, 4], F32)
        sc2b_t = T("sc2b", [128, 16], BF16)
        rgv = T("rgv", [128, 16, 11], F32)

        P.dma(lambda e: e.dma_start(out=c2, in_=c2_d), w=["c2"])
        P.dma(lambda e: e.dma_start(out=hgv, in_=hgvec_d), w=["hgv"])
        P.dma(lambda e: e.dma_start(out=rgv[:], in_=rgvec_d), w=["rgv"])
        P.act(lambda e: e.activation(out=sc2, in_=c2, func=AF.Silu), r=["c2"], w=["sc2"])
        P.dve(lambda e: e.tensor_copy(out=sc2b_t[:], in_=sc2), r=["sc2"], w=["sc2b"])
        P.dve(lambda e: e.tensor_tensor(out=tmpv[:, 0, :], in0=hgv[:, 0, :], in1=hgv[:, 1, :], op=ALU.max), r=["hgv"], w=["tmpv0"])
        P.dve(lambda e: e.tensor_tensor(out=tmpv[:, 0, :], in0=tmpv[:, 0, :], in1=hgv[:, 2, :], op=ALU.max), r=["hgv", "tmpv0"], w=["tmpv0"])
        P.dve(lambda e: e.tensor_tensor(out=hgv[:, 0:3, :], in0=hgv[:, 0:3, :], in1=tmpv[:, 0:1, :].to_broadcast([128, 3, 16]),
                                        op=ALU.subtract), r=["hgv", "tmpv0"], w=["hgv"])
        P.act(lambda e: e.activation(out=hgv[:, 0:3, :], in_=hgv[:, 0:3, :], func=AF.Exp), r=["hgv"], w=["hgv"])
        P.dve(lambda e: e.tensor_tensor(out=tmpv[:, 1, :], in0=hgv[:, 0, :], in1=hgv[:, 1, :], op=ALU.add), r=["hgv"], w=["tmpv1"])
        P.dve(lambda e: e.tensor_tensor(out=tmpv[:, 1, :], in0=tmpv[:, 1, :], in1=hgv[:, 2, :], op=ALU.add), r=["hgv", "tmpv1"], w=["tmpv1"])
        P.dve(lambda e: e.reciprocal(out=tmpv[:, 2, :], in_=tmpv[:, 1, :]), r=["tmpv1"], w=["tmpv2"])
        P.dve(lambda e: e.tensor_tensor(out=lbv, in0=hgv[:, 0, :], in1=tmpv[:, 2, :], op=ALU.mult), r=["hgv", "tmpv2"], w=["lbv"])
        P.dve(lambda e: e.tensor_scalar(out=omlv, in0=lbv, scalar1=-1.0, scalar2=1.0, op0=ALU.mult, op1=ALU.add), r=["lbv"], w=["omlv"])
        ngv = hgv[:, 3, :]
        P.dve(lambda e: e.tensor_scalar(out=c1v, in0=omlv, scalar1=0.5, scalar2=None, op0=ALU.mult), r=["omlv"], w=["c1v"])
        P.dve(lambda e: e.tensor_scalar(out=c0v, in0=lbv, scalar1=0.5, scalar2=0.5, op0=ALU.mult, op1=ALU.add), r=["lbv"], w=["c0v"])
        P.pool(lambda e: e.memset(nhalf, -0.5), w=["nhalf"])
        P.pool(lambda e: e.memset(halfc, 0.5), w=["halfc"])
        lamv = rgv[:, :, 9:11]
        P.act(lambda e: e.activation(out=clv, in_=lamv, func=AF.Exp, scale=-1.0), r=["rgv"], w=["clv"])
        P.act(lambda e: e.activation(out=clv, in_=clv, func=AF.Ln, bias=1.0), r=["clv"], w=["clv"])
        P.dve(lambda e: e.tensor_scalar(out=cl2v, in0=clv, scalar1=-16.0, scalar2=None, op0=ALU.mult), r=["clv"], w=["cl2v"])
        P.dve(lambda e: e.tensor_scalar(out=clhv, in0=clv, scalar1=8.0, scalar2=None, op0=ALU.mult), r=["clv"], w=["clhv"])
        P.dve(lambda e: e.tensor_scalar(out=clv, in0=clv, scalar1=-8.0, scalar2=None, op0=ALU.mult), r=["clv", "cl2v", "clhv"], w=["clv"])
        P.dve(lambda e: e.tensor_scalar(out=chhv, in0=clhv, scalar1=0.5, scalar2=None, op0=ALU.mult), r=["clhv"], w=["chhv"])
        P.dve(lambda e: e.tensor_scalar(out=clqv, in0=clv, scalar1=0.5, scalar2=None, op0=ALU.mult), r=["clv"], w=["clqv"])
        P.dve(lambda e: e.tensor_scalar(out=rgvh[:], in0=rgv[:, :, 5:9], scalar1=0.5, scalar2=None, op0=ALU.mult), r=["rgv"], w=["rgvh"])

        def phase_mod(layer, scratch_off, bct_off):
            A.at(scratch_off)
            wada = [A.bf16(8 * 512).rearrange("p (k n) -> p k n", n=512) for _ in range(2)]
            modL = A.f32(3072)
            modC = A.f32(3072)
            rows = A.f32(5120)
            gsrow = A.f32(2048)
            L = "L%d" % layer
            P.dma(lambda e: e.dma_start(out=rows[0:1, 0:3072], in_=rows_d[:, ROW_ADAB[layer]:ROW_ADAB[layer] + 3072]), w=[L + "rows"])
            P.dma(lambda e: e.dma_start(out=rows[0:1, 3072:4096], in_=rows_d[:, ROW_NG[layer]:ROW_NG[layer] + 1024]), w=[L + "rowsg"])
            P.dma(lambda e: e.dma_start(out=rows[0:1, 4096:5120], in_=rows_d[:, ROW_FNG:ROW_FNG + 1024]), w=[L + "rowsf"])
            for nb in range(6):
                wb = wada[nb % 2]
                wk = L + "wada%d" % (nb % 2)
                P.dma(lambda e, wb=wb, nb=nb: e.dma_start(out=wb, in_=adaw_d[layer, nb]), w=[wk], q="pool")
                for r, mod in ((0, modL), (1, modC)):
                    bk, bkey = newbank()
                    for kc in range(8):
                        P.pe(lambda e, bk=bk, wb=wb, kc=kc, r=r: e.matmul(
                            bk[0:1, :], lhsT=sc2b_t[:, r * 8 + kc:r * 8 + kc + 1], rhs=wb[:, kc, :],
                            start=(kc == 0), stop=(kc == 7)), r=[wk, "sc2b"], w=bkey)
                    P.dve(lambda e, bk=bk, mod=mod, nb=nb: e.tensor_tensor(
                        out=mod[0:1, nb * 512:(nb + 1) * 512], in0=bk[0:1, :], in1=rows[0:1, nb * 512:(nb + 1) * 512],
                        op=ALU.add), r=bkey + [L + "rows"], w=[L + "mod%d_%d" % (r, nb)])
            modkeys = lambda r, lo, hi: [L + "mod%d_%d" % (r, nb) for nb in range(lo // 512, (hi + 511) // 512)]
            for r, mod in ((0, modL), (1, modC)):
                P.dve(lambda e, mod=mod, r=r: e.scalar_tensor_tensor(
                    out=gsrow[0:1, r * 1024:(r + 1) * 1024], in0=mod[0:1, 1024:2048], scalar=1.0, in1=rows[0:1, 3072:4096],
                    op0=ALU.add, op1=ALU.mult), r=modkeys(r, 1024, 2048) + [L + "rowsg"], w=[L + "gsrow%d" % r])
            A.at(bct_off)
            bct = [A.f32(1024) for _ in range(4)]

            def bcast(dst, src_row, rkeys, wkey):
                for hf in range(2):
                    bk, bkey = newbank()
                    P.pe(lambda e, bk=bk, hf=hf: e.matmul(bk[:, :], lhsT=ones_f[0:1, :], rhs=src_row[0:1, hf * 512:(hf + 1) * 512],
                                                          start=True, stop=True), r=rkeys + ["ones_f"], w=bkey)
                    P.act(lambda e, bk=bk, hf=hf: e.copy(out=dst[:, hf * 512:(hf + 1) * 512], in_=bk[:, :]), r=bkey, w=[wkey + str(hf)])
                return [wkey + "0", wkey + "1"]

            keys = {}
            keys["gsL"] = bcast(bct[0], gsrow[:, 0:1024], [L + "gsrow0"], L + "gsL")
            keys["shL"] = bcast(bct[1], modL[:, 0:1024], modkeys(0, 0, 1024), L + "shL")
            keys["gsC"] = bcast(bct[2], gsrow[:, 1024:2048], [L + "gsrow1"], L + "gsC")
            keys["shC"] = bcast(bct[3], modC[:, 0:1024], modkeys(1, 0, 1024), L + "shC")
            keys["gateL"] = bcast(gate_l, modL[:, 2048:3072], modkeys(0, 2048, 3072), L + "gateL")
            if layer == 0:
                keys["gateC"] = bcast(gate_c, modC[:, 2048:3072], modkeys(1, 2048, 3072), L + "gateC")
            else:
                keys["fng"] = bcast(gate_c, rows[:, 4096:5120], [L + "rowsf"], L + "fng")
            return bct, keys

        def phase_norm(layer, bct, keys, hT, src_tiles, permute_lat):
            L = "L%d" % layer
            xs = [A.f32(1024) for _ in range(2)]
            tmpn = A.f32(1024)
            hb = [A.bf16(1024) for _ in range(2)]
            junk = A.bf16(1024)
            for j in range(18):
                sx = xs[j % 2]
                xk = L + "xs%d" % (j % 2)
                src = src_tiles(j)
                P.dma(lambda e, sx=sx, src=src: e.dma_start(out=sx, in_=src[0]), r=src[1], w=[xk])
                ssj = ss_t[:, (j % 2):(j % 2) + 1]
                rsj = rstd_t[:, (j % 2):(j % 2) + 1]
                sk = L + "ss%d" % (j % 2)
                P.act(lambda e, sx=sx, ssj=ssj: e.activation(out=junk, in_=sx, func=AF.Square, accum_out=ssj), r=[xk], w=["junk", sk])
                P.act(lambda e, ssj=ssj, rsj=rsj: e.activation(out=rsj, in_=ssj, func=AF.Ln, bias=EPS, scale=1.0 / D), r=[sk], w=[sk + "r"])
                P.act(lambda e, rsj=rsj: e.activation(out=rsj, in_=rsj, func=AF.Exp, scale=-0.5), r=[sk + "r"], w=[sk + "r"])
                gs, sh = (bct[2], bct[3]) if j < 2 else (bct[0], bct[1])
                gk, shk = (keys["gsC"], keys["shC"]) if j < 2 else (keys["gsL"], keys["shL"])
                P.dve(lambda e, sx=sx, rsj=rsj, gs=gs: e.scalar_tensor_tensor(out=tmpn, in0=sx, scalar=rsj, in1=gs, op0=ALU.mult, op1=ALU.mult),
                      r=[xk, sk + "r"] + gk, w=[L + "tmpn"])
                hbj = hb[j % 2]
                hk = L + "hb%d" % (j % 2)
                P.dve(lambda e, hbj=hbj, sh=sh: e.tensor_tensor(out=hbj, in0=tmpn, in1=sh, op=ALU.add), r=[L + "tmpn"] + shk, w=[hk])
                for g4 in range(2):
                    bk, bkey = newbank()
                    bkb = bk[:].bitcast(BF16)
                    for q in range(4):
                        kc = g4 * 4 + q
                        P.pe(lambda e, bkb=bkb, q=q, kc=kc, hbj=hbj: e.transpose(bkb[:, q * 128:(q + 1) * 128], hbj[:, kc * 128:(kc + 1) * 128], ident[:]),
                             r=[hk, "ident"], w=bkey)
                    src_v = bkb[:, 0:512].rearrange("p (q c) -> p q c", c=128)
                    if permute_lat and j >= 2:
                        jl = j - 2
                        eng = P.act if g4 == 0 else P.dve
                        for q in range(4):
                            kc = g4 * 4 + q
                            dst = hT[:, kc, 256:2304].rearrange("p (w r) -> p w r", r=32)[:, :, 2 * jl:2 * jl + 2]
                            srcq = bkb[:, q * 128:(q + 1) * 128].rearrange("p (r w) -> p w r", r=2)
                            if g4 == 0:
                                P.act(lambda e, dst=dst, srcq=srcq: e.copy(out=dst, in_=srcq), r=bkey, w=[L + "hT%d_%d" % (j, kc)])
                            else:
                                P.dve(lambda e, dst=dst, srcq=srcq: e.tensor_copy(out=dst, in_=srcq), r=bkey, w=[L + "hT%d_%d" % (j, kc)])
                    else:
                        dst = hT[:, g4 * 4:(g4 + 1) * 4, j * 128:(j + 1) * 128]
                        wk = [L + "hT%d_%d" % (j, g4 * 4 + q) for q in range(4)]
                        if g4 == 0:
                            P.act(lambda e, dst=dst, src_v=src_v: e.copy(out=dst, in_=src_v), r=bkey, w=wk)
                        else:
                            P.dve(lambda e, dst=dst, src_v=src_v: e.tensor_copy(out=dst, in_=src_v), r=bkey, w=wk)

        def hT_keys(layer, c0, c1, permuted):
            L = "L%d" % layer
            if permuted and c1 > 256:
                tiles = set(range(2, 18))
                if c0 < 256:
                    tiles |= set(range(c0 // 128, 2))
            else:
                tiles = set(range(c0 // 128, (c1 + 127) // 128))
            return [L + "hT%d_%d" % (j, kc) for j in sorted(tiles) for kc in range(8)]

        def phase_out(layer, Oall, okeys_for_tile, wo_d, final, ec0=0, nec=16, from_x1=False, tag=""):
            L = "L%d" % layer + tag
            nh2 = nec // 2
            wo = A.bf16(nec * 1024).rearrange("p (k n) -> p k n", n=1024)
            xs = [A.f32(1024) for _ in range(2)]
            tmpo = [A.f32(512) for _ in range(2)]
            junk = A.bf16(1024)
            for half in range(2):
                P.dma(lambda e, half=half: e.dma_start(out=wo[:, half * nh2:(half + 1) * nh2, :], in_=wo_d[:, ec0 + half * nh2:ec0 + (half + 1) * nh2, :]),
                      w=[L + "wo%d" % half], q="pool")
            tiles = range(18) if layer == 0 else range(2, 18)
            for j in tiles:
                sx = xs[j % 2]
                xk = L + "oxs%d" % (j % 2)
                if layer == 0 and not from_x1:
                    src = (ctx_d[j * 128:(j + 1) * 128, :], []) if j < 2 else (x_d[(j - 2) * 128:(j - 1) * 128, :], [])
                elif layer == 0:
                    src = (ctx1_d[j * 128:(j + 1) * 128, :], ["ctx1_%d" % j]) if j < 2 else (x1_d[(j - 2) * 128:(j - 1) * 128, :], ["x1_%d" % (j - 2)])
                else:
                    src = (x1_d[(j - 2) * 128:(j - 1) * 128, :], ["x1_%d" % (j - 2)])
                P.dma(lambda e, sx=sx, src=src: e.dma_start(out=sx, in_=src[0]), r=src[1], w=[xk])
                gt = gate_c if (layer == 0 and j < 2) else gate_l
                L_ = "L%d" % layer
                gk = (L_ + "gateC0", L_ + "gateC1") if (layer == 0 and j < 2) else (L_ + "gateL0", L_ + "gateL1")
                col0 = j * 128 if layer == 0 else (j - 2) * 128
                for half in range(2):
                    bk, bkey = newbank()
                    for ec in range(nec):
                        P.pe(lambda e, bk=bk, ec=ec, half=half, col0=col0: e.matmul(
                            bk[:, :], lhsT=Oall[:, ec, col0:col0 + 128], rhs=wo[:, ec, half * 512:(half + 1) * 512],
                            start=(ec == 0), stop=(ec == nec - 1)), r=okeys_for_tile(j, ec) + [L + "wo%d" % (ec // nh2)], w=bkey)
                    tp = tmpo[half]
                    tk = L + "tmpo%d" % half
                    P.dve(lambda e, bk=bk, tp=tp, gt=gt, half=half: e.tensor_tensor(out=tp, in0=bk[:, :], in1=gt[:, half * 512:(half + 1) * 512], op=ALU.mult),
                          r=bkey + [gk[half]], w=[tk])
                    P.dve(lambda e, sx=sx, tp=tp, half=half: e.tensor_tensor(out=sx[:, half * 512:(half + 1) * 512], in0=tp, in1=sx[:, half * 512:(half + 1) * 512], op=ALU.add),
                          r=[tk, xk], w=[xk])
                if not final:
                    if j < 2:
                        P.dma(lambda e, sx=sx, j=j: e.dma_start(out=ctx1_d[j * 128:(j + 1) * 128, :], in_=sx), r=[xk], w=["ctx1_%d" % j])
                    else:
                        P.dma(lambda e, sx=sx, j=j: e.dma_start(out=x1_d[(j - 2) * 128:(j - 1) * 128, :], in_=sx), r=[xk], w=["x1_%d" % (j - 2)])
                else:
                    ssj = ss_t[:, 2 + (j % 2):3 + (j % 2)]
                    rsj = rstd_t[:, 2 + (j % 2):3 + (j % 2)]
                    sk = L + "oss%d" % (j % 2)
                    P.act(lambda e, sx=sx, ssj=ssj: e.activation(out=junk, in_=sx, func=AF.Square, accum_out=ssj), r=[xk], w=["ojunk", sk])
                    P.act(lambda e, ssj=ssj, rsj=rsj: e.activation(out=rsj, in_=ssj, func=AF.Ln, bias=EPS, scale=1.0 / D), r=[sk], w=[sk + "r"])
                    P.act(lambda e, rsj=rsj: e.activation(out=rsj, in_=rsj, func=AF.Exp, scale=-0.5), r=[sk + "r"], w=[sk + "r"])
                    P.dve(lambda e, sx=sx, rsj=rsj: e.scalar_tensor_tensor(out=sx, in0=sx, scalar=rsj, in1=gate_c, op0=ALU.mult, op1=ALU.mult),
                          r=[xk, sk + "r", L_ + "fng0", L_ + "fng1"], w=[xk])
                    P.dma(lambda e, sx=sx, j=j: e.dma_start(out=out_d[(j - 2) * 128:(j - 1) * 128, :], in_=sx), r=[xk], w=["out_%d" % (j - 2)])

        A.at(0)
        Oall = A.bf16(8 * NT).rearrange("p (h t) -> p h t", t=NT)
        hT = A.bf16(8 * NT).rearrange("p (k t) -> p k t", t=NT)
        heads_off = A.off
        bct, mkeys = phase_mod(0, heads_off + 8768, heads_off)
        phase_norm(0, bct, mkeys, hT,
                   lambda j: ((ctx_d[j * 128:(j + 1) * 128, :], []) if j < 2 else (x_d[(j - 2) * 128:(j - 1) * 128, :], [])),
                   permute_lat=False)
        P.barrier()

        A.at(heads_off)
        Qd = {"f": A.bf16(NT), "b": A.bf16(NT)}
        Kd = {"f": A.bf16(NT), "b": A.bf16(NT)}
        vT = A.bf16(NCH * 128).rearrange("p (n v) -> p n v", v=128)
        KT = {"f": A.bf16(NCH * 128).rearrange("p (n v) -> p n v", v=128),
              "b": A.bf16(NCH * 128).rearrange("p (n v) -> p n v", v=128)}
        Gs = [A.bf16(NT), A.bf16(NT)]
        Sbf = {"f": A.bf16(NCH * 128).rearrange("p (n v) -> p n v", v=128),
               "b": A.bf16(NCH * 128).rearrange("p (n v) -> p n v", v=128)}
        Tst = {"f": [A.f32(128), A.f32(128)], "b": [A.f32(128), A.f32(128)]}
        q32s = [A.f32(512) for _ in range(2)]
        sgts = [{"f": A.f32(512), "b": A.f32(512)} for _ in range(2)]
        lgts = [{"f": A.f32(512), "b": A.f32(512)} for _ in range(2)]
        pfts = [{"f": A.f32(512), "b": A.f32(512)} for _ in range(2)]
        e1s = [A.f32(512) for _ in range(2)]
        e2s = [A.f32(512) for _ in range(2)]
        vfms = [A.bf16(512) for _ in range(2)]
        tot4s = [small[:, 318:322], small[:, 524:528]]
        wbuf = [A.bf16(8 * 640).rearrange("p (k n) -> p k n", n=640)]
        Ps = {"f": [A.bf16(128), A.bf16(128)], "b": [A.bf16(128), A.bf16(128)]}
        osq = A.bf16(512)
        o32 = A.f32(512)
        rt = A.f32(512)
        l0_end = A.off
        global DBG_OFFS
        DBG_OFFS = dict(heads_off=heads_off, l0_end=l0_end)

        TBS = [(0, 512), (512, 512), (1024, 512), (1536, 512), (2048, 256)]
        ORDER = {"f": list(range(18)), "b": [1, 0] + list(range(17, 1, -1))}
        Av = {"f": AFv, "b": ABv}
        Bv = {"f": BFv, "b": BBv}
        Rv = {"f": RFv, "b": RBv}
        NH = 16 if stop_after is None else stop_after

        def emit_pb1(h, bi):
            s0, sz = TBS[bi]
            ncb = sz // 128
            n0 = s0 // 128
            ob, okey = pbanks[5], ["pb5s%d" % q_ for q_ in range(4)]
            for c2_ in range(0, ncb, 2):
                bk, bkeys = pbanks[6], ["pb6s%d" % q_ for q_ in range(4)]
                for cc in range(2):
                    n = n0 + c2_ + cc
                    cs = slice(n * 128, (n + 1) * 128)
                    c0_ = n * 128
                    for di, d in enumerate(("f", "b")):
                        sl = cc * 2 + di
                        o0 = sl * 128
                        rk_ = ["K%s%d" % (d, bi), "Q%s%d" % (d, bi)]
                        if d == "f":
                            P.pe(lambda e, bk=bk, o0=o0, c0_=c0_: e.matmul(bk[:, o0 + 64:o0 + 128], lhsT=Kd["f"][:, c0_:c0_ + 128], rhs=Qd["f"][:, c0_ + 64:c0_ + 128],
                                                                           start=True, stop=True), r=rk_, w=bkeys)
                            P.pe(lambda e, bk=bk, o0=o0, c0_=c0_: e.matmul(bk[0:64, o0:o0 + 64], lhsT=Kd["f"][:, c0_:c0_ + 64], rhs=Qd["f"][:, c0_:c0_ + 64],
                                                                           start=True, stop=True), r=rk_, w=bkeys)
                        else:
                            P.pe(lambda e, bk=bk, o0=o0, c0_=c0_: e.matmul(bk[:, o0:o0 + 64], lhsT=Kd["b"][:, c0_:c0_ + 128], rhs=Qd["b"][:, c0_:c0_ + 64],
                                                                           start=True, stop=True), r=rk_, w=bkeys)
                            P.pe(lambda e, bk=bk, o0=o0, c0_=c0_: e.matmul(bk[64:128, o0 + 64:o0 + 128], lhsT=Kd["b"][:, c0_ + 64:c0_ + 128], rhs=Qd["b"][:, c0_ + 64:c0_ + 128],
                                                                           start=True, stop=True), r=rk_, w=bkeys)
                for cc in range(2):
                    n = n0 + c2_ + cc
                    for di, d in enumerate(("f", "b")):
                        sl = cc * 2 + di
                        mk, mkk = MASKS[d]
                        pst = Ps[d][n % 2]
                        pkey = "Ps%s%d" % (d, n % 2)
                        P.dve(lambda e, bk=bk, sl=sl, mk=mk, pst=pst: e.tensor_tensor(out=pst, in0=bk[:, sl * 128:(sl + 1) * 128], in1=mk[:], op=ALU.mult),
                              r=bkeys + [mkk], w=[pkey])
                for cc in range(2):
                    c = c2_ + cc
                    n = n0 + c
                    cs = slice(n * 128, (n + 1) * 128)
                    oc = ob[:, c * 128:(c + 1) * 128]
                    P.pe(lambda e, oc=oc, n=n: e.matmul(oc, lhsT=vT[:, n, :], rhs=Ps["f"][n % 2], start=True, stop=False),
                         r=["Tv%d" % (n // 4), "Psf%d" % (n % 2)], w=okey)
                    P.pe(lambda e, oc=oc, n=n: e.matmul(oc, lhsT=vT[:, n, :], rhs=Ps["b"][n % 2], start=False, stop=False),
                         r=["Tv%d" % (n // 4), "Psb%d" % (n % 2)], w=okey)
                    P.pe(lambda e, oc=oc, n=n, cs=cs: e.matmul(oc, lhsT=Sbf["f"][:, n, :], rhs=Qd["f"][:, cs], start=False, stop=False),
                         r=["Sbff%d" % n, "Qf%d" % bi], w=okey)
                    P.pe(lambda e, oc=oc, n=n, cs=cs: e.matmul(oc, lhsT=Sbf["b"][:, n, :], rhs=Qd["b"][:, cs], start=False, stop=True),
                         r=["Sbfb%d" % n, "Qb%d" % bi], w=okey)
            return ob, okey

        def emit_pb2(h, bi, ob, okey):
            s0, sz = TBS[bi]
            P.act(lambda e, ob=ob, sz=sz: e.activation(out=osq[:, 0:sz], in_=ob[:, 0:sz], func=AF.Square), r=okey, w=["osq"])
            P.act(lambda e, ob=ob, sz=sz: e.copy(out=o32[:, 0:sz], in_=ob[:, 0:sz]), r=okey, w=["o32"])
            sb_, sskey = pbanks[6], ["pb6s%d" % q_ for q_ in range(4)]
            P.pe(lambda e, sb_=sb_, sz=sz: e.matmul(sb_[:, 0:sz], lhsT=ones_bf[:], rhs=osq[:, 0:sz], start=True, stop=True), r=["osq", "ones_bf"], w=sskey)
            P.act(lambda e, sb_=sb_, sz=sz: e.activation(out=rt[:, 0:sz], in_=sb_[:, 0:sz], func=AF.Ln, bias=EPS, scale=1.0 / 128), r=sskey, w=["rt"])
            P.act(lambda e, sz=sz: e.activation(out=rt[:, 0:sz], in_=rt[:, 0:sz], func=AF.Exp, scale=-0.5), r=["rt"], w=["rt"])
            P.dve(lambda e, sz=sz, h=h: e.scalar_tensor_tensor(out=o32[:, 0:sz], in0=o32[:, 0:sz], scalar=ngv[:, h:h + 1], in1=rt[:, 0:sz], op0=ALU.mult, op1=ALU.mult),
                  r=["o32", "rt", "hgv"], w=["o32"])
            P.dve(lambda e, s0=s0, sz=sz, h=h: e.tensor_tensor(out=Oall[:, h % 8, s0:s0 + sz], in0=o32[:, 0:sz], in1=Gs[h % 2][:, s0:s0 + sz], op=ALU.mult),
                  r=["o32", "G%d_%d" % (h % 2, bi)], w=["O%d_%d" % (h, bi)])

        def emit_phaseB(h, bi):
            ob, okey = emit_pb1(h, bi)
            emit_pb2(h, bi, ob, okey)

        def out_pass(ec0, from_x1, tag):
            P.barrier()
            A.at(heads_off)
            blk = lambda j: min(j // 4, 4)
            phase_out(0, Oall, lambda j, ec: ["O%d_%d" % (ec0 + ec, blk(j))], hgwo_d, final=False, ec0=ec0, nec=8, from_x1=from_x1, tag=tag)
            P.barrier()

        for h in range(NH):
            H = "h%d" % h
            wb = wbuf[0]
            if h == 8:
                for bi in range(5):
                    emit_phaseB(7, bi)
                out_pass(0, False, "a")
            wk = "wbuf0"
            if h == 0 or NO_PREFETCH:
                P.dma(lambda e, wb=wb, h=h: e.dma_start(out=wb, in_=hgw_d[h]), w=[wk], q="pool")
            c0_h = c0v[:, h:h + 1]
            c1_h = c1v[:, h:h + 1]

            def emit_proj(bi, wb=wb, wk=wk):
                s0, sz = TBS[bi]
                hk = hT_keys(0, s0, s0 + sz, False)
                banks = []
                for jj in range(5):
                    bk, bkey = bank_proj()
                    banks.append((bk, bkey))
                    for kc in range(8):
                        P.pe(lambda e, bk=bk, wb=wb, kc=kc, jj=jj, s0=s0, sz=sz: e.matmul(
                            bk[:, 0:sz], lhsT=wb[:, kc, jj * 128:(jj + 1) * 128], rhs=hT[:, kc, s0:s0 + sz],
                            start=(kc == 0), stop=(kc == 7)), r=[wk] + hk, w=bkey)
                return banks

            def emit_evac(bi, banks):
                s0, sz = TBS[bi]
                p = bi % 2
                K = lambda nm: nm + "_%d" % p
                q32 = q32s[p]; sgt = sgts[p]; lgt = lgts[p]; pft = pfts[p]
                e1t = {"f": e1s[p], "b": e1s[p]}; e2t = {"f": e2s[p], "b": e2s[p]}
                vfm = vfms[p]; tot4 = tot4s[p]
                (bq, kq), (bv, kv), (bzf, kzf), (bzb, kzb), (bg, kg) = banks
                bz = {"f": (bzf, kzf), "b": (bzb, kzb)}
                P.act(lambda e, bq=bq, sz=sz: e.activation(out=q32[:, 0:sz], in_=bq[:, 0:sz], func=AF.Silu), r=kq, w=[K("q32")])
                P.act(lambda e, bg=bg, s0=s0, sz=sz, Gh=Gs[h % 2]: e.activation(out=Gh[:, s0:s0 + sz], in_=bg[:, 0:sz], func=AF.Silu), r=kg, w=["G%d_%d" % (h % 2, bi)])
                P.dve(lambda e, bv=bv, sz=sz: e.tensor_copy(out=vfm[:, 0:sz], in_=bv[:, 0:sz]), r=kv, w=[K("vfm")])
                for d in ("f", "b"):
                    bzd, kzd = bz[d]
                    sg = sgt[d]
                    P.act(lambda e, bzd=bzd, sg=sg, sz=sz: e.activation(out=sg[:, 0:sz], in_=bzd[:, 0:sz], func=AF.Tanh, scale=0.5), r=kzd, w=[K("sg" + d)])

            def emit_rest1(bi, c0_h=c0_h, c1_h=c1_h, H=H):
                s0, sz = TBS[bi]
                p = bi % 2
                K = lambda nm: nm + "_%d" % p
                q32 = q32s[p]; sgt = sgts[p]; lgt = lgts[p]; pft = pfts[p]
                e1t = {"f": e1s[p], "b": e1s[p]}; e2t = {"f": e2s[p], "b": e2s[p]}
                vfm = vfms[p]; tot4 = tot4s[p]
                ncb = sz // 128
                n0 = s0 // 128
                for d in ("f", "b"):
                    sg, lg, pf = sgt[d], lgt[d], pft[d]
                    P.dve(lambda e, sg=sg, sz=sz: e.tensor_scalar(out=sg[:, 0:sz], in0=sg[:, 0:sz], scalar1=c1_h, scalar2=c0_h, op0=ALU.mult, op1=ALU.add),
                          r=[K("sg" + d), "c0v", "c1v"], w=[K("sg" + d)])
                    P.act(lambda e, sg=sg, lg=lg, sz=sz: e.activation(out=lg[:, 0:sz], in_=sg[:, 0:sz], func=AF.Ln), r=[K("sg" + d)], w=[K("lg" + d)])
                    P.pool(lambda e, sg=sg, sz=sz: e.tensor_scalar(out=sg[:, 0:sz], in0=sg[:, 0:sz], scalar1=-1.0, scalar2=1.0, op0=ALU.mult, op1=ALU.add),
                           r=[K("sg" + d)], w=[K("sg" + d)])
                lg, pf = lgt["f"], pft["f"]
                P.dve(lambda e, lg=lg, pf=pf, sz=sz: e.tensor_tensor_scan(out=pf[:, 0:sz], data0=smask[:, 0:sz], data1=lg[:, 0:sz], initial=0.0,
                                                                        op0=ALU.mult, op1=ALU.add), r=[K("lgf"), "smask"], w=[K("pff")])
                lg, pf = lgt["b"], pft["b"]
                P.pool(lambda e, pf=pf: e.memset(pf[:, 0:1], 0.0), w=[K("pfb0")])
                P.dve(lambda e, lg=lg, pf=pf, sz=sz: e.tensor_tensor_scan(out=pf[:, 1:sz], data0=lg[:, 0:sz - 1], data1=smask[:, 1:sz], initial=0.0,
                                                                        op0=ALU.add, op1=ALU.mult), r=[K("lgb"), "smask"], w=[K("pfb")])
                pf3 = pft["f"][:, 0:sz].rearrange("p (c j) -> p c j", j=128)
                pb3 = pft["b"][:, 0:sz].rearrange("p (c j) -> p c j", j=128)
                lb3 = lgt["b"][:, 0:sz].rearrange("p (c j) -> p c j", j=128)
                ex = H + "ex%d" % bi
                P.pool(lambda e, pf3=pf3, n0=n0, ncb=ncb: e.tensor_copy(out=AFv[:, n0:n0 + ncb], in_=pf3[:, :, MID_F]), r=[K("pff")], w=[ex + "AF"])
                P.pool(lambda e, pf3=pf3, n0=n0, ncb=ncb: e.tensor_tensor(out=BFv[:, n0:n0 + ncb], in0=pf3[:, :, 127], in1=pf3[:, :, MID_F], op=ALU.subtract),
                       r=[K("pff")], w=[ex + "BF"])
                P.pool(lambda e, pb3=pb3, lb3=lb3, ncb=ncb: e.tensor_tensor(out=tot4[:, 0:ncb], in0=pb3[:, :, 127], in1=lb3[:, :, 127], op=ALU.add),
                       r=[K("pfb"), K("pfb0"), K("lgb")], w=[K("tot4")])
                P.pool(lambda e, pb3=pb3, n0=n0, ncb=ncb: e.tensor_tensor(out=ABv[:, n0:n0 + ncb], in0=tot4[:, 0:ncb], in1=pb3[:, :, MID_B], op=ALU.subtract),
                       r=[K("tot4"), K("pfb")], w=[ex + "AB"])
                P.pool(lambda e, pb3=pb3, n0=n0, ncb=ncb: e.tensor_copy(out=BBv[:, n0:n0 + ncb], in_=pb3[:, :, MID_B]), r=[K("pfb")], w=[ex + "BB"])
                for d, mid, rk in (("f", MID_F, [K("pff")]), ("b", MID_B, [K("pfb"), K("pfb0")])):
                    p3 = pft[d][:, 0:sz].rearrange("p (c j) -> p c j", j=128)
                    l3 = lgt[d][:, 0:sz].rearrange("p (c j) -> p c j", j=128)
                    P.dve(lambda e, p3=p3, l3=l3, mid=mid, ncb=ncb: e.tensor_tensor(out=l3, in0=p3, in1=p3[:, :, mid:mid + 1].to_broadcast([128, ncb, 128]),
                                                                                    op=ALU.subtract), r=rk + [K("lg" + d)], w=[K("lg" + d)])

            def emit_rest2(bi):
                s0, sz = TBS[bi]
                p = bi % 2
                K = lambda nm: nm + "_%d" % p
                q32 = q32s[p]; sgt = sgts[p]; lgt = lgts[p]; pft = pfts[p]
                e1t = {"f": e1s[p], "b": e1s[p]}; e2t = {"f": e2s[p], "b": e2s[p]}
                vfm = vfms[p]; tot4 = tot4s[p]
                for d, sq, sk_ in (("f", 1.0, -1.0), ("b", -1.0, 1.0)):
                    lg, e1, e2, sg = lgt[d], e1t[d], e2t[d], sgt[d]
                    P.act(lambda e, lg=lg, e1=e1, sq=sq, sz=sz: e.activation(out=e1[:, 0:sz], in_=lg[:, 0:sz], func=AF.Exp, scale=sq), r=[K("lg" + d)], w=[K("e1")])
                    P.act(lambda e, lg=lg, e2=e2, sk_=sk_, sz=sz: e.activation(out=e2[:, 0:sz], in_=lg[:, 0:sz], func=AF.Exp, scale=sk_), r=[K("lg" + d)], w=[K("e2")])
                    P.pool(lambda e, e1=e1, d=d, s0=s0, sz=sz: e.tensor_tensor(out=Qd[d][:, s0:s0 + sz], in0=q32[:, 0:sz], in1=e1[:, 0:sz], op=ALU.mult),
                           r=[K("q32"), K("e1")], w=["Q%s%d" % (d, bi)])
                    P.pool(lambda e, e2=e2, sg=sg, d=d, s0=s0, sz=sz: e.tensor_tensor(out=Kd[d][:, s0:s0 + sz], in0=sg[:, 0:sz], in1=e2[:, 0:sz], op=ALU.mult),
                           r=[K("sg" + d), K("e2")], w=["K%s%d" % (d, bi)])

            def emit_tr(bi):
                s0, sz = TBS[bi]
                p = bi % 2
                K = lambda nm: nm + "_%d" % p
                q32 = q32s[p]; sgt = sgts[p]; lgt = lgts[p]; pft = pfts[p]
                e1t = {"f": e1s[p], "b": e1s[p]}; e2t = {"f": e2s[p], "b": e2s[p]}
                vfm = vfms[p]; tot4 = tot4s[p]
                ncb = sz // 128
                n0 = s0 // 128
                for nm, srcf, rkey, dstT in (("v", lambda c: vfm[:, c * 128:(c + 1) * 128], K("vfm"), vT),
                                             ("kf", lambda c, s0=s0: Kd["f"][:, s0 + c * 128:s0 + (c + 1) * 128], "Kf%d" % bi, KT["f"]),
                                             ("kb", lambda c, s0=s0: Kd["b"][:, s0 + c * 128:s0 + (c + 1) * 128], "Kb%d" % bi, KT["b"])):
                    bk, bkey = pbanks[7], ["pb7s%d" % q_ for q_ in range(4)]
                    bkb = bk[:].bitcast(BF16)
                    for c in range(ncb):
                        src = srcf(c)
                        P.pe(lambda e, bkb=bkb, c=c, src=src: e.transpose(bkb[:, c * 128:(c + 1) * 128], src, ident[:]), r=[rkey, "ident"], w=bkey)
                    dst = dstT[:, n0:n0 + ncb, :]
                    srcv = bkb[:, 0:ncb * 128].rearrange("p (c v) -> p c v", v=128)
                    wkey = "T%s%d" % (nm, bi)
                    P.act(lambda e, dst=dst, srcv=srcv: e.copy(out=dst, in_=srcv), r=bkey, w=[wkey])

            interleave = h > 0 and h != 8
            banks = emit_proj(0)
            if interleave:
                emit_phaseB(h - 1, 0)
            emit_evac(0, banks)
            emit_rest1(0)
            for bi in range(5):
                emit_rest2(bi)
                pb = None
                if bi + 1 < 5:
                    banks = emit_proj(bi + 1)
                    if interleave:
                        pb = emit_pb1(h - 1, bi + 1)
                    emit_evac(bi + 1, banks)
                emit_tr(bi)
                if bi + 1 < 5:
                    emit_rest1(bi + 1)
                if pb is not None:
                    emit_pb2(h - 1, bi + 1, pb[0], pb[1])
            if h + 1 < NH and not NO_PREFETCH:
                P.dma(lambda e, wb=wb, h=h: e.dma_start(out=wb, in_=hgw_d[h + 1]), w=[wk], q="pool")
            exk = lambda nm: [H + "ex%d" % bi + nm for bi in range(5)]
            P.pool(lambda e: e.tensor_tensor(out=TRv[:, 1:18], in0=AFv[:, 1:18], in1=BFv[:, 0:17], op=ALU.add), r=exk("AF") + exk("BF"), w=["TRv"])
            P.act(lambda e: e.activation(out=RFv[:, 1:18], in_=TRv[:, 1:18], func=AF.Exp), r=["TRv"], w=["RF"])
            P.pool(lambda e: e.tensor_tensor(out=TRv[:, 0:17], in0=ABv[:, 0:17], in1=BBv[:, 1:18], op=ALU.add), r=exk("AB") + exk("BB") + ["RF"], w=["TRv"])
            P.pool(lambda e: e.tensor_tensor(out=TRv[:, 17:18], in0=ABv[:, 17:18], in1=BBv[:, 0:1], op=ALU.add), r=exk("AB") + exk("BB") + ["TRv"], w=["TRv"])
            P.act(lambda e: e.activation(out=RBv[:, 0:18], in_=TRv[:, 0:18], func=AF.Exp), r=["TRv"], w=["RB"])
            for j2 in range(0, 18, 2):
                bk, bkeys = bank_aux()
                for jj in range(2):
                    j = j2 + jj
                    for di, d in enumerate(("f", "b")):
                        n = ORDER[d][j]
                        sl = jj * 2 + di
                        P.pe(lambda e, bk=bk, sl=sl, n=n, d=d: e.matmul(bk[:, sl * 128:(sl + 1) * 128], lhsT=KT[d][:, n, :], rhs=vT[:, n, :], start=True, stop=True),
                             r=["Tk%s%d" % (d, n // 4), "Tv%d" % (n // 4)], w=bkeys)
                for jj in range(2):
                    j = j2 + jj
                    for di, d in enumerate(("f", "b")):
                        n = ORDER[d][j]
                        sl = jj * 2 + di
                        tcur = Tst[d][j % 2]
                        tprev = Tst[d][(j + 1) % 2]
                        tk, tpk = "Tst%s%d" % (d, j % 2), "Tst%s%d" % (d, (j + 1) % 2)
                        skey = "Sbf%s%d" % (d, n)
                        if j == 0:
                            P.pool(lambda e, d=d, n=n: e.memset(Sbf[d][:, n, :], 0.0), w=[skey])
                            P.dve(lambda e, bk=bk, sl=sl, tcur=tcur: e.tensor_copy(out=tcur, in_=bk[:, sl * 128:(sl + 1) * 128]), r=bkeys, w=[tk])
                        else:
                            rcol = Rv[d][:, n:n + 1]
                            rk = "RF" if d == "f" else "RB"
                            P.pool(lambda e, d=d, n=n, tprev=tprev, rcol=rcol: e.tensor_scalar(out=Sbf[d][:, n, :], in0=tprev, scalar1=rcol, scalar2=1.0, op0=ALU.mult, op1=ALU.mult),
                                   r=[tpk, rk], w=[skey])
                            P.dve(lambda e, bk=bk, sl=sl, tcur=tcur, tprev=tprev, rcol=rcol: e.scalar_tensor_tensor(
                                out=tcur, in0=tprev, scalar=rcol, in1=bk[:, sl * 128:(sl + 1) * 128], op0=ALU.mult, op1=ALU.add),
                                r=[tpk, rk] + bkeys, w=[tk])
            if HEAD_BARRIER:
                P.barrier()

        for bi in range(5):
            emit_phaseB(NH - 1, bi)
        if NH <= 8:
            if NH < 8:
                P.pool(lambda e: e.memset(Oall[:, NH:8, :], 0.0), w=["O%d_%d" % (hh_, b_) for hh_ in range(NH, 8) for b_ in range(5)])
            out_pass(0, False, "a")
        else:
            out_pass(8, True, "b")

        final_keys = []
        if stop_after is None:
            A.at(0)
            hT1 = A.bf16(8 * NT).rearrange("p (k t) -> p k t", t=NT)
            O1 = A.bf16(8 * 2048).rearrange("p (h t) -> p h t", t=2048)
            l1_off = A.off
            bct, mkeys = phase_mod(1, 9216, 26624)
            phase_norm(1, bct, mkeys, hT1,
                       lambda j: ((ctx1_d[j * 128:(j + 1) * 128, :], ["ctx1_%d" % j]) if j < 2
                                  else (x1_d[(j - 2) * 128:(j - 1) * 128, :], ["x1_%d" % (j - 2)])),
                       permute_lat=True)
            P.barrier()
            A.at(l1_off)
            xr_c = A.f32(260)
            xr_l = A.f32(2052)
            xc = A.f32(2 * NT).rearrange("p (a t) -> p a t", t=NT)
            xcb = A.bf16(2 * NT).rearrange("p (a t) -> p a t", t=NT)
            gsl = A.bf16(2 * 2048).rearrange("p (a t) -> p a t", t=2048)
            hf = A.f32(2048)
            hctx = A.f32(256)
            rgwb = [A.bf16(2 * 8 * 256).rearrange("p (a k n) -> p a k n", k=8, n=256)]
            axb = [A.bf16(8 * 256).rearrange("p (q n) -> p q n", n=256)]
            tr_ = A.f32(512); ta2 = A.f32(512); tth = A.f32(512); thb = A.f32(512)
            trs = [A.f32(512) for _ in range(2)]; tigs = [A.f32(512) for _ in range(2)]
            tas = [A.f32(512) for _ in range(2)]; tus = [A.f32(512) for _ in range(2)]
            s2all = A.f32(NT)
            l1_end = A.off
            P.pool(lambda e: e.memset(xr_c[:, :], 0.0), w=["xr_c"])
            P.pool(lambda e: e.memset(xr_l[:, :], 0.0), w=["xr_l"])
            LB = [(0, 256), (256, 512), (768, 512), (1280, 512), (1792, 512)]
            def out_pass1(ec0, final, tag):
                P.barrier()
                A.at(l1_off)
                phase_out(1, O1, lambda j, ec: ["O1_%d" % (ec0 + ec)], rgwo_d, final=final, ec0=ec0, nec=8, tag=tag)
                if not final:
                    P.barrier()

            for hh in range(8):
                if hh == 4:
                    out_pass1(0, False, "a")
                wb = rgwb[0]
                wk = "rgwb0"
                ab = axb[0]
                ak = "axb0"
                if hh == 0:
                    P.dma(lambda e, wb=wb, hh=hh: e.dma_start(out=wb, in_=rgw_d[hh]), w=[wk], q="pool")
                if hh == 0:
                    P.dma(lambda e, ab=ab, hh=hh: e.dma_start(out=ab, in_=rgax_d[hh]), w=[ak], q="pool")
                for a in range(2):
                    ct = 2 * hh + a
                    cw = rgv[:, ct, 0:4]
                    cb = rgv[:, ct, 4:5]
                    for bi, (s0, sz) in enumerate(LB):
                        hk = hT_keys(1, s0, s0 + sz, True)
                        bx, kx = newbank()
                        bg, kg = newbank()
                        for (bk, bkey, co) in ((bx, kx, 0), (bg, kg, 128)):
                            if co == 128 and bi == 0:
                                continue
                            for kc in range(8):
                                P.pe(lambda e, bk=bk, wb=wb, a=a, kc=kc, co=co, s0=s0, sz=sz: e.matmul(
                                    bk[:, 0:sz], lhsT=wb[:, a, kc, co:co + 128], rhs=hT1[:, kc, s0:s0 + sz],
                                    start=(kc == 0), stop=(kc == 7)), r=[wk] + hk, w=bkey)
                        if bi == 0:
                            P.act(lambda e, bx=bx: e.copy(out=xr_c[:, 2:258], in_=bx[:, 0:256]), r=kx, w=["xr_c"])
                        else:
                            l0 = s0 - 256
                            P.act(lambda e, bx=bx, l0=l0: e.copy(out=xr_l[:, 2 + l0:2 + l0 + 512], in_=bx[:, 0:512]), r=kx, w=["xr_l"])
                            P.act(lambda e, bg=bg: e.activation(out=tth[:, 0:512], in_=bg[:, 0:512], func=AF.Tanh, scale=0.5), r=kg, w=["tth"])
                            P.dve(lambda e, bg=bg, a=a, l0=l0: e.scalar_tensor_tensor(out=gsl[:, a, l0:l0 + 512], in0=tth[:, 0:512], scalar=1.0, in1=bg[:, 0:512],
                                                                                    op0=ALU.add, op1=ALU.mult), r=kg + ["tth"], w=["gsl%d" % a])
                    for (xr, xk, c0, ln) in ((xr_c, "xr_c", 0, 256), (xr_l, "xr_l", 256, 2048)):
                        dst = xc[:, a, c0:c0 + ln]
                        ck = "xc%d_%d" % (a, 0 if c0 == 0 else 1)
                        P.dve(lambda e, xr=xr, dst=dst, ln=ln, cw=cw, cb=cb: e.tensor_scalar(out=dst, in0=xr[:, 2:2 + ln], scalar1=cw[:, 2:3], scalar2=cb, op0=ALU.mult, op1=ALU.add),
                              r=[xk, "rgv"], w=[ck])
                        for tap, off in ((0, 0), (1, 1), (3, 3)):
                            P.dve(lambda e, xr=xr, dst=dst, ln=ln, cw=cw, tap=tap, off=off: e.scalar_tensor_tensor(
                                out=dst, in0=xr[:, off:off + ln], scalar=cw[:, tap:tap + 1], in1=dst, op0=ALU.mult, op1=ALU.add),
                                r=[xk, "rgv", ck], w=[ck])
                        P.pool(lambda e, dst=dst, a=a, c0=c0, ln=ln: e.tensor_copy(out=xcb[:, a, c0:c0 + ln], in_=dst), r=[ck], w=["xcb%d_%d" % (a, 0 if c0 == 0 else 1)])
                if hh + 1 < 8:
                    P.dma(lambda e, wb=wb, hh=hh: e.dma_start(out=wb, in_=rgw_d[hh + 1]), w=[wk], q="pool")
                xcbk = lambda bi: ["xcb0_%d" % (0 if bi == 0 else 1), "xcb1_%d" % (0 if bi == 0 else 1)]
                for ao in range(2):
                    ct = 2 * hh + ao
                    for d in (0, 1):
                        order = list(range(5)) if d == 0 else [0, 4, 3, 2, 1]
                        prev_last = None
                        hb_a = rgvh[:, ct, d:d + 1]
                        hb_x = rgvh[:, ct, 2 + d:3 + d]
                        cl = clv[:, ct, d:d + 1]
                        clq = clqv[:, ct, d:d + 1]
                        chh = chhv[:, ct, d:d + 1]

                        def emit_coef(bi, both, ao=ao, d=d, ab=ab, ak=ak):
                            s0, sz = LB[bi]
                            res = []
                            for axi in ((0, 1) if both else (0,)):
                                bk, bkey = newbank()
                                for ic in range(2):
                                    q = (axi * 2 + d) * 2 + ic
                                    P.pe(lambda e, bk=bk, ab=ab, q=q, ao=ao, ic=ic, s0=s0, sz=sz: e.matmul(
                                        bk[:, 0:sz], lhsT=ab[:, q, ao * 128:(ao + 1) * 128], rhs=xcb[:, ic, s0:s0 + sz],
                                        start=(ic == 0), stop=(ic == 1)), r=[ak] + xcbk(bi), w=bkey)
                                res.append((bk, bkey))
                            return res

                        nxt = emit_coef(0, False)
                        for bi in range(5):
                            s0, sz = LB[bi]
                            (ba, ka), = nxt
                            P.act(lambda e, ba=ba, sz=sz, hb_a=hb_a: e.activation(out=tr_[:, 0:sz], in_=ba[:, 0:sz], func=AF.Tanh, bias=hb_a, scale=0.5), r=ka + ["rgvh"], w=["tr"])
                            if bi + 1 < 5:
                                nxt = emit_coef(bi + 1, False)
                            P.act(lambda e, sz=sz, chh=chh: e.activation(out=tth[:, 0:sz], in_=tr_[:, 0:sz], func=AF.Tanh, bias=chh, scale=chh), r=["tr", "chhv"], w=["tth"])
                            P.act(lambda e, sz=sz, cl=cl: e.activation(out=ta2[:, 0:sz], in_=tr_[:, 0:sz], func=AF.Exp, bias=cl, scale=cl), r=["tr", "clv"], w=["ta2"])
                            P.dve(lambda e, sz=sz, s0=s0: e.scalar_tensor_tensor(out=s2all[:, s0:s0 + sz], in0=ta2[:, 0:sz], scalar=1.0, in1=tth[:, 0:sz], op0=ALU.add, op1=ALU.mult),
                                  r=["ta2", "tth"], w=["s2_%d" % bi])
                        s2k = ["s2_%d" % b_ for b_ in range(5)]
                        P.act(lambda e: e.activation(out=s2all[:, :], in_=s2all[:, :], func=AF.Sqrt), r=s2k, w=s2k)
                        def stage1(step, cf, ao=ao, d=d, hb_a=hb_a, hb_x=hb_x, clq=clq):
                            bi = order[step]
                            s0, sz = LB[bi]
                            p = step % 2
                            tr2, tig, ta, tu = trs[p], tigs[p], tas[p], tus[p]
                            (ba, ka), (bx, kx) = cf
                            P.act(lambda e, ba=ba, sz=sz: e.activation(out=tr2[:, 0:sz], in_=ba[:, 0:sz], func=AF.Tanh, bias=hb_a, scale=0.5), r=ka + ["rgvh"], w=["tr_%d" % p])
                            P.act(lambda e, bx=bx, sz=sz: e.activation(out=tig[:, 0:sz], in_=bx[:, 0:sz], func=AF.Tanh, bias=hb_x, scale=0.5), r=kx + ["rgvh"], w=["tig_%d" % p])
                            P.act(lambda e, sz=sz: e.activation(out=ta[:, 0:sz], in_=tr2[:, 0:sz], func=AF.Exp, bias=clq, scale=clq), r=["tr_%d" % p, "clqv"], w=["ta_%d" % p])
                            P.dve(lambda e, sz=sz, s0=s0: e.scalar_tensor_tensor(out=tu[:, 0:sz], in0=tig[:, 0:sz], scalar=1.0, in1=xc[:, ao, s0:s0 + sz],
                                                                             op0=ALU.add, op1=ALU.mult),
                                  r=["tig_%d" % p, "xc%d_%d" % (ao, 0 if bi == 0 else 1)], w=["tu_%d" % p])
                            P.pool(lambda e, sz=sz, s0=s0: e.tensor_tensor(out=tu[:, 0:sz], in0=tu[:, 0:sz], in1=s2all[:, s0:s0 + sz], op=ALU.mult), r=["tu_%d" % p, "s2_%d" % bi], w=["tu_%d" % p])

                        cf = emit_coef(order[0], True)
                        stage1(0, cf)
                        for step, bi in enumerate(order):
                            s0, sz = LB[bi]
                            p = step % 2
                            ta, tu = tas[p], tus[p]
                            tak, tuk = "ta_%d" % p, "tu_%d" % p
                            if step + 1 < len(order):
                                cf = emit_coef(order[step + 1], True)
                                stage1(step + 1, cf)
                            if d == 0:
                                dst = hctx[:, 0:256] if bi == 0 else hf[:, s0 - 256:s0 - 256 + sz]
                                dkey = "hctx" if bi == 0 else "hf%d" % bi
                                init = 0.0 if step == 0 else prev_last
                                P.dve(lambda e, dst=dst, sz=sz, init=init, ta=ta, tu=tu: e.tensor_tensor_scan(out=dst, data0=ta[:, 0:sz], data1=tu[:, 0:sz], initial=init,
                                                                                             op0=ALU.mult, op1=ALU.add),
                                      r=[tak, tuk] + ([] if step == 0 else [pkey_prev]), w=[dkey])
                                prev_last = dst[:, sz - 1:sz]
                                pkey_prev = dkey
                            else:
                                dst = hctx[:, 0:256] if bi == 0 else thb[:, 0:sz]
                                dkey = "hctx" if bi == 0 else "thb"
                                init = 0.0 if step == 0 else prev_last
                                rk = [tak, tuk] + ([] if step == 0 else ["carry"])
                                P.dve(lambda e, dst=dst, sz=sz, init=init, ta=ta, tu=tu: e.tensor_tensor_scan(out=dst[:, ::-1], data0=ta[:, 0:sz][:, ::-1], data1=tu[:, 0:sz][:, ::-1],
                                                                                             initial=init, op0=ALU.mult, op1=ALU.add), r=rk, w=[dkey])
                                cc = carry[:, 0:1]
                                P.pool(lambda e, dst=dst, cc=cc: e.tensor_copy(out=cc, in_=dst[:, 0:1]), r=[dkey], w=["carry"])
                                prev_last = cc
                                if bi > 0:
                                    l0 = s0 - 256
                                    P.pool(lambda e, sz=sz, l0=l0: e.tensor_tensor(out=thb[:, 0:sz], in0=thb[:, 0:sz], in1=hf[:, l0:l0 + sz], op=ALU.add),
                                           r=["thb", "carry", "hf%d" % bi], w=["thb"])
                                    w0 = l0 // 32
                                    dsto = O1[:, ct % 8, :].rearrange("p (r w) -> p w r", w=64)[:, w0:w0 + 16, :]
                                    srcs = thb[:, 0:512].rearrange("p (w r) -> p w r", r=32)
                                    srcg = gsl[:, ao, l0:l0 + 512].rearrange("p (w r) -> p w r", r=32)
                                    P.dve(lambda e, dsto=dsto, srcs=srcs, srcg=srcg: e.scalar_tensor_tensor(out=dsto, in0=srcs, scalar=0.25, in1=srcg, op0=ALU.mult, op1=ALU.mult),
                                          r=["thb", "gsl%d" % ao], w=["O1_%d" % ct])
                if hh + 1 < 8:
                    P.dma(lambda e, ab=ab, hh=hh: e.dma_start(out=ab, in_=rgax_d[hh + 1]), w=[ak], q="pool")
            out_pass1(8, True, "b")
            final_keys = ["out_%d" % j for j in range(16)]
        else:
            final_keys = ["x1_%d" % j for j in range(16)] + ["ctx1_0", "ctx1_1"]
        P.finalize(final_keys)
        print("ops:", P.n_ops, "arena L0 end:", l0_end, "of", AW)
    return nc


def prep_inputs(inputs):
    f = lambda a: np.ascontiguousarray(np.asarray(a, dtype=np.float32))
    x, c, ctx, c_ctx = f(inputs["x"]), f(inputs["c"]), f(inputs["ctx"]), f(inputs["c_ctx"])
    ada_w, ada_b = f(inputs["ada_w"]), f(inputs["ada_b"])
    adaw = np.stack([ada_w[i].reshape(8, 128, 6, 512).transpose(2, 1, 0, 3) for i in range(2)], 0)
    rows = np.concatenate([ada_b[0], ada_b[1], f(inputs["norm_g"])[0], f(inputs["norm_g"])[1], f(inputs["final_norm_g"])])[None, :]
    hw = f(inputs["hg_w_in"])[0].reshape(8, 128, 5, 16, 128)
    hgw = hw.transpose(3, 1, 0, 2, 4).reshape(16, 128, 8, 640)
    hgwo = f(inputs["hg_w_out"])[0].reshape(16, 128, D).transpose(1, 0, 2)
    lbr = f(inputs["hg_lower_bounds"]).reshape(3, 16, 128)
    hgn = f(inputs["hg_norm_g"])[0].reshape(1, 16, 128)
    hgvec = np.concatenate([lbr, hgn], 0).transpose(2, 0, 1)
    rw = f(inputs["rg_w_in"])[0].reshape(8, 128, 2, 8, 2, 128)
    rgw = rw.transpose(3, 1, 4, 0, 2, 5).reshape(8, 128, 2, 8, 256)
    wa, wx = f(inputs["rg_w_a"])[0], f(inputs["rg_w_x"])[0]
    wax = np.stack([wa, wx], 0).reshape(2, 2, 8, 2, 128, 256)
    rgax = wax.transpose(2, 4, 0, 1, 3, 5).reshape(8, 128, 8, 256)
    rgwo = f(inputs["rg_w_out"])[0].reshape(16, 128, D).transpose(1, 0, 2)
    cw = f(inputs["rg_conv_w"])[0].reshape(4, 16, 128)
    cb = f(inputs["rg_conv_b"])[0].reshape(1, 16, 128)
    ba = f(inputs["rg_b_a"])[0].reshape(2, 16, 128)
    bx = f(inputs["rg_b_x"])[0].reshape(2, 16, 128)
    lam = f(inputs["rg_lambda"])[0].reshape(2, 16, 128)
    rgvec = np.concatenate([cw, cb, ba, bx, lam], 0).transpose(2, 1, 0)
    shared = dict(adaw=f(adaw), rows=f(rows), hgw=f(hgw), hgwo=f(hgwo), hgvec=f(hgvec), rgw=f(rgw), rgax=f(rgax),
                  rgwo=f(rgwo), rgvec=f(rgvec))
    maps = []
    for b in range(8):
        c2 = np.concatenate([c[b].reshape(8, 128).T, c_ctx.reshape(8, 128).T], 1)
        m = dict(shared)
        m.update(x=f(x[b]), ctx=f(ctx[b]), c2=f(c2))
        maps.append(m)
    return maps


def kernel(**inputs):
    maps = prep_inputs(inputs)
    nc = build_nc()
    res = run_bass_kernel_spmd(nc, maps, core_ids=list(range(8)))
    return np.stack([np.asarray(r["out"], dtype=np.float32) for r in res.results], 0)
```

```python
import contextlib
import numpy as np
import concourse.bass as bass
import concourse.mybir as mybir
from concourse.bass_utils import run_bass_kernel_spmd

F32 = mybir.dt.float32
BF16 = mybir.dt.bfloat16
AF = mybir.ActivationFunctionType
ALU = mybir.AluOpType

import os
HEAD_BARRIER = os.environ.get("HEAD_BARRIER", "0") == "1"
NO_PREFETCH = os.environ.get("NO_PREFETCH", "0") == "1"
D = 1024
E = 2048
NT = 2304
NCH = 18
EPS = 1e-6
MID_F = 63
MID_B = 64


class Prog:
    ENGS = ("pe", "act", "dve", "pool", "sp")
    N_DMA_SEMS = {"sp": 12, "pool": 6, "act": 4}

    def __init__(self, nc):
        self.nc = nc
        self.ops = []

    def op(self, eng, fn, reads=(), writes=(), dma=False, barrier=False):
        self.ops.append(dict(eng=eng, fn=fn, reads=tuple(reads), writes=tuple(writes), dma=dma, barrier=barrier))
        return len(self.ops) - 1

    def pe(self, fn, r=(), w=()): return self.op("pe", fn, r, w)
    def act(self, fn, r=(), w=()): return self.op("act", fn, r, w)
    def dve(self, fn, r=(), w=()): return self.op("dve", fn, r, w)
    def pool(self, fn, r=(), w=()): return self.op("pool", fn, r, w)
    def dma(self, fn, r=(), w=(), q="sp"): return self.op(q, fn, r, w, dma=True)

    def barrier(self):
        for e in self.ENGS:
            self.op(e, None, barrier=True)

    def finalize(self, final_keys):
        nc = self.nc
        ops = self.ops
        self.op("sp", None, reads=final_keys)
        n = len(ops)
        last_writer, readers = {}, {}
        deps = [None] * n
        dma_cnt = {q: 0 for q in self.N_DMA_SEMS}
        dma_last_on_sem, dma_sem_of, dma_val_of, dma_semval = {}, {}, {}, {}
        last_on_eng = {}
        for i, o in enumerate(ops):
            d = set()
            if o["barrier"]:
                d.update(last_on_eng.values())
                d.update(dma_last_on_sem.values())
            for k in o["reads"]:
                if k in last_writer:
                    d.add(last_writer[k])
            for k in o["writes"]:
                if k in last_writer:
                    d.add(last_writer[k])
                d.update(readers.get(k, ()))
            if o["dma"]:
                q = o["eng"]
                s = (q, dma_cnt[q] % self.N_DMA_SEMS[q])
                dma_cnt[q] += 1
                if s in dma_last_on_sem:
                    d.add(dma_last_on_sem[s])
                dma_last_on_sem[s] = i
                dma_sem_of[i] = s
                dma_semval[s] = dma_semval.get(s, 0) + 16
                dma_val_of[i] = dma_semval[s]
            elif o["fn"] is not None:
                last_on_eng[o["eng"]] = i
            for k in o["reads"]:
                readers.setdefault(k, []).append(i)
            for k in o["writes"]:
                last_writer[k] = i
                readers[k] = []
            d.discard(i)
            if o["eng"] == "pe":
                d = {j for j in d if ops[j]["eng"] != "pe"}
            deps[i] = d
        needed = set()
        for i in range(n):
            needed.update(deps[i])
        sig = {}
        cnt = {e: 0 for e in self.ENGS}
        for i, o in enumerate(ops):
            if o["dma"] or o["fn"] is None:
                continue
            if i in needed:
                cnt[o["eng"]] += 1
                sig[i] = (("eng", o["eng"]), cnt[o["eng"]])
        for i in dma_sem_of:
            sig[i] = (("dma",) + dma_sem_of[i], dma_val_of[i])
        known = {e: {} for e in self.ENGS}
        clock = [None] * n
        waits = [None] * n
        for i, o in enumerate(ops):
            kn = known[o["eng"]]
            wm = {}
            for j in sorted(deps[i]):
                if j not in sig:
                    continue
                s, v = sig[j]
                if kn.get(s, 0) >= v:
                    continue
                wm[s] = max(wm.get(s, 0), v)
                for s2, v2 in clock[j].items():
                    if kn.get(s2, 0) < v2:
                        kn[s2] = v2
                kn[s] = v
            waits[i] = list(wm.items())
            clock[i] = dict(kn)
        with contextlib.ExitStack() as st:
            sems = {}
            for e in self.ENGS:
                sems[("eng", e)] = st.enter_context(nc.semaphore("s_" + e))
            for q, k in self.N_DMA_SEMS.items():
                for t in range(k):
                    sems[("dma", q, t)] = st.enter_context(nc.semaphore("d_%s%d" % (q, t)))
            block = st.enter_context(nc.Block())

            def make(ename):
                def body(eng):
                    for i, o in enumerate(ops):
                        if o["eng"] != ename:
                            continue
                        for s, v in waits[i]:
                            eng.wait_ge(sems[s], v)
                        if o["fn"] is None:
                            continue
                        ins = o["fn"](eng)
                        if i in sig:
                            ins.then_inc(sems[sig[i][0]], 16 if o["dma"] else 1)
                return body

            block.tensor(make("pe"))
            block.scalar(make("act"))
            block.vector(make("dve"))
            block.gpsimd(make("pool"))
            block.sync(make("sp"))
        self.n_ops = n


class Arena:
    def __init__(self, ap_all, words):
        self.ap = ap_all
        self.words = words
        self.off = 0

    def at(self, off):
        self.off = off

    def f32(self, n):
        v = self.ap[:, self.off:self.off + n]
        self.off += n
        assert self.off <= self.words, ("arena overflow", self.off, self.words)
        return v

    def bf16(self, n):
        w = (n + 1) // 2
        v = self.ap[:, self.off:self.off + w].bitcast(BF16)
        self.off += w
        assert self.off <= self.words, ("arena overflow", self.off, self.words)
        return v


ROW_ADAB = (0, 3072)
ROW_NG = (6144, 7168)
ROW_FNG = 8192
NROWS = 9216


def build_nc(dbg=False, stop_after=None):
    nc = bass.Bass("TRN2", target_bir_lowering=False)
    dt = nc.dram_tensor
    x_d = dt("x", [2048, D], F32, kind="ExternalInput").ap()
    ctx_d = dt("ctx", [256, D], F32, kind="ExternalInput").ap()
    c2_d = dt("c2", [128, 16], F32, kind="ExternalInput").ap()
    adaw_d = dt("adaw", [2, 6, 128, 8, 512], F32, kind="ExternalInput").ap()
    rows_d = dt("rows", [1, NROWS], F32, kind="ExternalInput").ap()
    hgw_d = dt("hgw", [16, 128, 8, 640], F32, kind="ExternalInput").ap()
    hgwo_d = dt("hgwo", [128, 16, D], F32, kind="ExternalInput").ap()
    hgvec_d = dt("hgvec", [128, 4, 16], F32, kind="ExternalInput").ap()
    rgw_d = dt("rgw", [8, 128, 2, 8, 256], F32, kind="ExternalInput").ap()
    rgax_d = dt("rgax", [8, 128, 8, 256], F32, kind="ExternalInput").ap()
    rgwo_d = dt("rgwo", [128, 16, D], F32, kind="ExternalInput").ap()
    rgvec_d = dt("rgvec", [128, 16, 11], F32, kind="ExternalInput").ap()
    out_d = dt("out", [2048, D], F32, kind="ExternalOutput").ap()
    kind1 = "ExternalOutput" if dbg else "Internal"
    x1_d = dt("x1", [2048, D], F32, kind=kind1).ap()
    ctx1_d = dt("ctx1", [256, D], F32, kind=kind1).ap()

    AW = 48800
    with contextlib.ExitStack() as st:
        T = lambda name, shape, dty: st.enter_context(nc.sbuf_tensor(name, shape, dty))
        arena_t = T("arena", [128, AW], F32)
        top_t = T("top", [128, 2048], F32)
        ident = T("ident", [128, 128], BF16)
        identf = T("identf", [128, 128], F32)
        ones_bf = T("ones_bf", [128, 128], BF16)
        ones_f = T("ones_f", [1, 128], F32)
        smask = T("smask", [128, 512], F32)
        m01L = T("m01L", [128, 128], F32)
        m01U = T("m01U", [128, 128], F32)
        small = T("small", [128, 640], F32)
        pbanks = [st.enter_context(nc.psum_tensor("pb%d" % i, [128, 512], F32)) for i in range(8)]

        P = Prog(nc)
        A = Arena(arena_t[:], AW)
        bank_ctr = [0]

        def newbank():
            i = bank_ctr[0] % 8
            bank_ctr[0] += 1
            return pbanks[i], ["pb%ds%d" % (i, q) for q in range(4)]

        pool_ctr = {"proj": 0, "aux": 0}

        def bank_proj():
            i = pool_ctr["proj"] % 5
            pool_ctr["proj"] += 1
            return pbanks[i], ["pb%ds%d" % (i, q) for q in range(4)]

        def bank_aux():
            i = 5 + pool_ctr["aux"] % 3
            pool_ctr["aux"] += 1
            return pbanks[i], ["pb%ds%d" % (i, q) for q in range(4)]

        gate_l = top_t[:, 0:1024]
        gate_c = top_t[:, 1024:2048]

        P.pool(lambda e: e.memset(identf[:], 1.0), w=["identf"])
        P.pool(lambda e: e.affine_select(out=identf[:], in_=identf[:], pattern=[[-1, 128]], compare_op=ALU.is_equal,
                                         fill=0.0, base=0, channel_multiplier=1), r=["identf"], w=["identf"])
        P.dve(lambda e: e.tensor_copy(out=ident[:], in_=identf[:]), r=["identf"], w=["ident"])
        P.pool(lambda e: e.memset(ones_bf[:], 1.0), w=["ones_bf"])
        P.pool(lambda e: e.memset(ones_f[:], 1.0), w=["ones_f"])
        P.pool(lambda e: e.memset(smask[:], 1.0), w=["smask"])
        smv = smask[:].rearrange("p (c j) -> p c j", j=128)
        P.pool(lambda e: e.memset(smv[:, :, 0:1], 0.0), r=["smask"], w=["smask"])
        for (m, cm, pat, key) in ((m01L, -1, 1, "maskL"), (m01U, 1, -1, "maskU")):
            P.pool(lambda e, m=m: e.memset(m[:], 1.0), w=[key])
            P.pool(lambda e, m=m, cm=cm, pat=pat: e.affine_select(
                out=m[:], in_=m[:], pattern=[[pat, 128]], compare_op=ALU.is_ge, fill=0.0, base=0,
                channel_multiplier=cm), r=[key], w=[key])
        MASKS = {"f": (m01L, "maskL"), "b": (m01U, "maskU")}
        for i in range(8):
            P.dve(lambda e, i=i: e.memset(pbanks[i][:], 0.0), w=["pb%ds%d" % (i, q) for q in range(4)])

        c2 = small[:, 0:16]
        sc2 = small[:, 16:32]
        hgv = small[:, 32:96].rearrange("p (a h) -> p a h", h=16)
        lbv = small[:, 96:112]
        omlv = small[:, 112:128]
        tmpv = small[:, 128:176].rearrange("p (a h) -> p a h", h=16)
        ss_t = small[:, 176:184]
        rstd_t = small[:, 184:192]
        AFv = small[:, 192:210]
        BFv = small[:, 210:228]
        ABv = small[:, 228:246]
        BBv = small[:, 246:264]
        RFv = small[:, 264:282]
        RBv = small[:, 282:300]
        TRv = small[:, 300:318]
        tot4 = small[:, 318:322]
        clv = small[:, 322:354].rearrange("p (c d) -> p c d", d=2)
        cl2v = small[:, 354:386].rearrange("p (c d) -> p c d", d=2)
        clhv = small[:, 386:418].rearrange("p (c d) -> p c d", d=2)
        carry = small[:, 418:426]
        nhalf = small[:, 426:427]
        halfc = small[:, 427:428]
        c0v = small[:, 428:444]
        c1v = small[:, 444:460]
        chhv = small[:, 460:492].rearrange("p (c d) -> p c d", d=2)
        clqv = small[:, 492:524].rearrange("p (c d) -> p c d", d=2)
        rgvh = T("rgvh", [128, 16, 4], F32)
        sc2b_t = T("sc2b", [128, 16], BF16)
        rgv = T("rgv", [128, 16, 11], F32)

        P.dma(lambda e: e.dma_start(out=c2, in_=c2_d), w=["c2"])
        P.dma(lambda e: e.dma_start(out=hgv, in_=hgvec_d), w=["hgv"])
        P.dma(lambda e: e.dma_start(out=rgv[:], in_=rgvec_d), w=["rgv"])
        P.act(lambda e: e.activation(out=sc2, in_=c2, func=AF.Silu), r=["c2"], w=["sc2"])
        P.dve(lambda e: e.tensor_copy(out=sc2b_t[:], in_=sc2), r=["sc2"], w=["sc2b"])
        P.dve(lambda e: e.tensor_tensor(out=tmpv[:, 0, :], in0=hgv[:, 0, :], in1=hgv[:, 1, :], op=ALU.max), r=["hgv"], w=["tmpv0"])
        P.dve(lambda e: e.tensor_tensor(out=tmpv[:, 0, :], in0=tmpv[:, 0, :], in1=hgv[:, 2, :], op=ALU.max), r=["hgv", "tmpv0"], w=["tmpv0"])
        P.dve(lambda e: e.tensor_tensor(out=hgv[:, 0:3, :], in0=hgv[:, 0:3, :], in1=tmpv[:, 0:1, :].to_broadcast([128, 3, 16]),
                                        op=ALU.subtract), r=["hgv", "tmpv0"], w=["hgv"])
        P.act(lambda e: e.activation(out=hgv[:, 0:3, :], in_=hgv[:, 0:3, :], func=AF.Exp), r=["hgv"], w=["hgv"])
        P.dve(lambda e: e.tensor_tensor(out=tmpv[:, 1, :], in0=hgv[:, 0, :], in1=hgv[:, 1, :], op=ALU.add), r=["hgv"], w=["tmpv1"])
        P.dve(lambda e: e.tensor_tensor(out=tmpv[:, 1, :], in0=tmpv[:, 1, :], in1=hgv[:, 2, :], op=ALU.add), r=["hgv", "tmpv1"], w=["tmpv1"])
        P.dve(lambda e: e.reciprocal(out=tmpv[:, 2, :], in_=tmpv[:, 1, :]), r=["tmpv1"], w=["tmpv2"])
        P.dve(lambda e: e.tensor_tensor(out=lbv, in0=hgv[:, 0, :], in1=tmpv[:, 2, :], op=ALU.mult), r=["hgv", "tmpv2"], w=["lbv"])
        P.dve(lambda e: e.tensor_scalar(out=omlv, in0=lbv, scalar1=-1.0, scalar2=1.0, op0=ALU.mult, op1=ALU.add), r=["lbv"], w=["omlv"])
        ngv = hgv[:, 3, :]
        P.dve(lambda e: e.tensor_scalar(out=c1v, in0=omlv, scalar1=0.5, scalar2=None, op0=ALU.mult), r=["omlv"], w=["c1v"])
        P.dve(lambda e: e.tensor_scalar(out=c0v, in0=lbv, scalar1=0.5, scalar2=0.5, op0=ALU.mult, op1=ALU.add), r=["lbv"], w=["c0v"])
        P.pool(lambda e: e.memset(nhalf, -0.5), w=["nhalf"])
        P.pool(lambda e: e.memset(halfc, 0.5), w=["halfc"])
        lamv = rgv[:, :, 9:11]
        P.act(lambda e: e.activation(out=clv, in_=lamv, func=AF.Exp, scale=-1.0), r=["rgv"], w=["clv"])
        P.act(lambda e: e.activation(out=clv, in_=clv, func=AF.Ln, bias=1.0), r=["clv"], w=["clv"])
        P.dve(lambda e: e.tensor_scalar(out=cl2v, in0=clv, scalar1=-16.0, scalar2=None, op0=ALU.mult), r=["clv"], w=["cl2v"])
        P.dve(lambda e: e.tensor_scalar(out=clhv, in0=clv, scalar1=8.0, scalar2=None, op0=ALU.mult), r=["clv"], w=["clhv"])
        P.dve(lambda e: e.tensor_scalar(out=clv, in0=clv, scalar1=-8.0, scalar2=None, op0=ALU.mult), r=["clv", "cl2v", "clhv"], w=["clv"])
        P.dve(lambda e: e.tensor_scalar(out=chhv, in0=clhv, scalar1=0.5, scalar2=None, op0=ALU.mult), r=["clhv"], w=["chhv"])
        P.dve(lambda e: e.tensor_scalar(out=clqv, in0=clv, scalar1=0.5, scalar2=None, op0=ALU.mult), r=["clv"], w=["clqv"])
        P.dve(lambda e: e.tensor_scalar(out=rgvh[:], in0=rgv[:, :, 5:9], scalar1=0.5, scalar2=None, op0=ALU.mult), r=["rgv"], w=["rgvh"])

        def phase_mod(layer, scratch_off, bct_off):
            A.at(scratch_off)
            wada = [A.bf16(8 * 512).rearrange("p (k n) -> p k n", n=512) for _ in range(2)]
            modL = A.f32(3072)
            modC = A.f32(3072)
            rows = A.f32(5120)
            gsrow = A.f32(2048)
            L = "L%d" % layer
            P.dma(lambda e: e.dma_start(out=rows[0:1, 0:3072], in_=rows_d[:, ROW_ADAB[layer]:ROW_ADAB[layer] + 3072]), w=[L + "rows"])
            P.dma(lambda e: e.dma_start(out=rows[0:1, 3072:4096], in_=rows_d[:, ROW_NG[layer]:ROW_NG[layer] + 1024]), w=[L + "rowsg"])
            P.dma(lambda e: e.dma_start(out=rows[0:1, 4096:5120], in_=rows_d[:, ROW_FNG:ROW_FNG + 1024]), w=[L + "rowsf"])
            for nb in range(6):
                wb = wada[nb % 2]
                wk = L + "wada%d" % (nb % 2)
                P.dma(lambda e, wb=wb, nb=nb: e.dma_start(out=wb, in_=adaw_d[layer, nb]), w=[wk], q="pool")
                for r, mod in ((0, modL), (1, modC)):
                    bk, bkey = newbank()
                    for kc in range(8):
                        P.pe(lambda e, bk=bk, wb=wb, kc=kc, r=r: e.matmul(
                            bk[0:1, :], lhsT=sc2b_t[:, r * 8 + kc:r * 8 + kc + 1], rhs=wb[:, kc, :],
                            start=(kc == 0), stop=(kc == 7)), r=[wk, "sc2b"], w=bkey)
                    P.dve(lambda e, bk=bk, mod=mod, nb=nb: e.tensor_tensor(
                        out=mod[0:1, nb * 512:(nb + 1) * 512], in0=bk[0:1, :], in1=rows[0:1, nb * 512:(nb + 1) * 512],
                        op=ALU.add), r=bkey + [L + "rows"], w=[L + "mod%d_%d" % (r, nb)])
            modkeys = lambda r, lo, hi: [L + "mod%d_%d" % (r, nb) for nb in range(lo // 512, (hi + 511) // 512)]
            for r, mod in ((0, modL), (1, modC)):
                P.dve(lambda e, mod=mod, r=r: e.scalar_tensor_tensor(
                    out=gsrow[0:1, r * 1024:(r + 1) * 1024], in0=mod[0:1, 1024:2048], scalar=1.0, in1=rows[0:1, 3072:4096],
                    op0=ALU.add, op1=ALU.mult), r=modkeys(r, 1024, 2048) + [L + "rowsg"], w=[L + "gsrow%d" % r])
            A.at(bct_off)
            bct = [A.f32(1024) for _ in range(4)]

            def bcast(dst, src_row, rkeys, wkey):
                for hf in range(2):
                    bk, bkey = newbank()
                    P.pe(lambda e, bk=bk, hf=hf: e.matmul(bk[:, :], lhsT=ones_f[0:1, :], rhs=src_row[0:1, hf * 512:(hf + 1) * 512],
                                                          start=True, stop=True), r=rkeys + ["ones_f"], w=bkey)
                    P.act(lambda e, bk=bk, hf=hf: e.copy(out=dst[:, hf * 512:(hf + 1) * 512], in_=bk[:, :]), r=bkey, w=[wkey + str(hf)])
                return [wkey + "0", wkey + "1"]

            keys = {}
            keys["gsL"] = bcast(bct[0], gsrow[:, 0:1024], [L + "gsrow0"], L + "gsL")
            keys["shL"] = bcast(bct[1], modL[:, 0:1024], modkeys(0, 0, 1024), L + "shL")
            keys["gsC"] = bcast(bct[2], gsrow[:, 1024:2048], [L + "gsrow1"], L + "gsC")
            keys["shC"] = bcast(bct[3], modC[:, 0:1024], modkeys(1, 0, 1024), L + "shC")
            keys["gateL"] = bcast(gate_l, modL[:, 2048:3072], modkeys(0, 2048, 3072), L + "gateL")
            if layer == 0:
                keys["gateC"] = bcast(gate_c, modC[:, 2048:3072], modkeys(1, 2048, 3072), L + "gateC")
            else:
                keys["fng"] = bcast(gate_c, rows[:, 4096:5120], [L + "rowsf"], L + "fng")
            return bct, keys

        def phase_norm(layer, bct, keys, hT, src_tiles, permute_lat):
            L = "L%d" % layer
            xs = [A.f32(1024) for _ in range(2)]
            tmpn = A.f32(1024)
            hb = [A.bf16(1024) for _ in range(2)]
            junk = A.bf16(1024)
            for j in range(18):
                sx = xs[j % 2]
                xk = L + "xs%d" % (j % 2)
                src = src_tiles(j)
                P.dma(lambda e, sx=sx, src=src: e.dma_start(out=sx, in_=src[0]), r=src[1], w=[xk])
                ssj = ss_t[:, (j % 2):(j % 2) + 1]
                rsj = rstd_t[:, (j % 2):(j % 2) + 1]
                sk = L + "ss%d" % (j % 2)
                P.act(lambda e, sx=sx, ssj=ssj: e.activation(out=junk, in_=sx, func=AF.Square, accum_out=ssj), r=[xk], w=["junk", sk])
                P.act(lambda e, ssj=ssj, rsj=rsj: e.activation(out=rsj, in_=ssj, func=AF.Ln, bias=EPS, scale=1.0 / D), r=[sk], w=[sk + "r"])
                P.act(lambda e, rsj=rsj: e.activation(out=rsj, in_=rsj, func=AF.Exp, scale=-0.5), r=[sk + "r"], w=[sk + "r"])
                gs, sh = (bct[2], bct[3]) if j < 2 else (bct[0], bct[1])
                gk, shk = (keys["gsC"], keys["shC"]) if j < 2 else (keys["gsL"], keys["shL"])
                P.dve(lambda e, sx=sx, rsj=rsj, gs=gs: e.scalar_tensor_tensor(out=tmpn, in0=sx, scalar=rsj, in1=gs, op0=ALU.mult, op1=ALU.mult),
                      r=[xk, sk + "r"] + gk, w=[L + "tmpn"])
                hbj = hb[j % 2]
                hk = L + "hb%d" % (j % 2)
                P.dve(lambda e, hbj=hbj, sh=sh: e.tensor_tensor(out=hbj, in0=tmpn, in1=sh, op=ALU.add), r=[L + "tmpn"] + shk, w=[hk])
                for g4 in range(2):
                    bk, bkey = newbank()
                    bkb = bk[:].bitcast(BF16)
                    for q in range(4):
                        kc = g4 * 4 + q
                        P.pe(lambda e, bkb=bkb, q=q, kc=kc, hbj=hbj: e.transpose(bkb[:, q * 128:(q + 1) * 128], hbj[:, kc * 128:(kc + 1) * 128], ident[:]),
                             r=[hk, "ident"], w=bkey)
                    src_v = bkb[:, 0:512].rearrange("p (q c) -> p q c", c=128)
                    if permute_lat and j >= 2:
                        jl = j - 2
                        eng = P.act if g4 == 0 else P.dve
                        for q in range(4):
                            kc = g4 * 4 + q
                            dst = hT[:, kc, 256:2304].rearrange("p (w r) -> p w r", r=32)[:, :, 2 * jl:2 * jl + 2]
                            srcq = bkb[:, q * 128:(q + 1) * 128].rearrange("p (r w) -> p w r", r=2)
                            if g4 == 0:
                                P.act(lambda e, dst=dst, srcq=srcq: e.copy(out=dst, in_=srcq), r=bkey, w=[L + "hT%d_%d" % (j, kc)])
                            else:
                                P.dve(lambda e, dst=dst, srcq=srcq: e.tensor_copy(out=dst, in_=srcq), r=bkey, w=[L + "hT%d_%d" % (j, kc)])
                    else:
                        dst = hT[:, g4 * 4:(g4 + 1) * 4, j * 128:(j + 1) * 128]
                        wk = [L + "hT%d_%d" % (j, g4 * 4 + q) for q in range(4)]
                        if g4 == 0:
                            P.act(lambda e, dst=dst, src_v=src_v: e.copy(out=dst, in_=src_v), r=bkey, w=wk)
                        else:
                            P.dve(lambda e, dst=dst, src_v=src_v: e.tensor_copy(out=dst, in_=src_v), r=bkey, w=wk)

        def hT_keys(layer, c0, c1, permuted):
            L = "L%d" % layer
            if permuted and c1 > 256:
                tiles = set(range(2, 18))
                if c0 < 256:
                    tiles |= set(range(c0 // 128, 2))
            else:
                tiles = set(range(c0 // 128, (c1 + 127) // 128))
            return [L + "hT%d_%d" % (j, kc) for j in sorted(tiles) for kc in range(8)]

        def phase_out(layer, Oall, okeys_for_tile, wo_d, final, ec0=0, nec=16, from_x1=False, tag=""):
            L = "L%d" % layer + tag
            nh2 = nec // 2
            wo = A.bf16(nec * 1024).rearrange("p (k n) -> p k n", n=1024)
            xs = [A.f32(1024) for _ in range(2)]
            tmpo = [A.f32(512) for _ in range(2)]
            junk = A.bf16(1024)
            for half in range(2):
                P.dma(lambda e, half=half: e.dma_start(out=wo[:, half * nh2:(half + 1) * nh2, :], in_=wo_d[:, ec0 + half * nh2:ec0 + (half + 1) * nh2, :]),
                      w=[L + "wo%d" % half], q="pool")
            tiles = range(18) if layer == 0 else range(2, 18)
            for j in tiles:
                sx = xs[j % 2]
                xk = L + "oxs%d" % (j % 2)
                if layer == 0 and not from_x1:
                    src = (ctx_d[j * 128:(j + 1) * 128, :], []) if j < 2 else (x_d[(j - 2) * 128:(j - 1) * 128, :], [])
                elif layer == 0:
                    src = (ctx1_d[j * 128:(j + 1) * 128, :], ["ctx1_%d" % j]) if j < 2 else (x1_d[(j - 2) * 128:(j - 1) * 128, :], ["x1_%d" % (j - 2)])
                else:
                    src = (x1_d[(j - 2) * 128:(j - 1) * 128, :], ["x1_%d" % (j - 2)])
                P.dma(lambda e, sx=sx, src=src: e.dma_start(out=sx, in_=src[0]), r=src[1], w=[xk])
                gt = gate_c if (layer == 0 and j < 2) else gate_l
                L_ = "L%d" % layer
                gk = (L_ + "gateC0", L_ + "gateC1") if (layer == 0 and j < 2) else (L_ + "gateL0", L_ + "gateL1")
                col0 = j * 128 if layer == 0 else (j - 2) * 128
                for half in range(2):
                    bk, bkey = newbank()
                    for ec in range(nec):
                        P.pe(lambda e, bk=bk, ec=ec, half=half, col0=col0: e.matmul(
                            bk[:, :], lhsT=Oall[:, ec, col0:col0 + 128], rhs=wo[:, ec, half * 512:(half + 1) * 512],
                            start=(ec == 0), stop=(ec == nec - 1)), r=okeys_for_tile(j, ec) + [L + "wo%d" % (ec // nh2)], w=bkey)
                    tp = tmpo[half]
                    tk = L + "tmpo%d" % half
                    P.dve(lambda e, bk=bk, tp=tp, gt=gt, half=half: e.tensor_tensor(out=tp, in0=bk[:, :], in1=gt[:, half * 512:(half + 1) * 512], op=ALU.mult),
                          r=bkey + [gk[half]], w=[tk])
                    P.dve(lambda e, sx=sx, tp=tp, half=half: e.tensor_tensor(out=sx[:, half * 512:(half + 1) * 512], in0=tp, in1=sx[:, half * 512:(half + 1) * 512], op=ALU.add),
                          r=[tk, xk], w=[xk])
                if not final:
                    if j < 2:
                        P.dma(lambda e, sx=sx, j=j: e.dma_start(out=ctx1_d[j * 128:(j + 1) * 128, :], in_=sx), r=[xk], w=["ctx1_%d" % j])
                    else:
                        P.dma(lambda e, sx=sx, j=j: e.dma_start(out=x1_d[(j - 2) * 128:(j - 1) * 128, :], in_=sx), r=[xk], w=["x1_%d" % (j - 2)])
                else:
                    ssj = ss_t[:, 2 + (j % 2):3 + (j % 2)]
                    rsj = rstd_t[:, 2 + (j % 2):3 + (j % 2)]
                    sk = L + "oss%d" % (j % 2)
                    P.act(lambda e, sx=sx, ssj=ssj: e.activation(out=junk, in_=sx, func=AF.Square, accum_out=ssj), r=[xk], w=["ojunk", sk])
                    P.act(lambda e, ssj=ssj, rsj=rsj: e.activation(out=rsj, in_=ssj, func=AF.Ln, bias=EPS, scale=1.0 / D), r=[sk], w=[sk + "r"])
                    P.act(lambda e, rsj=rsj: e.activation(out=rsj, in_=rsj, func=AF.Exp, scale=-0.5), r=[sk + "r"], w=[sk + "r"])
                    P.dve(lambda e, sx=sx, rsj=rsj: e.scalar_tensor_tensor(out=sx, in0=sx, scalar=rsj, in1=gate_c, op0=ALU.mult, op1=ALU.mult),
                          r=[xk, sk + "r", L_ + "fng0", L_ + "fng1"], w=[xk])
                    P.dma(lambda e, sx=sx, j=j: e.dma_start(out=out_d[(j - 2) * 128:(j - 1) * 128, :], in_=sx), r=[xk], w=["out_%d" % (j - 2)])

        A.at(0)
        Oall = A.bf16(8 * NT).rearrange("p (h t) -> p h t", t=NT)
        hT = A.bf16(8 * NT).rearrange("p (k t) -> p k t", t=NT)
        heads_off = A.off
        bct, mkeys = phase_mod(0, heads_off + 8768, heads_off)
        phase_norm(0, bct, mkeys, hT,
                   lambda j: ((ctx_d[j * 128:(j + 1) * 128, :], []) if j < 2 else (x_d[(j - 2) * 128:(j - 1) * 128, :], [])),
                   permute_lat=False)
        P.barrier()

        A.at(heads_off)
        Qd = {"f": A.bf16(NT), "b": A.bf16(NT)}
        Kd = {"f": A.bf16(NT), "b": A.bf16(NT)}
        vT = A.bf16(NCH * 128).rearrange("p (n v) -> p n v", v=128)
        KT = {"f": A.bf16(NCH * 128).rearrange("p (n v) -> p n v", v=128),
              "b": A.bf16(NCH * 128).rearrange("p (n v) -> p n v", v=128)}
        Gs = [A.bf16(NT), A.bf16(NT)]
        Sbf = {"f": A.bf16(NCH * 128).rearrange("p (n v) -> p n v", v=128),
               "b": A.bf16(NCH * 128).rearrange("p (n v) -> p n v", v=128)}
        Tst = {"f": [A.f32(128), A.f32(128)], "b": [A.f32(128), A.f32(128)]}
        q32s = [A.f32(512) for _ in range(2)]
        sgts = [{"f": A.f32(512), "b": A.f32(512)} for _ in range(2)]
        lgts = [{"f": A.f32(512), "b": A.f32(512)} for _ in range(2)]
        pfts = [{"f": A.f32(512), "b": A.f32(512)} for _ in range(2)]
        e1s = [A.f32(512) for _ in range(2)]
        e2s = [A.f32(512) for _ in range(2)]
        vfms = [A.bf16(512) for _ in range(2)]
        tot4s = [small[:, 318:322], small[:, 524:528]]
        wbuf = [A.bf16(8 * 640).rearrange("p (k n) -> p k n", n=640)]
        Ps = {"f": [A.bf16(128), A.bf16(128)], "b": [A.bf16(128), A.bf16(128)]}
        osq = A.bf16(512)
        o32 = A.f32(512)
        rt = A.f32(512)
        l0_end = A.off
        global DBG_OFFS
        DBG_OFFS = dict(heads_off=heads_off, l0_end=l0_end)

        TBS = [(0, 512), (512, 512), (1024, 512), (1536, 512), (2048, 256)]
        ORDER = {"f": list(range(18)), "b": [1, 0] + list(range(17, 1, -1))}
        Av = {"f": AFv, "b": ABv}
        Bv = {"f": BFv, "b": BBv}
        Rv = {"f": RFv, "b": RBv}
        NH = 16 if stop_after is None else stop_after

        def emit_pb1(h, bi):
            s0, sz = TBS[bi]
            ncb = sz // 128
            n0 = s0 // 128
            ob, okey = pbanks[5], ["pb5s%d" % q_ for q_ in range(4)]
            for c2_ in range(0, ncb, 2):
                bk, bkeys = pbanks[6], ["pb6s%d" % q_ for q_ in range(4)]
                for cc in range(2):
                    n = n0 + c2_ + cc
                    cs = slice(n * 128, (n + 1) * 128)
                    c0_ = n * 128
                    for di, d in enumerate(("f", "b")):
                        sl = cc * 2 + di
                        o0 = sl * 128
                        rk_ = ["K%s%d" % (d, bi), "Q%s%d" % (d, bi)]
                        if d == "f":
                            P.pe(lambda e, bk=bk, o0=o0, c0_=c0_: e.matmul(bk[:, o0 + 64:o0 + 128], lhsT=Kd["f"][:, c0_:c0_ + 128], rhs=Qd["f"][:, c0_ + 64:c0_ + 128],
                                                                           start=True, stop=True), r=rk_, w=bkeys)
                            P.pe(lambda e, bk=bk, o0=o0, c0_=c0_: e.matmul(bk[0:64, o0:o0 + 64], lhsT=Kd["f"][:, c0_:c0_ + 64], rhs=Qd["f"][:, c0_:c0_ + 64],
                                                                           start=True, stop=True), r=rk_, w=bkeys)
                        else:
                            P.pe(lambda e, bk=bk, o0=o0, c0_=c0_: e.matmul(bk[:, o0:o0 + 64], lhsT=Kd["b"][:, c0_:c0_ + 128], rhs=Qd["b"][:, c0_:c0_ + 64],
                                                                           start=True, stop=True), r=rk_, w=bkeys)
                            P.pe(lambda e, bk=bk, o0=o0, c0_=c0_: e.matmul(bk[64:128, o0 + 64:o0 + 128], lhsT=Kd["b"][:, c0_ + 64:c0_ + 128], rhs=Qd["b"][:, c0_ + 64:c0_ + 128],
                                                                           start=True, stop=True), r=rk_, w=bkeys)
                for cc in range(2):
                    n = n0 + c2_ + cc
                    for di, d in enumerate(("f", "b")):
                        sl = cc * 2 + di
                        mk, mkk = MASKS[d]
                        pst = Ps[d][n % 2]
                        pkey = "Ps%s%d" % (d, n % 2)
                        P.dve(lambda e, bk=bk, sl=sl, mk=mk, pst=pst: e.tensor_tensor(out=pst, in0=bk[:, sl * 128:(sl + 1) * 128], in1=mk[:], op=ALU.mult),
                              r=bkeys + [mkk], w=[pkey])
                for cc in range(2):
                    c = c2_ + cc
                    n = n0 + c
                    cs = slice(n * 128, (n + 1) * 128)
                    oc = ob[:, c * 128:(c + 1) * 128]
                    P.pe(lambda e, oc=oc, n=n: e.matmul(oc, lhsT=vT[:, n, :], rhs=Ps["f"][n % 2], start=True, stop=False),
                         r=["Tv%d" % (n // 4), "Psf%d" % (n % 2)], w=okey)
                    P.pe(lambda e, oc=oc, n=n: e.matmul(oc, lhsT=vT[:, n, :], rhs=Ps["b"][n % 2], start=False, stop=False),
                         r=["Tv%d" % (n // 4), "Psb%d" % (n % 2)], w=okey)
                    P.pe(lambda e, oc=oc, n=n, cs=cs: e.matmul(oc, lhsT=Sbf["f"][:, n, :], rhs=Qd["f"][:, cs], start=False, stop=False),
                         r=["Sbff%d" % n, "Qf%d" % bi], w=okey)
                    P.pe(lambda e, oc=oc, n=n, cs=cs: e.matmul(oc, lhsT=Sbf["b"][:, n, :], rhs=Qd["b"][:, cs], start=False, stop=True),
                         r=["Sbfb%d" % n, "Qb%d" % bi], w=okey)
            return ob, okey

        def emit_pb2(h, bi, ob, okey):
            s0, sz = TBS[bi]
            P.act(lambda e, ob=ob, sz=sz: e.activation(out=osq[:, 0:sz], in_=ob[:, 0:sz], func=AF.Square), r=okey, w=["osq"])
            P.act(lambda e, ob=ob, sz=sz: e.copy(out=o32[:, 0:sz], in_=ob[:, 0:sz]), r=okey, w=["o32"])
            sb_, sskey = pbanks[6], ["pb6s%d" % q_ for q_ in range(4)]
            P.pe(lambda e, sb_=sb_, sz=sz: e.matmul(sb_[:, 0:sz], lhsT=ones_bf[:], rhs=osq[:, 0:sz], start=True, stop=True), r=["osq", "ones_bf"], w=sskey)
            P.act(lambda e, sb_=sb_, sz=sz: e.activation(out=rt[:, 0:sz], in_=sb_[:, 0:sz], func=AF.Ln, bias=EPS, scale=1.0 / 128), r=sskey, w=["rt"])
            P.act(lambda e, sz=sz: e.activation(out=rt[:, 0:sz], in_=rt[:, 0:sz], func=AF.Exp, scale=-0.5), r=["rt"], w=["rt"])
            P.dve(lambda e, sz=sz, h=h: e.scalar_tensor_tensor(out=o32[:, 0:sz], in0=o32[:, 0:sz], scalar=ngv[:, h:h + 1], in1=rt[:, 0:sz], op0=ALU.mult, op1=ALU.mult),
                  r=["o32", "rt", "hgv"], w=["o32"])
            P.dve(lambda e, s0=s0, sz=sz, h=h: e.tensor_tensor(out=Oall[:, h % 8, s0:s0 + sz], in0=o32[:, 0:sz], in1=Gs[h % 2][:, s0:s0 + sz], op=ALU.mult),
                  r=["o32", "G%d_%d" % (h % 2, bi)], w=["O%d_%d" % (h, bi)])

        def emit_phaseB(h, bi):
            ob, okey = emit_pb1(h, bi)
            emit_pb2(h, bi, ob, okey)

        def out_pass(ec0, from_x1, tag):
            P.barrier()
            A.at(heads_off)
            blk = lambda j: min(j // 4, 4)
            phase_out(0, Oall, lambda j, ec: ["O%d_%d" % (ec0 + ec, blk(j))], hgwo_d, final=False, ec0=ec0, nec=8, from_x1=from_x1, tag=tag)
            P.barrier()

        for h in range(NH):
            H = "h%d" % h
            wb = wbuf[0]
            if h == 8:
                for bi in range(5):
                    emit_phaseB(7, bi)
                out_pass(0, False, "a")
            wk = "wbuf0"
            if h == 0 or NO_PREFETCH:
                P.dma(lambda e, wb=wb, h=h: e.dma_start(out=wb, in_=hgw_d[h]), w=[wk], q="pool")
            c0_h = c0v[:, h:h + 1]
            c1_h = c1v[:, h:h + 1]

            def emit_proj(bi, wb=wb, wk=wk):
                s0, sz = TBS[bi]
                hk = hT_keys(0, s0, s0 + sz, False)
                banks = []
                for jj in range(5):
                    bk, bkey = bank_proj()
                    banks.append((bk, bkey))
                    for kc in range(8):
                        P.pe(lambda e, bk=bk, wb=wb, kc=kc, jj=jj, s0=s0, sz=sz: e.matmul(
                            bk[:, 0:sz], lhsT=wb[:, kc, jj * 128:(jj + 1) * 128], rhs=hT[:, kc, s0:s0 + sz],
                            start=(kc == 0), stop=(kc == 7)), r=[wk] + hk, w=bkey)
                return banks

            def emit_evac(bi, banks):
                s0, sz = TBS[bi]
                p = bi % 2
                K = lambda nm: nm + "_%d" % p
                q32 = q32s[p]; sgt = sgts[p]; lgt = lgts[p]; pft = pfts[p]
                e1t = {"f": e1s[p], "b": e1s[p]}; e2t = {"f": e2s[p], "b": e2s[p]}
                vfm = vfms[p]; tot4 = tot4s[p]
                (bq, kq), (bv, kv), (bzf, kzf), (bzb, kzb), (bg, kg) = banks
                bz = {"f": (bzf, kzf), "b": (bzb, kzb)}
                P.act(lambda e, bq=bq, sz=sz: e.activation(out=q32[:, 0:sz], in_=bq[:, 0:sz], func=AF.Silu), r=kq, w=[K("q32")])
                P.act(lambda e, bg=bg, s0=s0, sz=sz, Gh=Gs[h % 2]: e.activation(out=Gh[:, s0:s0 + sz], in_=bg[:, 0:sz], func=AF.Silu), r=kg, w=["G%d_%d" % (h % 2, bi)])
                P.dve(lambda e, bv=bv, sz=sz: e.tensor_copy(out=vfm[:, 0:sz], in_=bv[:, 0:sz]), r=kv, w=[K("vfm")])
                for d in ("f", "b"):
                    bzd, kzd = bz[d]
                    sg = sgt[d]
                    P.act(lambda e, bzd=bzd, sg=sg, sz=sz: e.activation(out=sg[:, 0:sz], in_=bzd[:, 0:sz], func=AF.Tanh, scale=0.5), r=kzd, w=[K("sg" + d)])

            def emit_rest1(bi, c0_h=c0_h, c1_h=c1_h, H=H):
                s0, sz = TBS[bi]
                p = bi % 2
                K = lambda nm: nm + "_%d" % p
                q32 = q32s[p]; sgt = sgts[p]; lgt = lgts[p]; pft = pfts[p]
                e1t = {"f": e1s[p], "b": e1s[p]}; e2t = {"f": e2s[p], "b": e2s[p]}
                vfm = vfms[p]; tot4 = tot4s[p]
                ncb = sz // 128
                n0 = s0 // 128
                for d in ("f", "b"):
                    sg, lg, pf = sgt[d], lgt[d], pft[d]
                    P.dve(lambda e, sg=sg, sz=sz: e.tensor_scalar(out=sg[:, 0:sz], in0=sg[:, 0:sz], scalar1=c1_h, scalar2=c0_h, op0=ALU.mult, op1=ALU.add),
                          r=[K("sg" + d), "c0v", "c1v"], w=[K("sg" + d)])
                    P.act(lambda e, sg=sg, lg=lg, sz=sz: e.activation(out=lg[:, 0:sz], in_=sg[:, 0:sz], func=AF.Ln), r=[K("sg" + d)], w=[K("lg" + d)])
                    P.pool(lambda e, sg=sg, sz=sz: e.tensor_scalar(out=sg[:, 0:sz], in0=sg[:, 0:sz], scalar1=-1.0, scalar2=1.0, op0=ALU.mult, op1=ALU.add),
                           r=[K("sg" + d)], w=[K("sg" + d)])
                lg, pf = lgt["f"], pft["f"]
                P.dve(lambda e, lg=lg, pf=pf, sz=sz: e.tensor_tensor_scan(out=pf[:, 0:sz], data0=smask[:, 0:sz], data1=lg[:, 0:sz], initial=0.0,
                                                                        op0=ALU.mult, op1=ALU.add), r=[K("lgf"), "smask"], w=[K("pff")])
                lg, pf = lgt["b"], pft["b"]
                P.pool(lambda e, pf=pf: e.memset(pf[:, 0:1], 0.0), w=[K("pfb0")])
                P.dve(lambda e, lg=lg, pf=pf, sz=sz: e.tensor_tensor_scan(out=pf[:, 1:sz], data0=lg[:, 0:sz - 1], data1=smask[:, 1:sz], initial=0.0,
                                                                        op0=ALU.add, op1=ALU.mult), r=[K("lgb"), "smask"], w=[K("pfb")])
                pf3 = pft["f"][:, 0:sz].rearrange("p (c j) -> p c j", j=128)
                pb3 = pft["b"][:, 0:sz].rearrange("p (c j) -> p c j", j=128)
                lb3 = lgt["b"][:, 0:sz].rearrange("p (c j) -> p c j", j=128)
                ex = H + "ex%d" % bi
                P.pool(lambda e, pf3=pf3, n0=n0, ncb=ncb: e.tensor_copy(out=AFv[:, n0:n0 + ncb], in_=pf3[:, :, MID_F]), r=[K("pff")], w=[ex + "AF"])
                P.pool(lambda e, pf3=pf3, n0=n0, ncb=ncb: e.tensor_tensor(out=BFv[:, n0:n0 + ncb], in0=pf3[:, :, 127], in1=pf3[:, :, MID_F], op=ALU.subtract),
                       r=[K("pff")], w=[ex + "BF"])
                P.pool(lambda e, pb3=pb3, lb3=lb3, ncb=ncb: e.tensor_tensor(out=tot4[:, 0:ncb], in0=pb3[:, :, 127], in1=lb3[:, :, 127], op=ALU.add),
                       r=[K("pfb"), K("pfb0"), K("lgb")], w=[K("tot4")])
                P.pool(lambda e, pb3=pb3, n0=n0, ncb=ncb: e.tensor_tensor(out=ABv[:, n0:n0 + ncb], in0=tot4[:, 0:ncb], in1=pb3[:, :, MID_B], op=ALU.subtract),
                       r=[K("tot4"), K("pfb")], w=[ex + "AB"])
                P.pool(lambda e, pb3=pb3, n0=n0, ncb=ncb: e.tensor_copy(out=BBv[:, n0:n0 + ncb], in_=pb3[:, :, MID_B]), r=[K("pfb")], w=[ex + "BB"])
                for d, mid, rk in (("f", MID_F, [K("pff")]), ("b", MID_B, [K("pfb"), K("pfb0")])):
                    p3 = pft[d][:, 0:sz].rearrange("p (c j) -> p c j", j=128)
                    l3 = lgt[d][:, 0:sz].rearrange("p (c j) -> p c j", j=128)
                    P.dve(lambda e, p3=p3, l3=l3, mid=mid, ncb=ncb: e.tensor_tensor(out=l3, in0=p3, in1=p3[:, :, mid:mid + 1].to_broadcast([128, ncb, 128]),
                                                                                    op=ALU.subtract), r=rk + [K("lg" + d)], w=[K("lg" + d)])

            def emit_rest2(bi):
                s0, sz = TBS[bi]
                p = bi % 2
                K = lambda nm: nm + "_%d" % p
                q32 = q32s[p]; sgt = sgts[p]; lgt = lgts[p]; pft = pfts[p]
                e1t = {"f": e1s[p], "b": e1s[p]}; e2t = {"f": e2s[p], "b": e2s[p]}
                vfm = vfms[p]; tot4 = tot4s[p]
                for d, sq, sk_ in (("f", 1.0, -1.0), ("b", -1.0, 1.0)):
                    lg, e1, e2, sg = lgt[d], e1t[d], e2t[d], sgt[d]
                    P.act(lambda e, lg=lg, e1=e1, sq=sq, sz=sz: e.activation(out=e1[:, 0:sz], in_=lg[:, 0:sz], func=AF.Exp, scale=sq), r=[K("lg" + d)], w=[K("e1")])
                    P.act(lambda e, lg=lg, e2=e2, sk_=sk_, sz=sz: e.activation(out=e2[:, 0:sz], in_=lg[:, 0:sz], func=AF.Exp, scale=sk_), r=[K("lg" + d)], w=[K("e2")])
                    P.pool(lambda e, e1=e1, d=d, s0=s0, sz=sz: e.tensor_tensor(out=Qd[d][:, s0:s0 + sz], in0=q32[:, 0:sz], in1=e1[:, 0:sz], op=ALU.mult),
                           r=[K("q32"), K("e1")], w=["Q%s%d" % (d, bi)])
                    P.pool(lambda e, e2=e2, sg=sg, d=d, s0=s0, sz=sz: e.tensor_tensor(out=Kd[d][:, s0:s0 + sz], in0=sg[:, 0:sz], in1=e2[:, 0:sz], op=ALU.mult),
                           r=[K("sg" + d), K("e2")], w=["K%s%d" % (d, bi)])

            def emit_tr(bi):
                s0, sz = TBS[bi]
                p = bi % 2
                K = lambda nm: nm + "_%d" % p
                q32 = q32s[p]; sgt = sgts[p]; lgt = lgts[p]; pft = pfts[p]
                e1t = {"f": e1s[p], "b": e1s[p]}; e2t = {"f": e2s[p], "b": e2s[p]}
                vfm = vfms[p]; tot4 = tot4s[p]
                ncb = sz // 128
                n0 = s0 // 128
                for nm, srcf, rkey, dstT in (("v", lambda c: vfm[:, c * 128:(c + 1) * 128], K("vfm"), vT),
                                             ("kf", lambda c, s0=s0: Kd["f"][:, s0 + c * 128:s0 + (c + 1) * 128], "Kf%d" % bi, KT["f"]),
                                             ("kb", lambda c, s0=s0: Kd["b"][:, s0 + c * 128:s0 + (c + 1) * 128], "Kb%d" % bi, KT["b"])):
                    bk, bkey = pbanks[7], ["pb7s%d" % q_ for q_ in range(4)]
                    bkb = bk[:].bitcast(BF16)
                    for c in range(ncb):
                        src = srcf(c)
                        P.pe(lambda e, bkb=bkb, c=c, src=src: e.transpose(bkb[:, c * 128:(c + 1) * 128], src, ident[:]), r=[rkey, "ident"], w=bkey)
                    dst = dstT[:, n0:n0 + ncb, :]
                    srcv = bkb[:, 0:ncb * 128].rearrange("p (c v) -> p c v", v=128)
                    wkey = "T%s%d" % (nm, bi)
                    P.act(lambda e, dst=dst, srcv=srcv: e.copy(out=dst, in_=srcv), r=bkey, w=[wkey])

            interleave = h > 0 and h != 8
            banks = emit_proj(0)
            pb0 = emit_pb1(h - 1, 0) if interleave else None
            emit_evac(0, banks)
            emit_rest1(0)
            if pb0 is not None:
                emit_pb2(h - 1, 0, pb0[0], pb0[1])
            for bi in range(5):
                emit_rest2(bi)
                pb = None
                if bi + 1 < 5:
                    banks = emit_proj(bi + 1)
                    if interleave:
                        pb = emit_pb1(h - 1, bi + 1)
                    emit_evac(bi + 1, banks)
                emit_tr(bi)
                if bi + 1 < 5:
                    emit_rest1(bi + 1)
                if pb is not None:
                    emit_pb2(h - 1, bi + 1, pb[0], pb[1])
            if h + 1 < NH and not NO_PREFETCH:
                P.dma(lambda e, wb=wb, h=h: e.dma_start(out=wb, in_=hgw_d[h + 1]), w=[wk], q="pool")
            exk = lambda nm: [H + "ex%d" % bi + nm for bi in range(5)]
            P.pool(lambda e: e.tensor_tensor(out=TRv[:, 1:18], in0=AFv[:, 1:18], in1=BFv[:, 0:17], op=ALU.add), r=exk("AF") + exk("BF"), w=["TRv"])
            P.act(lambda e: e.activation(out=RFv[:, 1:18], in_=TRv[:, 1:18], func=AF.Exp), r=["TRv"], w=["RF"])
            P.pool(lambda e: e.tensor_tensor(out=TRv[:, 0:17], in0=ABv[:, 0:17], in1=BBv[:, 1:18], op=ALU.add), r=exk("AB") + exk("BB") + ["RF"], w=["TRv"])
            P.pool(lambda e: e.tensor_tensor(out=TRv[:, 17:18], in0=ABv[:, 17:18], in1=BBv[:, 0:1], op=ALU.add), r=exk("AB") + exk("BB") + ["TRv"], w=["TRv"])
            P.act(lambda e: e.activation(out=RBv[:, 0:18], in_=TRv[:, 0:18], func=AF.Exp), r=["TRv"], w=["RB"])
            for j2 in range(0, 18, 2):
                bk, bkeys = bank_aux()
                for jj in range(2):
                    j = j2 + jj
                    for di, d in enumerate(("f", "b")):
                        n = ORDER[d][j]
                        sl = jj * 2 + di
                        P.pe(lambda e, bk=bk, sl=sl, n=n, d=d: e.matmul(bk[:, sl * 128:(sl + 1) * 128], lhsT=KT[d][:, n, :], rhs=vT[:, n, :], start=True, stop=True),
                             r=["Tk%s%d" % (d, n // 4), "Tv%d" % (n // 4)], w=bkeys)
                for jj in range(2):
                    j = j2 + jj
                    for di, d in enumerate(("f", "b")):
                        n = ORDER[d][j]
                        sl = jj * 2 + di
                        tcur = Tst[d][j % 2]
                        tprev = Tst[d][(j + 1) % 2]
                        tk, tpk = "Tst%s%d" % (d, j % 2), "Tst%s%d" % (d, (j + 1) % 2)
                        skey = "Sbf%s%d" % (d, n)
                        if j == 0:
                            P.pool(lambda e, d=d, n=n: e.memset(Sbf[d][:, n, :], 0.0), w=[skey])
                            P.dve(lambda e, bk=bk, sl=sl, tcur=tcur: e.tensor_copy(out=tcur, in_=bk[:, sl * 128:(sl + 1) * 128]), r=bkeys, w=[tk])
                        else:
                            rcol = Rv[d][:, n:n + 1]
                            rk = "RF" if d == "f" else "RB"
                            P.pool(lambda e, d=d, n=n, tprev=tprev, rcol=rcol: e.tensor_scalar(out=Sbf[d][:, n, :], in0=tprev, scalar1=rcol, scalar2=1.0, op0=ALU.mult, op1=ALU.mult),
                                   r=[tpk, rk], w=[skey])
                            P.dve(lambda e, bk=bk, sl=sl, tcur=tcur, tprev=tprev, rcol=rcol: e.scalar_tensor_tensor(
                                out=tcur, in0=tprev, scalar=rcol, in1=bk[:, sl * 128:(sl + 1) * 128], op0=ALU.mult, op1=ALU.add),
                                r=[tpk, rk] + bkeys, w=[tk])
            if HEAD_BARRIER:
                P.barrier()

        for bi in range(5):
            emit_phaseB(NH - 1, bi)
        if NH <= 8:
            if NH < 8:
                P.pool(lambda e: e.memset(Oall[:, NH:8, :], 0.0), w=["O%d_%d" % (hh_, b_) for hh_ in range(NH, 8) for b_ in range(5)])
            out_pass(0, False, "a")
        else:
            out_pass(8, True, "b")

        final_keys = []
        if stop_after is None:
            A.at(0)
            hT1 = A.bf16(8 * NT).rearrange("p (k t) -> p k t", t=NT)
            O1 = A.bf16(16 * 2048).rearrange("p (h t) -> p h t", t=2048)
            l1_off = A.off
            bct, mkeys = phase_mod(1, 9216, 26624)
            phase_norm(1, bct, mkeys, hT1,
                       lambda j: ((ctx1_d[j * 128:(j + 1) * 128, :], ["ctx1_%d" % j]) if j < 2
                                  else (x1_d[(j - 2) * 128:(j - 1) * 128, :], ["x1_%d" % (j - 2)])),
                       permute_lat=True)
            P.barrier()
            A.at(l1_off)
            xr_c = A.f32(260)
            xr_l = A.f32(2052)
            xc = A.f32(2 * NT).rearrange("p (a t) -> p a t", t=NT)
            xcb = A.bf16(2 * NT).rearrange("p (a t) -> p a t", t=NT)
            gsl = A.bf16(2 * 2048).rearrange("p (a t) -> p a t", t=2048)
            hf = A.f32(2048)
            hctx = A.f32(256)
            rgwb = [A.bf16(2 * 8 * 256).rearrange("p (a k n) -> p a k n", k=8, n=256)]
            axb = [A.bf16(8 * 256).rearrange("p (q n) -> p q n", n=256)]
            tr_ = A.f32(512); tig = A.f32(512); ta = A.f32(512); ta2 = A.f32(512); tth = A.f32(512)
            tu = A.f32(512); thb = A.f32(512)
            s2all = A.f32(NT)
            l1_end = A.off
            P.pool(lambda e: e.memset(xr_c[:, :], 0.0), w=["xr_c"])
            P.pool(lambda e: e.memset(xr_l[:, :], 0.0), w=["xr_l"])
            LB = [(0, 256), (256, 512), (768, 512), (1280, 512), (1792, 512)]
            for hh in range(8):
                wb = rgwb[0]
                wk = "rgwb0"
                ab = axb[0]
                ak = "axb0"
                if hh == 0:
                    P.dma(lambda e, wb=wb, hh=hh: e.dma_start(out=wb, in_=rgw_d[hh]), w=[wk], q="pool")
                if hh == 0:
                    P.dma(lambda e, ab=ab, hh=hh: e.dma_start(out=ab, in_=rgax_d[hh]), w=[ak], q="pool")
                for a in range(2):
                    ct = 2 * hh + a
                    cw = rgv[:, ct, 0:4]
                    cb = rgv[:, ct, 4:5]
                    for bi, (s0, sz) in enumerate(LB):
                        hk = hT_keys(1, s0, s0 + sz, True)
                        bx, kx = newbank()
                        bg, kg = newbank()
                        for (bk, bkey, co) in ((bx, kx, 0), (bg, kg, 128)):
                            if co == 128 and bi == 0:
                                continue
                            for kc in range(8):
                                P.pe(lambda e, bk=bk, wb=wb, a=a, kc=kc, co=co, s0=s0, sz=sz: e.matmul(
                                    bk[:, 0:sz], lhsT=wb[:, a, kc, co:co + 128], rhs=hT1[:, kc, s0:s0 + sz],
                                    start=(kc == 0), stop=(kc == 7)), r=[wk] + hk, w=bkey)
                        if bi == 0:
                            P.act(lambda e, bx=bx: e.copy(out=xr_c[:, 2:258], in_=bx[:, 0:256]), r=kx, w=["xr_c"])
                        else:
                            l0 = s0 - 256
                            P.act(lambda e, bx=bx, l0=l0: e.copy(out=xr_l[:, 2 + l0:2 + l0 + 512], in_=bx[:, 0:512]), r=kx, w=["xr_l"])
                            P.act(lambda e, bg=bg: e.activation(out=tth[:, 0:512], in_=bg[:, 0:512], func=AF.Tanh, scale=0.5), r=kg, w=["tth"])
                            P.dve(lambda e, bg=bg, a=a, l0=l0: e.scalar_tensor_tensor(out=gsl[:, a, l0:l0 + 512], in0=tth[:, 0:512], scalar=1.0, in1=bg[:, 0:512],
                                                                                    op0=ALU.add, op1=ALU.mult), r=kg + ["tth"], w=["gsl%d" % a])
                    for (xr, xk, c0, ln) in ((xr_c, "xr_c", 0, 256), (xr_l, "xr_l", 256, 2048)):
                        dst = xc[:, a, c0:c0 + ln]
                        ck = "xc%d_%d" % (a, 0 if c0 == 0 else 1)
                        P.dve(lambda e, xr=xr, dst=dst, ln=ln, cw=cw, cb=cb: e.tensor_scalar(out=dst, in0=xr[:, 2:2 + ln], scalar1=cw[:, 2:3], scalar2=cb, op0=ALU.mult, op1=ALU.add),
                              r=[xk, "rgv"], w=[ck])
                        for tap, off in ((0, 0), (1, 1), (3, 3)):
                            P.dve(lambda e, xr=xr, dst=dst, ln=ln, cw=cw, tap=tap, off=off: e.scalar_tensor_tensor(
                                out=dst, in0=xr[:, off:off + ln], scalar=cw[:, tap:tap + 1], in1=dst, op0=ALU.mult, op1=ALU.add),
                                r=[xk, "rgv", ck], w=[ck])
                        P.pool(lambda e, dst=dst, a=a, c0=c0, ln=ln: e.tensor_copy(out=xcb[:, a, c0:c0 + ln], in_=dst), r=[ck], w=["xcb%d_%d" % (a, 0 if c0 == 0 else 1)])
                if hh + 1 < 8:
                    P.dma(lambda e, wb=wb, hh=hh: e.dma_start(out=wb, in_=rgw_d[hh + 1]), w=[wk], q="pool")
                xcbk = lambda bi: ["xcb0_%d" % (0 if bi == 0 else 1), "xcb1_%d" % (0 if bi == 0 else 1)]
                for ao in range(2):
                    ct = 2 * hh + ao
                    for d in (0, 1):
                        order = list(range(5)) if d == 0 else [0, 4, 3, 2, 1]
                        prev_last = None
                        hb_a = rgvh[:, ct, d:d + 1]
                        hb_x = rgvh[:, ct, 2 + d:3 + d]
                        cl = clv[:, ct, d:d + 1]
                        clq = clqv[:, ct, d:d + 1]
                        chh = chhv[:, ct, d:d + 1]

                        def emit_coef(bi, both, ao=ao, d=d, ab=ab, ak=ak):
                            s0, sz = LB[bi]
                            res = []
                            for axi in ((0, 1) if both else (0,)):
                                bk, bkey = newbank()
                                for ic in range(2):
                                    q = (axi * 2 + d) * 2 + ic
                                    P.pe(lambda e, bk=bk, ab=ab, q=q, ao=ao, ic=ic, s0=s0, sz=sz: e.matmul(
                                        bk[:, 0:sz], lhsT=ab[:, q, ao * 128:(ao + 1) * 128], rhs=xcb[:, ic, s0:s0 + sz],
                                        start=(ic == 0), stop=(ic == 1)), r=[ak] + xcbk(bi), w=bkey)
                                res.append((bk, bkey))
                            return res

                        nxt = emit_coef(0, False)
                        for bi in range(5):
                            s0, sz = LB[bi]
                            (ba, ka), = nxt
                            P.act(lambda e, ba=ba, sz=sz, hb_a=hb_a: e.activation(out=tr_[:, 0:sz], in_=ba[:, 0:sz], func=AF.Tanh, bias=hb_a, scale=0.5), r=ka + ["rgvh"], w=["tr"])
                            if bi + 1 < 5:
                                nxt = emit_coef(bi + 1, False)
                            P.act(lambda e, sz=sz, chh=chh: e.activation(out=tth[:, 0:sz], in_=tr_[:, 0:sz], func=AF.Tanh, bias=chh, scale=chh), r=["tr", "chhv"], w=["tth"])
                            P.act(lambda e, sz=sz, cl=cl: e.activation(out=ta2[:, 0:sz], in_=tr_[:, 0:sz], func=AF.Exp, bias=cl, scale=cl), r=["tr", "clv"], w=["ta2"])
                            P.dve(lambda e, sz=sz, s0=s0: e.scalar_tensor_tensor(out=s2all[:, s0:s0 + sz], in0=ta2[:, 0:sz], scalar=1.0, in1=tth[:, 0:sz], op0=ALU.add, op1=ALU.mult),
                                  r=["ta2", "tth"], w=["s2_%d" % bi])
                        s2k = ["s2_%d" % b_ for b_ in range(5)]
                        P.act(lambda e: e.activation(out=s2all[:, :], in_=s2all[:, :], func=AF.Sqrt), r=s2k, w=s2k)
                        nxt = emit_coef(order[0], True)
                        for step, bi in enumerate(order):
                            s0, sz = LB[bi]
                            (ba, ka), (bx, kx) = nxt
                            P.act(lambda e, ba=ba, sz=sz, hb_a=hb_a: e.activation(out=tr_[:, 0:sz], in_=ba[:, 0:sz], func=AF.Tanh, bias=hb_a, scale=0.5), r=ka + ["rgvh"], w=["tr"])
                            P.act(lambda e, bx=bx, sz=sz, hb_x=hb_x: e.activation(out=tig[:, 0:sz], in_=bx[:, 0:sz], func=AF.Tanh, bias=hb_x, scale=0.5), r=kx + ["rgvh"], w=["tig"])
                            if step + 1 < len(order):
                                nxt = emit_coef(order[step + 1], True)
                            P.act(lambda e, sz=sz, clq=clq: e.activation(out=ta[:, 0:sz], in_=tr_[:, 0:sz], func=AF.Exp, bias=clq, scale=clq), r=["tr", "clqv"], w=["ta"])
                            P.dve(lambda e, sz=sz, ao=ao, s0=s0: e.scalar_tensor_tensor(out=tu[:, 0:sz], in0=tig[:, 0:sz], scalar=1.0, in1=xc[:, ao, s0:s0 + sz],
                                                                                       op0=ALU.add, op1=ALU.mult),
                                  r=["tig", "xc%d_%d" % (ao, 0 if bi == 0 else 1)], w=["tu"])
                            P.pool(lambda e, sz=sz, s0=s0: e.tensor_tensor(out=tu[:, 0:sz], in0=tu[:, 0:sz], in1=s2all[:, s0:s0 + sz], op=ALU.mult), r=["tu", "s2_%d" % bi], w=["tu"])
                            if d == 0:
                                dst = hctx[:, 0:256] if bi == 0 else hf[:, s0 - 256:s0 - 256 + sz]
                                dkey = "hctx" if bi == 0 else "hf%d" % bi
                                init = 0.0 if step == 0 else prev_last
                                P.dve(lambda e, dst=dst, sz=sz, init=init: e.tensor_tensor_scan(out=dst, data0=ta[:, 0:sz], data1=tu[:, 0:sz], initial=init,
                                                                                             op0=ALU.mult, op1=ALU.add),
                                      r=["ta", "tu"] + ([] if step == 0 else [pkey_prev]), w=[dkey])
                                prev_last = dst[:, sz - 1:sz]
                                pkey_prev = dkey
                            else:
                                dst = hctx[:, 0:256] if bi == 0 else thb[:, 0:sz]
                                dkey = "hctx" if bi == 0 else "thb"
                                init = 0.0 if step == 0 else prev_last
                                rk = ["ta", "tu"] + ([] if step == 0 else ["carry"])
                                P.dve(lambda e, dst=dst, sz=sz, init=init: e.tensor_tensor_scan(out=dst[:, ::-1], data0=ta[:, 0:sz][:, ::-1], data1=tu[:, 0:sz][:, ::-1],
                                                                                             initial=init, op0=ALU.mult, op1=ALU.add), r=rk, w=[dkey])
                                cc = carry[:, 0:1]
                                P.pool(lambda e, dst=dst, cc=cc: e.tensor_copy(out=cc, in_=dst[:, 0:1]), r=[dkey], w=["carry"])
                                prev_last = cc
                                if bi > 0:
                                    l0 = s0 - 256
                                    P.pool(lambda e, sz=sz, l0=l0: e.tensor_tensor(out=thb[:, 0:sz], in0=thb[:, 0:sz], in1=hf[:, l0:l0 + sz], op=ALU.add),
                                           r=["thb", "carry", "hf%d" % bi], w=["thb"])
                                    w0 = l0 // 32
                                    dsto = O1[:, ct, :].rearrange("p (r w) -> p w r", w=64)[:, w0:w0 + 16, :]
                                    srcs = thb[:, 0:512].rearrange("p (w r) -> p w r", r=32)
                                    srcg = gsl[:, ao, l0:l0 + 512].rearrange("p (w r) -> p w r", r=32)
                                    P.dve(lambda e, dsto=dsto, srcs=srcs, srcg=srcg: e.scalar_tensor_tensor(out=dsto, in0=srcs, scalar=0.25, in1=srcg, op0=ALU.mult, op1=ALU.mult),
                                          r=["thb", "gsl%d" % ao], w=["O1_%d" % ct])
                if hh + 1 < 8:
                    P.dma(lambda e, ab=ab, hh=hh: e.dma_start(out=ab, in_=rgax_d[hh + 1]), w=[ak], q="pool")
            P.barrier()
            A.at(l1_off)
            phase_out(1, O1, lambda j, ec: ["O1_%d" % ec], rgwo_d, final=True)
            final_keys = ["out_%d" % j for j in range(16)]
        else:
            final_keys = ["x1_%d" % j for j in range(16)] + ["ctx1_0", "ctx1_1"]
        P.finalize(final_keys)
        print("ops:", P.n_ops, "arena L0 end:", l0_end, "of", AW)
    return nc


def prep_inputs(inputs):
    f = lambda a: np.ascontiguousarray(np.asarray(a, dtype=np.float32))
    x, c, ctx, c_ctx = f(inputs["x"]), f(inputs["c"]), f(inputs["ctx"]), f(inputs["c_ctx"])
    ada_w, ada_b = f(inputs["ada_w"]), f(inputs["ada_b"])
    adaw = np.stack([ada_w[i].reshape(8, 128, 6, 512).transpose(2, 1, 0, 3) for i in range(2)], 0)
    rows = np.concatenate([ada_b[0], ada_b[1], f(inputs["norm_g"])[0], f(inputs["norm_g"])[1], f(inputs["final_norm_g"])])[None, :]
    hw = f(inputs["hg_w_in"])[0].reshape(8, 128, 5, 16, 128)
    hgw = hw.transpose(3, 1, 0, 2, 4).reshape(16, 128, 8, 640)
    hgwo = f(inputs["hg_w_out"])[0].reshape(16, 128, D).transpose(1, 0, 2)
    lbr = f(inputs["hg_lower_bounds"]).reshape(3, 16, 128)
    hgn = f(inputs["hg_norm_g"])[0].reshape(1, 16, 128)
    hgvec = np.concatenate([lbr, hgn], 0).transpose(2, 0, 1)
    rw = f(inputs["rg_w_in"])[0].reshape(8, 128, 2, 8, 2, 128)
    rgw = rw.transpose(3, 1, 4, 0, 2, 5).reshape(8, 128, 2, 8, 256)
    wa, wx = f(inputs["rg_w_a"])[0], f(inputs["rg_w_x"])[0]
    wax = np.stack([wa, wx], 0).reshape(2, 2, 8, 2, 128, 256)
    rgax = wax.transpose(2, 4, 0, 1, 3, 5).reshape(8, 128, 8, 256)
    rgwo = f(inputs["rg_w_out"])[0].reshape(16, 128, D).transpose(1, 0, 2)
    cw = f(inputs["rg_conv_w"])[0].reshape(4, 16, 128)
    cb = f(inputs["rg_conv_b"])[0].reshape(1, 16, 128)
    ba = f(inputs["rg_b_a"])[0].reshape(2, 16, 128)
    bx = f(inputs["rg_b_x"])[0].reshape(2, 16, 128)
    lam = f(inputs["rg_lambda"])[0].reshape(2, 16, 128)
    rgvec = np.concatenate([cw, cb, ba, bx, lam], 0).transpose(2, 1, 0)
    shared = dict(adaw=f(adaw), rows=f(rows), hgw=f(hgw), hgwo=f(hgwo), hgvec=f(hgvec), rgw=f(rgw), rgax=f(rgax),
                  rgwo=f(rgwo), rgvec=f(rgvec))
    maps = []
    for b in range(8):
        c2 = np.concatenate([c[b].reshape(8, 128).T, c_ctx.reshape(8, 128).T], 1)
        m = dict(shared)
        m.update(x=f(x[b]), ctx=f(ctx[b]), c2=f(c2))
        maps.append(m)
    return maps


def kernel(**inputs):
    maps = prep_inputs(inputs)
    nc = build_nc()
    res = run_bass_kernel_spmd(nc, maps, core_ids=list(range(8)))
    return np.stack([np.asarray(r["out"], dtype=np.float32) for r in res.results], 0)
```

```python
import contextlib
import numpy as np
import concourse.bass as bass
import concourse.mybir as mybir
from concourse.bass_utils import run_bass_kernel_spmd

F32 = mybir.dt.float32
BF16 = mybir.dt.bfloat16
AF = mybir.ActivationFunctionType
ALU = mybir.AluOpType

import os
HEAD_BARRIER = os.environ.get("HEAD_BARRIER", "0") == "1"
NO_PREFETCH = os.environ.get("NO_PREFETCH", "0") == "1"
D = 1024
E = 2048
NT = 2304
NCH = 18
EPS = 1e-6
MID_F = 63
MID_B = 64


class Prog:
    ENGS = ("pe", "act", "dve", "pool", "sp")
    N_DMA_SEMS = {"sp": 12, "pool": 6, "act": 4}

    def __init__(self, nc):
        self.nc = nc
        self.ops = []

    def op(self, eng, fn, reads=(), writes=(), dma=False, barrier=False):
        self.ops.append(dict(eng=eng, fn=fn, reads=tuple(reads), writes=tuple(writes), dma=dma, barrier=barrier))
        return len(self.ops) - 1

    def pe(self, fn, r=(), w=()): return self.op("pe", fn, r, w)
    def act(self, fn, r=(), w=()): return self.op("act", fn, r, w)
    def dve(self, fn, r=(), w=()): return self.op("dve", fn, r, w)
    def pool(self, fn, r=(), w=()): return self.op("pool", fn, r, w)
    def dma(self, fn, r=(), w=(), q="sp"): return self.op(q, fn, r, w, dma=True)

    def barrier(self):
        for e in self.ENGS:
            self.op(e, None, barrier=True)

    def finalize(self, final_keys):
        nc = self.nc
        ops = self.ops
        self.op("sp", None, reads=final_keys)
        n = len(ops)
        last_writer, readers = {}, {}
        deps = [None] * n
        dma_cnt = {q: 0 for q in self.N_DMA_SEMS}
        dma_last_on_sem, dma_sem_of, dma_val_of, dma_semval = {}, {}, {}, {}
        last_on_eng = {}
        for i, o in enumerate(ops):
            d = set()
            if o["barrier"]:
                d.update(last_on_eng.values())
                d.update(dma_last_on_sem.values())
            for k in o["reads"]:
                if k in last_writer:
                    d.add(last_writer[k])
            for k in o["writes"]:
                if k in last_writer:
                    d.add(last_writer[k])
                d.update(readers.get(k, ()))
            if o["dma"]:
                q = o["eng"]
                s = (q, dma_cnt[q] % self.N_DMA_SEMS[q])
                dma_cnt[q] += 1
                if s in dma_last_on_sem:
                    d.add(dma_last_on_sem[s])
                dma_last_on_sem[s] = i
                dma_sem_of[i] = s
                dma_semval[s] = dma_semval.get(s, 0) + 16
                dma_val_of[i] = dma_semval[s]
            elif o["fn"] is not None:
                last_on_eng[o["eng"]] = i
            for k in o["reads"]:
                readers.setdefault(k, []).append(i)
            for k in o["writes"]:
                last_writer[k] = i
                readers[k] = []
            d.discard(i)
            if o["eng"] == "pe":
                d = {j for j in d if ops[j]["eng"] != "pe"}
            deps[i] = d
        needed = set()
        for i in range(n):
            needed.update(deps[i])
        sig = {}
        cnt = {e: 0 for e in self.ENGS}
        for i, o in enumerate(ops):
            if o["dma"] or o["fn"] is None:
                continue
            if i in needed:
                cnt[o["eng"]] += 1
                sig[i] = (("eng", o["eng"]), cnt[o["eng"]])
        for i in dma_sem_of:
            sig[i] = (("dma",) + dma_sem_of[i], dma_val_of[i])
        known = {e: {} for e in self.ENGS}
        clock = [None] * n
        waits = [None] * n
        for i, o in enumerate(ops):
            kn = known[o["eng"]]
            wm = {}
            for j in sorted(deps[i]):
                if j not in sig:
                    continue
                s, v = sig[j]
                if kn.get(s, 0) >= v:
                    continue
                wm[s] = max(wm.get(s, 0), v)
                for s2, v2 in clock[j].items():
                    if kn.get(s2, 0) < v2:
                        kn[s2] = v2
                kn[s] = v
            waits[i] = list(wm.items())
            clock[i] = dict(kn)
        with contextlib.ExitStack() as st:
            sems = {}
            for e in self.ENGS:
                sems[("eng", e)] = st.enter_context(nc.semaphore("s_" + e))
            for q, k in self.N_DMA_SEMS.items():
                for t in range(k):
                    sems[("dma", q, t)] = st.enter_context(nc.semaphore("d_%s%d" % (q, t)))
            block = st.enter_context(nc.Block())

            def make(ename):
                def body(eng):
                    for i, o in enumerate(ops):
                        if o["eng"] != ename:
                            continue
                        for s, v in waits[i]:
                            eng.wait_ge(sems[s], v)
                        if o["fn"] is None:
                            continue
                        ins = o["fn"](eng)
                        if i in sig:
                            ins.then_inc(sems[sig[i][0]], 16 if o["dma"] else 1)
                return body

            block.tensor(make("pe"))
            block.scalar(make("act"))
            block.vector(make("dve"))
            block.gpsimd(make("pool"))
            block.sync(make("sp"))
        self.n_ops = n


class Arena:
    def __init__(self, ap_all, words):
        self.ap = ap_all
        self.words = words
        self.off = 0

    def at(self, off):
        self.off = off

    def f32(self, n):
        v = self.ap[:, self.off:self.off + n]
        self.off += n
        assert self.off <= self.words, ("arena overflow", self.off, self.words)
        return v

    def bf16(self, n):
        w = (n + 1) // 2
        v = self.ap[:, self.off:self.off + w].bitcast(BF16)
        self.off += w
        assert self.off <= self.words, ("arena overflow", self.off, self.words)
        return v


ROW_ADAB = (0, 3072)
ROW_NG = (6144, 7168)
ROW_FNG = 8192
NROWS = 9216


def build_nc(dbg=False, stop_after=None):
    nc = bass.Bass("TRN2", target_bir_lowering=False)
    dt = nc.dram_tensor
    x_d = dt("x", [2048, D], F32, kind="ExternalInput").ap()
    ctx_d = dt("ctx", [256, D], F32, kind="ExternalInput").ap()
    c2_d = dt("c2", [128, 16], F32, kind="ExternalInput").ap()
    adaw_d = dt("adaw", [2, 6, 128, 8, 512], F32, kind="ExternalInput").ap()
    rows_d = dt("rows", [1, NROWS], F32, kind="ExternalInput").ap()
    hgw_d = dt("hgw", [16, 128, 8, 640], F32, kind="ExternalInput").ap()
    hgwo_d = dt("hgwo", [128, 16, D], F32, kind="ExternalInput").ap()
    hgvec_d = dt("hgvec", [128, 4, 16], F32, kind="ExternalInput").ap()
    rgw_d = dt("rgw", [8, 128, 2, 8, 256], F32, kind="ExternalInput").ap()
    rgax_d = dt("rgax", [8, 128, 8, 256], F32, kind="ExternalInput").ap()
    rgwo_d = dt("rgwo", [128, 16, D], F32, kind="ExternalInput").ap()
    rgvec_d = dt("rgvec", [128, 16, 11], F32, kind="ExternalInput").ap()
    out_d = dt("out", [2048, D], F32, kind="ExternalOutput").ap()
    kind1 = "ExternalOutput" if dbg else "Internal"
    x1_d = dt("x1", [2048, D], F32, kind=kind1).ap()
    ctx1_d = dt("ctx1", [256, D], F32, kind=kind1).ap()

    AW = 48800
    with contextlib.ExitStack() as st:
        T = lambda name, shape, dty: st.enter_context(nc.sbuf_tensor(name, shape, dty))
        arena_t = T("arena", [128, AW], F32)
        top_t = T("top", [128, 2048], F32)
        ident = T("ident", [128, 128], BF16)
        identf = T("identf", [128, 128], F32)
        ones_bf = T("ones_bf", [128, 128], BF16)
        ones_f = T("ones_f", [1, 128], F32)
        smask = T("smask", [128, 512], F32)
        m01L = T("m01L", [128, 128], F32)
        m01U = T("m01U", [128, 128], F32)
        small = T("small", [128, 640], F32)
        pbanks = [st.enter_context(nc.psum_tensor("pb%d" % i, [128, 512], F32)) for i in range(8)]

        P = Prog(nc)
        A = Arena(arena_t[:], AW)
        bank_ctr = [0]

        def newbank():
            i = bank_ctr[0] % 8
            bank_ctr[0] += 1
            return pbanks[i], ["pb%ds%d" % (i, q) for q in range(4)]

        pool_ctr = {"proj": 0, "aux": 0}

        def bank_proj():
            i = pool_ctr["proj"] % 5
            pool_ctr["proj"] += 1
            return pbanks[i], ["pb%ds%d" % (i, q) for q in range(4)]

        def bank_aux():
            i = 5 + pool_ctr["aux"] % 3
            pool_ctr["aux"] += 1
            return pbanks[i], ["pb%ds%d" % (i, q) for q in range(4)]

        gate_l = top_t[:, 0:1024]
        gate_c = top_t[:, 1024:2048]

        P.pool(lambda e: e.memset(identf[:], 1.0), w=["identf"])
        P.pool(lambda e: e.affine_select(out=identf[:], in_=identf[:], pattern=[[-1, 128]], compare_op=ALU.is_equal,
                                         fill=0.0, base=0, channel_multiplier=1), r=["identf"], w=["identf"])
        P.dve(lambda e: e.tensor_copy(out=ident[:], in_=identf[:]), r=["identf"], w=["ident"])
        P.pool(lambda e: e.memset(ones_bf[:], 1.0), w=["ones_bf"])
        P.pool(lambda e: e.memset(ones_f[:], 1.0), w=["ones_f"])
        P.pool(lambda e: e.memset(smask[:], 1.0), w=["smask"])
        smv = smask[:].rearrange("p (c j) -> p c j", j=128)
        P.pool(lambda e: e.memset(smv[:, :, 0:1], 0.0), r=["smask"], w=["smask"])
        for (m, cm, pat, key) in ((m01L, -1, 1, "maskL"), (m01U, 1, -1, "maskU")):
            P.pool(lambda e, m=m: e.memset(m[:], 1.0), w=[key])
            P.pool(lambda e, m=m, cm=cm, pat=pat: e.affine_select(
                out=m[:], in_=m[:], pattern=[[pat, 128]], compare_op=ALU.is_ge, fill=0.0, base=0,
                channel_multiplier=cm), r=[key], w=[key])
        MASKS = {"f": (m01L, "maskL"), "b": (m01U, "maskU")}
        for i in range(8):
            P.dve(lambda e, i=i: e.memset(pbanks[i][:], 0.0), w=["pb%ds%d" % (i, q) for q in range(4)])

        c2 = small[:, 0:16]
        sc2 = small[:, 16:32]
        hgv = small[:, 32:96].rearrange("p (a h) -> p a h", h=16)
        lbv = small[:, 96:112]
        omlv = small[:, 112:128]
        tmpv = small[:, 128:176].rearrange("p (a h) -> p a h", h=16)
        ss_t = small[:, 176:184]
        rstd_t = small[:, 184:192]
        AFv = small[:, 192:210]
        BFv = small[:, 210:228]
        ABv = small[:, 228:246]
        BBv = small[:, 246:264]
        RFv = small[:, 264:282]
        RBv = small[:, 282:300]
        TRv = small[:, 300:318]
        tot4 = small[:, 318:322]
        clv = small[:, 322:354].rearrange("p (c d) -> p c d", d=2)
        cl2v = small[:, 354:386].rearrange("p (c d) -> p c d", d=2)
        clhv = small[:, 386:418].rearrange("p (c d) -> p c d", d=2)
        carry = small[:, 418:426]
        nhalf = small[:, 426:427]
        halfc = small[:, 427:428]
        c0v = small[:, 428:444]
        c1v = small[:, 444:460]
        chhv = small[:, 460:492].rearrange("p (c d) -> p c d", d=2)
        clqv = small[:, 492:524].rearrange("p (c d) -> p c d", d=2)
        rgvh = T("rgvh", [128, 16, 4], F32)
        sc2b_t = T("sc2b", [128, 16], BF16)
        rgv = T("rgv", [128, 16, 11], F32)

        P.dma(lambda e: e.dma_start(out=c2, in_=c2_d), w=["c2"])
        P.dma(lambda e: e.dma_start(out=hgv, in_=hgvec_d), w=["hgv"])
        P.dma(lambda e: e.dma_start(out=rgv[:], in_=rgvec_d), w=["rgv"])
        P.act(lambda e: e.activation(out=sc2, in_=c2, func=AF.Silu), r=["c2"], w=["sc2"])
        P.dve(lambda e: e.tensor_copy(out=sc2b_t[:], in_=sc2), r=["sc2"], w=["sc2b"])
        P.dve(lambda e: e.tensor_tensor(out=tmpv[:, 0, :], in0=hgv[:, 0, :], in1=hgv[:, 1, :], op=ALU.max), r=["hgv"], w=["tmpv0"])
        P.dve(lambda e: e.tensor_tensor(out=tmpv[:, 0, :], in0=tmpv[:, 0, :], in1=hgv[:, 2, :], op=ALU.max), r=["hgv", "tmpv0"], w=["tmpv0"])
        P.dve(lambda e: e.tensor_tensor(out=hgv[:, 0:3, :], in0=hgv[:, 0:3, :], in1=tmpv[:, 0:1, :].to_broadcast([128, 3, 16]),
                                        op=ALU.subtract), r=["hgv", "tmpv0"], w=["hgv"])
        P.act(lambda e: e.activation(out=hgv[:, 0:3, :], in_=hgv[:, 0:3, :], func=AF.Exp), r=["hgv"], w=["hgv"])
        P.dve(lambda e: e.tensor_tensor(out=tmpv[:, 1, :], in0=hgv[:, 0, :], in1=hgv[:, 1, :], op=ALU.add), r=["hgv"], w=["tmpv1"])
        P.dve(lambda e: e.tensor_tensor(out=tmpv[:, 1, :], in0=tmpv[:, 1, :], in1=hgv[:, 2, :], op=ALU.add), r=["hgv", "tmpv1"], w=["tmpv1"])
        P.dve(lambda e: e.reciprocal(out=tmpv[:, 2, :], in_=tmpv[:, 1, :]), r=["tmpv1"], w=["tmpv2"])
        P.dve(lambda e: e.tensor_tensor(out=lbv, in0=hgv[:, 0, :], in1=tmpv[:, 2, :], op=ALU.mult), r=["hgv", "tmpv2"], w=["lbv"])
        P.dve(lambda e: e.tensor_scalar(out=omlv, in0=lbv, scalar1=-1.0, scalar2=1.0, op0=ALU.mult, op1=ALU.add), r=["lbv"], w=["omlv"])
        ngv = hgv[:, 3, :]
        P.dve(lambda e: e.tensor_scalar(out=c1v, in0=omlv, scalar1=0.5, scalar2=None, op0=ALU.mult), r=["omlv"], w=["c1v"])
        P.dve(lambda e: e.tensor_scalar(out=c0v, in0=lbv, scalar1=0.5, scalar2=0.5, op0=ALU.mult, op1=ALU.add), r=["lbv"], w=["c0v"])
        P.pool(lambda e: e.memset(nhalf, -0.5), w=["nhalf"])
        P.pool(lambda e: e.memset(halfc, 0.5), w=["halfc"])
        lamv = rgv[:, :, 9:11]
        P.act(lambda e: e.activation(out=clv, in_=lamv, func=AF.Exp, scale=-1.0), r=["rgv"], w=["clv"])
        P.act(lambda e: e.activation(out=clv, in_=clv, func=AF.Ln, bias=1.0), r=["clv"], w=["clv"])
        P.dve(lambda e: e.tensor_scalar(out=cl2v, in0=clv, scalar1=-16.0, scalar2=None, op0=ALU.mult), r=["clv"], w=["cl2v"])
        P.dve(lambda e: e.tensor_scalar(out=clhv, in0=clv, scalar1=8.0, scalar2=None, op0=ALU.mult), r=["clv"], w=["clhv"])
        P.dve(lambda e: e.tensor_scalar(out=clv, in0=clv, scalar1=-8.0, scalar2=None, op0=ALU.mult), r=["clv", "cl2v", "clhv"], w=["clv"])
        P.dve(lambda e: e.tensor_scalar(out=chhv, in0=clhv, scalar1=0.5, scalar2=None, op0=ALU.mult), r=["clhv"], w=["chhv"])
        P.dve(lambda e: e.tensor_scalar(out=clqv, in0=clv, scalar1=0.5, scalar2=None, op0=ALU.mult), r=["clv"], w=["clqv"])
        P.dve(lambda e: e.tensor_scalar(out=rgvh[:], in0=rgv[:, :, 5:9], scalar1=0.5, scalar2=None, op0=ALU.mult), r=["rgv"], w=["rgvh"])

        def phase_mod(layer, scratch_off, bct_off):
            A.at(scratch_off)
            wada = [A.bf16(8 * 512).rearrange("p (k n) -> p k n", n=512) for _ in range(2)]
            modL = A.f32(3072)
            modC = A.f32(3072)
            rows = A.f32(5120)
            gsrow = A.f32(2048)
            L = "L%d" % layer
            P.dma(lambda e: e.dma_start(out=rows[0:1, 0:3072], in_=rows_d[:, ROW_ADAB[layer]:ROW_ADAB[layer] + 3072]), w=[L + "rows"])
            P.dma(lambda e: e.dma_start(out=rows[0:1, 3072:4096], in_=rows_d[:, ROW_NG[layer]:ROW_NG[layer] + 1024]), w=[L + "rowsg"])
            P.dma(lambda e: e.dma_start(out=rows[0:1, 4096:5120], in_=rows_d[:, ROW_FNG:ROW_FNG + 1024]), w=[L + "rowsf"])
            for nb in range(6):
                wb = wada[nb % 2]
                wk = L + "wada%d" % (nb % 2)
                P.dma(lambda e, wb=wb, nb=nb: e.dma_start(out=wb, in_=adaw_d[layer, nb]), w=[wk], q="pool")
                for r, mod in ((0, modL), (1, modC)):
                    bk, bkey = newbank()
                    for kc in range(8):
                        P.pe(lambda e, bk=bk, wb=wb, kc=kc, r=r: e.matmul(
                            bk[0:1, :], lhsT=sc2b_t[:, r * 8 + kc:r * 8 + kc + 1], rhs=wb[:, kc, :],
                            start=(kc == 0), stop=(kc == 7)), r=[wk, "sc2b"], w=bkey)
                    P.dve(lambda e, bk=bk, mod=mod, nb=nb: e.tensor_tensor(
                        out=mod[0:1, nb * 512:(nb + 1) * 512], in0=bk[0:1, :], in1=rows[0:1, nb * 512:(nb + 1) * 512],
                        op=ALU.add), r=bkey + [L + "rows"], w=[L + "mod%d_%d" % (r, nb)])
            modkeys = lambda r, lo, hi: [L + "mod%d_%d" % (r, nb) for nb in range(lo // 512, (hi + 511) // 512)]
            for r, mod in ((0, modL), (1, modC)):
                P.dve(lambda e, mod=mod, r=r: e.scalar_tensor_tensor(
                    out=gsrow[0:1, r * 1024:(r + 1) * 1024], in0=mod[0:1, 1024:2048], scalar=1.0, in1=rows[0:1, 3072:4096],
                    op0=ALU.add, op1=ALU.mult), r=modkeys(r, 1024, 2048) + [L + "rowsg"], w=[L + "gsrow%d" % r])
            A.at(bct_off)
            bct = [A.f32(1024) for _ in range(4)]

            def bcast(dst, src_row, rkeys, wkey):
                for hf in range(2):
                    bk, bkey = newbank()
                    P.pe(lambda e, bk=bk, hf=hf: e.matmul(bk[:, :], lhsT=ones_f[0:1, :], rhs=src_row[0:1, hf * 512:(hf + 1) * 512],
                                                          start=True, stop=True), r=rkeys + ["ones_f"], w=bkey)
                    P.act(lambda e, bk=bk, hf=hf: e.copy(out=dst[:, hf * 512:(hf + 1) * 512], in_=bk[:, :]), r=bkey, w=[wkey + str(hf)])
                return [wkey + "0", wkey + "1"]

            keys = {}
            keys["gsL"] = bcast(bct[0], gsrow[:, 0:1024], [L + "gsrow0"], L + "gsL")
            keys["shL"] = bcast(bct[1], modL[:, 0:1024], modkeys(0, 0, 1024), L + "shL")
            keys["gsC"] = bcast(bct[2], gsrow[:, 1024:2048], [L + "gsrow1"], L + "gsC")
            keys["shC"] = bcast(bct[3], modC[:, 0:1024], modkeys(1, 0, 1024), L + "shC")
            keys["gateL"] = bcast(gate_l, modL[:, 2048:3072], modkeys(0, 2048, 3072), L + "gateL")
            if layer == 0:
                keys["gateC"] = bcast(gate_c, modC[:, 2048:3072], modkeys(1, 2048, 3072), L + "gateC")
            else:
                keys["fng"] = bcast(gate_c, rows[:, 4096:5120], [L + "rowsf"], L + "fng")
            return bct, keys

        def phase_norm(layer, bct, keys, hT, src_tiles, permute_lat):
            L = "L%d" % layer
            xs = [A.f32(1024) for _ in range(2)]
            tmpn = A.f32(1024)
            hb = [A.bf16(1024) for _ in range(2)]
            junk = A.bf16(1024)
            for j in range(18):
                sx = xs[j % 2]
                xk = L + "xs%d" % (j % 2)
                src = src_tiles(j)
                P.dma(lambda e, sx=sx, src=src: e.dma_start(out=sx, in_=src[0]), r=src[1], w=[xk])
                ssj = ss_t[:, (j % 2):(j % 2) + 1]
                rsj = rstd_t[:, (j % 2):(j % 2) + 1]
                sk = L + "ss%d" % (j % 2)
                P.act(lambda e, sx=sx, ssj=ssj: e.activation(out=junk, in_=sx, func=AF.Square, accum_out=ssj), r=[xk], w=["junk", sk])
                P.act(lambda e, ssj=ssj, rsj=rsj: e.activation(out=rsj, in_=ssj, func=AF.Ln, bias=EPS, scale=1.0 / D), r=[sk], w=[sk + "r"])
                P.act(lambda e, rsj=rsj: e.activation(out=rsj, in_=rsj, func=AF.Exp, scale=-0.5), r=[sk + "r"], w=[sk + "r"])
                gs, sh = (bct[2], bct[3]) if j < 2 else (bct[0], bct[1])
                gk, shk = (keys["gsC"], keys["shC"]) if j < 2 else (keys["gsL"], keys["shL"])
                P.dve(lambda e, sx=sx, rsj=rsj, gs=gs: e.scalar_tensor_tensor(out=tmpn, in0=sx, scalar=rsj, in1=gs, op0=ALU.mult, op1=ALU.mult),
                      r=[xk, sk + "r"] + gk, w=[L + "tmpn"])
                hbj = hb[j % 2]
                hk = L + "hb%d" % (j % 2)
                P.dve(lambda e, hbj=hbj, sh=sh: e.tensor_tensor(out=hbj, in0=tmpn, in1=sh, op=ALU.add), r=[L + "tmpn"] + shk, w=[hk])
                for g4 in range(2):
                    bk, bkey = newbank()
                    bkb = bk[:].bitcast(BF16)
                    for q in range(4):
                        kc = g4 * 4 + q
                        P.pe(lambda e, bkb=bkb, q=q, kc=kc, hbj=hbj: e.transpose(bkb[:, q * 128:(q + 1) * 128], hbj[:, kc * 128:(kc + 1) * 128], ident[:]),
                             r=[hk, "ident"], w=bkey)
                    src_v = bkb[:, 0:512].rearrange("p (q c) -> p q c", c=128)
                    if permute_lat and j >= 2:
                        jl = j - 2
                        eng = P.act if g4 == 0 else P.dve
                        for q in range(4):
                            kc = g4 * 4 + q
                            dst = hT[:, kc, 256:2304].rearrange("p (w r) -> p w r", r=32)[:, :, 2 * jl:2 * jl + 2]
                            srcq = bkb[:, q * 128:(q + 1) * 128].rearrange("p (r w) -> p w r", r=2)
                            if g4 == 0:
                                P.act(lambda e, dst=dst, srcq=srcq: e.copy(out=dst, in_=srcq), r=bkey, w=[L + "hT%d_%d" % (j, kc)])
                            else:
                                P.dve(lambda e, dst=dst, srcq=srcq: e.tensor_copy(out=dst, in_=srcq), r=bkey, w=[L + "hT%d_%d" % (j, kc)])
                    else:
                        dst = hT[:, g4 * 4:(g4 + 1) * 4, j * 128:(j + 1) * 128]
                        wk = [L + "hT%d_%d" % (j, g4 * 4 + q) for q in range(4)]
                        if g4 == 0:
                            P.act(lambda e, dst=dst, src_v=src_v: e.copy(out=dst, in_=src_v), r=bkey, w=wk)
                        else:
                            P.dve(lambda e, dst=dst, src_v=src_v: e.tensor_copy(out=dst, in_=src_v), r=bkey, w=wk)

        def hT_keys(layer, c0, c1, permuted):
            L = "L%d" % layer
            if permuted and c1 > 256:
                tiles = set(range(2, 18))
                if c0 < 256:
                    tiles |= set(range(c0 // 128, 2))
            else:
                tiles = set(range(c0 // 128, (c1 + 127) // 128))
            return [L + "hT%d_%d" % (j, kc) for j in sorted(tiles) for kc in range(8)]

        def phase_out(layer, Oall, okeys_for_tile, wo_d, final, ec0=0, nec=16, from_x1=False, tag=""):
            L = "L%d" % layer + tag
            nh2 = nec // 2
            wo = A.bf16(nec * 1024).rearrange("p (k n) -> p k n", n=1024)
            xs = [A.f32(1024) for _ in range(2)]
            tmpo = [A.f32(512) for _ in range(2)]
            junk = A.bf16(1024)
            for half in range(2):
                P.dma(lambda e, half=half: e.dma_start(out=wo[:, half * nh2:(half + 1) * nh2, :], in_=wo_d[:, ec0 + half * nh2:ec0 + (half + 1) * nh2, :]),
                      w=[L + "wo%d" % half], q="pool")
            tiles = range(18) if layer == 0 else range(2, 18)
            for j in tiles:
                sx = xs[j % 2]
                xk = L + "oxs%d" % (j % 2)
                if layer == 0 and not from_x1:
                    src = (ctx_d[j * 128:(j + 1) * 128, :], []) if j < 2 else (x_d[(j - 2) * 128:(j - 1) * 128, :], [])
                elif layer == 0:
                    src = (ctx1_d[j * 128:(j + 1) * 128, :], ["ctx1_%d" % j]) if j < 2 else (x1_d[(j - 2) * 128:(j - 1) * 128, :], ["x1_%d" % (j - 2)])
                else:
                    src = (x1_d[(j - 2) * 128:(j - 1) * 128, :], ["x1_%d" % (j - 2)])
                P.dma(lambda e, sx=sx, src=src: e.dma_start(out=sx, in_=src[0]), r=src[1], w=[xk])
                gt = gate_c if (layer == 0 and j < 2) else gate_l
                L_ = "L%d" % layer
                gk = (L_ + "gateC0", L_ + "gateC1") if (layer == 0 and j < 2) else (L_ + "gateL0", L_ + "gateL1")
                col0 = j * 128 if layer == 0 else (j - 2) * 128
                for half in range(2):
                    bk, bkey = newbank()
                    for ec in range(nec):
                        P.pe(lambda e, bk=bk, ec=ec, half=half, col0=col0: e.matmul(
                            bk[:, :], lhsT=Oall[:, ec, col0:col0 + 128], rhs=wo[:, ec, half * 512:(half + 1) * 512],
                            start=(ec == 0), stop=(ec == nec - 1)), r=okeys_for_tile(j, ec) + [L + "wo%d" % (ec // nh2)], w=bkey)
                    tp = tmpo[half]
                    tk = L + "tmpo%d" % half
                    P.dve(lambda e, bk=bk, tp=tp, gt=gt, half=half: e.tensor_tensor(out=tp, in0=bk[:, :], in1=gt[:, half * 512:(half + 1) * 512], op=ALU.mult),
                          r=bkey + [gk[half]], w=[tk])
                    P.dve(lambda e, sx=sx, tp=tp, half=half: e.tensor_tensor(out=sx[:, half * 512:(half + 1) * 512], in0=tp, in1=sx[:, half * 512:(half + 1) * 512], op=ALU.add),
                          r=[tk, xk], w=[xk])
                if not final:
                    if j < 2:
                        P.dma(lambda e, sx=sx, j=j: e.dma_start(out=ctx1_d[j * 128:(j + 1) * 128, :], in_=sx), r=[xk], w=["ctx1_%d" % j])
                    else:
                        P.dma(lambda e, sx=sx, j=j: e.dma_start(out=x1_d[(j - 2) * 128:(j - 1) * 128, :], in_=sx), r=[xk], w=["x1_%d" % (j - 2)])
                else:
                    ssj = ss_t[:, 2 + (j % 2):3 + (j % 2)]
                    rsj = rstd_t[:, 2 + (j % 2):3 + (j % 2)]
                    sk = L + "oss%d" % (j % 2)
                    P.act(lambda e, sx=sx, ssj=ssj: e.activation(out=junk, in_=sx, func=AF.Square, accum_out=ssj), r=[xk], w=["ojunk", sk])
                    P.act(lambda e, ssj=ssj, rsj=rsj: e.activation(out=rsj, in_=ssj, func=AF.Ln, bias=EPS, scale=1.0 / D), r=[sk], w=[sk + "r"])
                    P.act(lambda e, rsj=rsj: e.activation(out=rsj, in_=rsj, func=AF.Exp, scale=-0.5), r=[sk + "r"], w=[sk + "r"])
                    P.dve(lambda e, sx=sx, rsj=rsj: e.scalar_tensor_tensor(out=sx, in0=sx, scalar=rsj, in1=gate_c, op0=ALU.mult, op1=ALU.mult),
                          r=[xk, sk + "r", L_ + "fng0", L_ + "fng1"], w=[xk])
                    P.dma(lambda e, sx=sx, j=j: e.dma_start(out=out_d[(j - 2) * 128:(j - 1) * 128, :], in_=sx), r=[xk], w=["out_%d" % (j - 2)])

        A.at(0)
        Oall = A.bf16(8 * NT).rearrange("p (h t) -> p h t", t=NT)
        hT = A.bf16(8 * NT).rearrange("p (k t) -> p k t", t=NT)
        heads_off = A.off
        bct, mkeys = phase_mod(0, heads_off + 8768, heads_off)
        phase_norm(0, bct, mkeys, hT,
                   lambda j: ((ctx_d[j * 128:(j + 1) * 128, :], []) if j < 2 else (x_d[(j - 2) * 128:(j - 1) * 128, :], [])),
                   permute_lat=False)
        P.barrier()

        A.at(heads_off)
        Qd = {"f": A.bf16(NT), "b": A.bf16(NT)}
        Kd = {"f": A.bf16(NT), "b": A.bf16(NT)}
        vT = A.bf16(NCH * 128).rearrange("p (n v) -> p n v", v=128)
        KT = {"f": A.bf16(NCH * 128).rearrange("p (n v) -> p n v", v=128),
              "b": A.bf16(NCH * 128).rearrange("p (n v) -> p n v", v=128)}
        Gs = [A.bf16(NT), A.bf16(NT)]
        Sbf = {"f": A.bf16(NCH * 128).rearrange("p (n v) -> p n v", v=128),
               "b": A.bf16(NCH * 128).rearrange("p (n v) -> p n v", v=128)}
        Tst = {"f": [A.f32(128), A.f32(128)], "b": [A.f32(128), A.f32(128)]}
        q32s = [A.f32(512) for _ in range(2)]
        sgts = [{"f": A.f32(512), "b": A.f32(512)} for _ in range(2)]
        lgts = [{"f": A.f32(512), "b": A.f32(512)} for _ in range(2)]
        pfts = [{"f": A.f32(512), "b": A.f32(512)} for _ in range(2)]
        e1s = [A.f32(512) for _ in range(2)]
        e2s = [A.f32(512) for _ in range(2)]
        vfms = [A.bf16(512) for _ in range(2)]
        tot4s = [small[:, 318:322], small[:, 524:528]]
        wbuf = [A.bf16(8 * 640).rearrange("p (k n) -> p k n", n=640) for _ in range(2)]
        Ps = {"f": [A.bf16(128), A.bf16(128)], "b": [A.bf16(128), A.bf16(128)]}
        osq = A.bf16(512)
        o32 = A.f32(512)
        rt = A.f32(512)
        l0_end = A.off
        global DBG_OFFS
        DBG_OFFS = dict(heads_off=heads_off, l0_end=l0_end)

        TBS = [(0, 512), (512, 512), (1024, 512), (1536, 512), (2048, 256)]
        ORDER = {"f": list(range(18)), "b": [1, 0] + list(range(17, 1, -1))}
        Av = {"f": AFv, "b": ABv}
        Bv = {"f": BFv, "b": BBv}
        Rv = {"f": RFv, "b": RBv}
        NH = 16 if stop_after is None else stop_after

        def emit_pb1(h, bi):
            s0, sz = TBS[bi]
            ncb = sz // 128
            n0 = s0 // 128
            ob, okey = pbanks[5], ["pb5s%d" % q_ for q_ in range(4)]
            for c2_ in range(0, ncb, 2):
                bk, bkeys = pbanks[6], ["pb6s%d" % q_ for q_ in range(4)]
                for cc in range(2):
                    n = n0 + c2_ + cc
                    cs = slice(n * 128, (n + 1) * 128)
                    c0_ = n * 128
                    for di, d in enumerate(("f", "b")):
                        sl = cc * 2 + di
                        o0 = sl * 128
                        rk_ = ["K%s%d" % (d, bi), "Q%s%d" % (d, bi)]
                        if d == "f":
                            P.pe(lambda e, bk=bk, o0=o0, c0_=c0_: e.matmul(bk[:, o0 + 64:o0 + 128], lhsT=Kd["f"][:, c0_:c0_ + 128], rhs=Qd["f"][:, c0_ + 64:c0_ + 128],
                                                                           start=True, stop=True), r=rk_, w=bkeys)
                            P.pe(lambda e, bk=bk, o0=o0, c0_=c0_: e.matmul(bk[0:64, o0:o0 + 64], lhsT=Kd["f"][:, c0_:c0_ + 64], rhs=Qd["f"][:, c0_:c0_ + 64],
                                                                           start=True, stop=True), r=rk_, w=bkeys)
                        else:
                            P.pe(lambda e, bk=bk, o0=o0, c0_=c0_: e.matmul(bk[:, o0:o0 + 64], lhsT=Kd["b"][:, c0_:c0_ + 128], rhs=Qd["b"][:, c0_:c0_ + 64],
                                                                           start=True, stop=True), r=rk_, w=bkeys)
                            P.pe(lambda e, bk=bk, o0=o0, c0_=c0_: e.matmul(bk[64:128, o0 + 64:o0 + 128], lhsT=Kd["b"][:, c0_ + 64:c0_ + 128], rhs=Qd["b"][:, c0_ + 64:c0_ + 128],
                                                                           start=True, stop=True), r=rk_, w=bkeys)
                for cc in range(2):
                    n = n0 + c2_ + cc
                    for di, d in enumerate(("f", "b")):
                        sl = cc * 2 + di
                        mk, mkk = MASKS[d]
                        pst = Ps[d][n % 2]
                        pkey = "Ps%s%d" % (d, n % 2)
                        P.dve(lambda e, bk=bk, sl=sl, mk=mk, pst=pst: e.tensor_tensor(out=pst, in0=bk[:, sl * 128:(sl + 1) * 128], in1=mk[:], op=ALU.mult),
                              r=bkeys + [mkk], w=[pkey])
                for cc in range(2):
                    c = c2_ + cc
                    n = n0 + c
                    cs = slice(n * 128, (n + 1) * 128)
                    oc = ob[:, c * 128:(c + 1) * 128]
                    P.pe(lambda e, oc=oc, n=n: e.matmul(oc, lhsT=vT[:, n, :], rhs=Ps["f"][n % 2], start=True, stop=False),
                         r=["Tv%d" % (n // 4), "Psf%d" % (n % 2)], w=okey)
                    P.pe(lambda e, oc=oc, n=n: e.matmul(oc, lhsT=vT[:, n, :], rhs=Ps["b"][n % 2], start=False, stop=False),
                         r=["Tv%d" % (n // 4), "Psb%d" % (n % 2)], w=okey)
                    P.pe(lambda e, oc=oc, n=n, cs=cs: e.matmul(oc, lhsT=Sbf["f"][:, n, :], rhs=Qd["f"][:, cs], start=False, stop=False),
                         r=["Sbff%d" % n, "Qf%d" % bi], w=okey)
                    P.pe(lambda e, oc=oc, n=n, cs=cs: e.matmul(oc, lhsT=Sbf["b"][:, n, :], rhs=Qd["b"][:, cs], start=False, stop=True),
                         r=["Sbfb%d" % n, "Qb%d" % bi], w=okey)
            return ob, okey

        def emit_pb2(h, bi, ob, okey):
            s0, sz = TBS[bi]
            P.act(lambda e, ob=ob, sz=sz: e.activation(out=osq[:, 0:sz], in_=ob[:, 0:sz], func=AF.Square), r=okey, w=["osq"])
            P.act(lambda e, ob=ob, sz=sz: e.copy(out=o32[:, 0:sz], in_=ob[:, 0:sz]), r=okey, w=["o32"])
            sb_, sskey = pbanks[6], ["pb6s%d" % q_ for q_ in range(4)]
            P.pe(lambda e, sb_=sb_, sz=sz: e.matmul(sb_[:, 0:sz], lhsT=ones_bf[:], rhs=osq[:, 0:sz], start=True, stop=True), r=["osq", "ones_bf"], w=sskey)
            P.act(lambda e, sb_=sb_, sz=sz: e.activation(out=rt[:, 0:sz], in_=sb_[:, 0:sz], func=AF.Ln, bias=EPS, scale=1.0 / 128), r=sskey, w=["rt"])
            P.act(lambda e, sz=sz: e.activation(out=rt[:, 0:sz], in_=rt[:, 0:sz], func=AF.Exp, scale=-0.5), r=["rt"], w=["rt"])
            P.dve(lambda e, sz=sz, h=h: e.scalar_tensor_tensor(out=o32[:, 0:sz], in0=o32[:, 0:sz], scalar=ngv[:, h:h + 1], in1=rt[:, 0:sz], op0=ALU.mult, op1=ALU.mult),
                  r=["o32", "rt", "hgv"], w=["o32"])
            P.dve(lambda e, s0=s0, sz=sz, h=h: e.tensor_tensor(out=Oall[:, h % 8, s0:s0 + sz], in0=o32[:, 0:sz], in1=Gs[h % 2][:, s0:s0 + sz], op=ALU.mult),
                  r=["o32", "G%d_%d" % (h % 2, bi)], w=["O%d_%d" % (h, bi)])

        def emit_phaseB(h, bi):
            ob, okey = emit_pb1(h, bi)
            emit_pb2(h, bi, ob, okey)

        def out_pass(ec0, from_x1, tag):
            P.barrier()
            A.at(heads_off)
            blk = lambda j: min(j // 4, 4)
            phase_out(0, Oall, lambda j, ec: ["O%d_%d" % (ec0 + ec, blk(j))], hgwo_d, final=False, ec0=ec0, nec=8, from_x1=from_x1, tag=tag)
            P.barrier()

        for h in range(NH):
            H = "h%d" % h
            wb = wbuf[h % 2]
            if h == 8:
                for bi in range(5):
                    emit_phaseB(7, bi)
                out_pass(0, False, "a")
            wk = "wbuf%d" % (h % 2)
            if h == 0 or NO_PREFETCH:
                P.dma(lambda e, wb=wb, h=h: e.dma_start(out=wb, in_=hgw_d[h]), w=[wk], q="pool")
            if h + 1 < NH and not NO_PREFETCH:
                P.dma(lambda e, wbn=wbuf[(h + 1) % 2], h=h: e.dma_start(out=wbn, in_=hgw_d[h + 1]), w=["wbuf%d" % ((h + 1) % 2)], q="pool")
            c0_h = c0v[:, h:h + 1]
            c1_h = c1v[:, h:h + 1]

            def emit_proj(bi, wb=wb, wk=wk):
                s0, sz = TBS[bi]
                hk = hT_keys(0, s0, s0 + sz, False)
                banks = []
                for jj in range(5):
                    bk, bkey = bank_proj()
                    banks.append((bk, bkey))
                    for kc in range(8):
                        P.pe(lambda e, bk=bk, wb=wb, kc=kc, jj=jj, s0=s0, sz=sz: e.matmul(
                            bk[:, 0:sz], lhsT=wb[:, kc, jj * 128:(jj + 1) * 128], rhs=hT[:, kc, s0:s0 + sz],
                            start=(kc == 0), stop=(kc == 7)), r=[wk] + hk, w=bkey)
                return banks

            def emit_evac(bi, banks):
                s0, sz = TBS[bi]
                p = bi % 2
                K = lambda nm: nm + "_%d" % p
                q32 = q32s[p]; sgt = sgts[p]; lgt = lgts[p]; pft = pfts[p]
                e1t = {"f": e1s[p], "b": e1s[p]}; e2t = {"f": e2s[p], "b": e2s[p]}
                vfm = vfms[p]; tot4 = tot4s[p]
                (bq, kq), (bv, kv), (bzf, kzf), (bzb, kzb), (bg, kg) = banks
                bz = {"f": (bzf, kzf), "b": (bzb, kzb)}
                P.act(lambda e, bq=bq, sz=sz: e.activation(out=q32[:, 0:sz], in_=bq[:, 0:sz], func=AF.Silu), r=kq, w=[K("q32")])
                P.act(lambda e, bg=bg, s0=s0, sz=sz, Gh=Gs[h % 2]: e.activation(out=Gh[:, s0:s0 + sz], in_=bg[:, 0:sz], func=AF.Silu), r=kg, w=["G%d_%d" % (h % 2, bi)])
                P.dve(lambda e, bv=bv, sz=sz: e.tensor_copy(out=vfm[:, 0:sz], in_=bv[:, 0:sz]), r=kv, w=[K("vfm")])
                for d in ("f", "b"):
                    bzd, kzd = bz[d]
                    sg = sgt[d]
                    P.act(lambda e, bzd=bzd, sg=sg, sz=sz: e.activation(out=sg[:, 0:sz], in_=bzd[:, 0:sz], func=AF.Tanh, scale=0.5), r=kzd, w=[K("sg" + d)])

            def emit_rest1(bi, c0_h=c0_h, c1_h=c1_h, H=H):
                s0, sz = TBS[bi]
                p = bi % 2
                K = lambda nm: nm + "_%d" % p
                q32 = q32s[p]; sgt = sgts[p]; lgt = lgts[p]; pft = pfts[p]
                e1t = {"f": e1s[p], "b": e1s[p]}; e2t = {"f": e2s[p], "b": e2s[p]}
                vfm = vfms[p]; tot4 = tot4s[p]
                ncb = sz // 128
                n0 = s0 // 128
                for d in ("f", "b"):
                    sg, lg, pf = sgt[d], lgt[d], pft[d]
                    P.dve(lambda e, sg=sg, sz=sz: e.tensor_scalar(out=sg[:, 0:sz], in0=sg[:, 0:sz], scalar1=c1_h, scalar2=c0_h, op0=ALU.mult, op1=ALU.add),
                          r=[K("sg" + d), "c0v", "c1v"], w=[K("sg" + d)])
                    P.act(lambda e, sg=sg, lg=lg, sz=sz: e.activation(out=lg[:, 0:sz], in_=sg[:, 0:sz], func=AF.Ln), r=[K("sg" + d)], w=[K("lg" + d)])
                    P.pool(lambda e, sg=sg, sz=sz: e.tensor_scalar(out=sg[:, 0:sz], in0=sg[:, 0:sz], scalar1=-1.0, scalar2=1.0, op0=ALU.mult, op1=ALU.add),
                           r=[K("sg" + d)], w=[K("sg" + d)])
                lg, pf = lgt["f"], pft["f"]
                P.dve(lambda e, lg=lg, pf=pf, sz=sz: e.tensor_tensor_scan(out=pf[:, 0:sz], data0=smask[:, 0:sz], data1=lg[:, 0:sz], initial=0.0,
                                                                        op0=ALU.mult, op1=ALU.add), r=[K("lgf"), "smask"], w=[K("pff")])
                lg, pf = lgt["b"], pft["b"]
                P.pool(lambda e, pf=pf: e.memset(pf[:, 0:1], 0.0), w=[K("pfb0")])
                P.dve(lambda e, lg=lg, pf=pf, sz=sz: e.tensor_tensor_scan(out=pf[:, 1:sz], data0=lg[:, 0:sz - 1], data1=smask[:, 1:sz], initial=0.0,
                                                                        op0=ALU.add, op1=ALU.mult), r=[K("lgb"), "smask"], w=[K("pfb")])
                pf3 = pft["f"][:, 0:sz].rearrange("p (c j) -> p c j", j=128)
                pb3 = pft["b"][:, 0:sz].rearrange("p (c j) -> p c j", j=128)
                lb3 = lgt["b"][:, 0:sz].rearrange("p (c j) -> p c j", j=128)
                ex = H + "ex%d" % bi
                P.pool(lambda e, pf3=pf3, n0=n0, ncb=ncb: e.tensor_copy(out=AFv[:, n0:n0 + ncb], in_=pf3[:, :, MID_F]), r=[K("pff")], w=[ex + "AF"])
                P.pool(lambda e, pf3=pf3, n0=n0, ncb=ncb: e.tensor_tensor(out=BFv[:, n0:n0 + ncb], in0=pf3[:, :, 127], in1=pf3[:, :, MID_F], op=ALU.subtract),
                       r=[K("pff")], w=[ex + "BF"])
                P.pool(lambda e, pb3=pb3, lb3=lb3, ncb=ncb: e.tensor_tensor(out=tot4[:, 0:ncb], in0=pb3[:, :, 127], in1=lb3[:, :, 127], op=ALU.add),
                       r=[K("pfb"), K("pfb0"), K("lgb")], w=[K("tot4")])
                P.pool(lambda e, pb3=pb3, n0=n0, ncb=ncb: e.tensor_tensor(out=ABv[:, n0:n0 + ncb], in0=tot4[:, 0:ncb], in1=pb3[:, :, MID_B], op=ALU.subtract),
                       r=[K("tot4"), K("pfb")], w=[ex + "AB"])
                P.pool(lambda e, pb3=pb3, n0=n0, ncb=ncb: e.tensor_copy(out=BBv[:, n0:n0 + ncb], in_=pb3[:, :, MID_B]), r=[K("pfb")], w=[ex + "BB"])
                for d, mid, rk in (("f", MID_F, [K("pff")]), ("b", MID_B, [K("pfb"), K("pfb0")])):
                    p3 = pft[d][:, 0:sz].rearrange("p (c j) -> p c j", j=128)
                    l3 = lgt[d][:, 0:sz].rearrange("p (c j) -> p c j", j=128)
                    P.dve(lambda e, p3=p3, l3=l3, mid=mid, ncb=ncb: e.tensor_tensor(out=l3, in0=p3, in1=p3[:, :, mid:mid + 1].to_broadcast([128, ncb, 128]),
                                                                                    op=ALU.subtract), r=rk + [K("lg" + d)], w=[K("lg" + d)])

            def emit_rest2(bi):
                s0, sz = TBS[bi]
                p = bi % 2
                K = lambda nm: nm + "_%d" % p
                q32 = q32s[p]; sgt = sgts[p]; lgt = lgts[p]; pft = pfts[p]
                e1t = {"f": e1s[p], "b": e1s[p]}; e2t = {"f": e2s[p], "b": e2s[p]}
                vfm = vfms[p]; tot4 = tot4s[p]
                for d, sq, sk_ in (("f", 1.0, -1.0), ("b", -1.0, 1.0)):
                    lg, e1, e2, sg = lgt[d], e1t[d], e2t[d], sgt[d]
                    P.act(lambda e, lg=lg, e1=e1, sq=sq, sz=sz: e.activation(out=e1[:, 0:sz], in_=lg[:, 0:sz], func=AF.Exp, scale=sq), r=[K("lg" + d)], w=[K("e1")])
                    P.act(lambda e, lg=lg, e2=e2, sk_=sk_, sz=sz: e.activation(out=e2[:, 0:sz], in_=lg[:, 0:sz], func=AF.Exp, scale=sk_), r=[K("lg" + d)], w=[K("e2")])
                    P.pool(lambda e, e1=e1, d=d, s0=s0, sz=sz: e.tensor_tensor(out=Qd[d][:, s0:s0 + sz], in0=q32[:, 0:sz], in1=e1[:, 0:sz], op=ALU.mult),
                           r=[K("q32"), K("e1")], w=["Q%s%d" % (d, bi)])
                    P.pool(lambda e, e2=e2, sg=sg, d=d, s0=s0, sz=sz: e.tensor_tensor(out=Kd[d][:, s0:s0 + sz], in0=sg[:, 0:sz], in1=e2[:, 0:sz], op=ALU.mult),
                           r=[K("sg" + d), K("e2")], w=["K%s%d" % (d, bi)])

            def emit_tr(bi):
                s0, sz = TBS[bi]
                p = bi % 2
                K = lambda nm: nm + "_%d" % p
                q32 = q32s[p]; sgt = sgts[p]; lgt = lgts[p]; pft = pfts[p]
                e1t = {"f": e1s[p], "b": e1s[p]}; e2t = {"f": e2s[p], "b": e2s[p]}
                vfm = vfms[p]; tot4 = tot4s[p]
                ncb = sz // 128
                n0 = s0 // 128
                for nm, srcf, rkey, dstT in (("v", lambda c: vfm[:, c * 128:(c + 1) * 128], K("vfm"), vT),
                                             ("kf", lambda c, s0=s0: Kd["f"][:, s0 + c * 128:s0 + (c + 1) * 128], "Kf%d" % bi, KT["f"]),
                                             ("kb", lambda c, s0=s0: Kd["b"][:, s0 + c * 128:s0 + (c + 1) * 128], "Kb%d" % bi, KT["b"])):
                    bk, bkey = pbanks[7], ["pb7s%d" % q_ for q_ in range(4)]
                    bkb = bk[:].bitcast(BF16)
                    for c in range(ncb):
                        src = srcf(c)
                        P.pe(lambda e, bkb=bkb, c=c, src=src: e.transpose(bkb[:, c * 128:(c + 1) * 128], src, ident[:]), r=[rkey, "ident"], w=bkey)
                    dst = dstT[:, n0:n0 + ncb, :]
                    srcv = bkb[:, 0:ncb * 128].rearrange("p (c v) -> p c v", v=128)
                    wkey = "T%s%d" % (nm, bi)
                    P.act(lambda e, dst=dst, srcv=srcv: e.copy(out=dst, in_=srcv), r=bkey, w=[wkey])

            interleave = h > 0 and h != 8
            banks = emit_proj(0)
            pb0 = emit_pb1(h - 1, 0) if interleave else None
            emit_evac(0, banks)
            emit_rest1(0)
            if pb0 is not None:
                emit_pb2(h - 1, 0, pb0[0], pb0[1])
            for bi in range(5):
                emit_rest2(bi)
                pb = None
                if bi + 1 < 5:
                    banks = emit_proj(bi + 1)
                    if interleave:
                        pb = emit_pb1(h - 1, bi + 1)
                    emit_evac(bi + 1, banks)
                emit_tr(bi)
                if bi + 1 < 5:
                    emit_rest1(bi + 1)
                if pb is not None:
                    emit_pb2(h - 1, bi + 1, pb[0], pb[1])
            exk = lambda nm: [H + "ex%d" % bi + nm for bi in range(5)]
            P.pool(lambda e: e.tensor_tensor(out=TRv[:, 1:18], in0=AFv[:, 1:18], in1=BFv[:, 0:17], op=ALU.add), r=exk("AF") + exk("BF"), w=["TRv"])
            P.act(lambda e: e.activation(out=RFv[:, 1:18], in_=TRv[:, 1:18], func=AF.Exp), r=["TRv"], w=["RF"])
            P.pool(lambda e: e.tensor_tensor(out=TRv[:, 0:17], in0=ABv[:, 0:17], in1=BBv[:, 1:18], op=ALU.add), r=exk("AB") + exk("BB") + ["RF"], w=["TRv"])
            P.pool(lambda e: e.tensor_tensor(out=TRv[:, 17:18], in0=ABv[:, 17:18], in1=BBv[:, 0:1], op=ALU.add), r=exk("AB") + exk("BB") + ["TRv"], w=["TRv"])
            P.act(lambda e: e.activation(out=RBv[:, 0:18], in_=TRv[:, 0:18], func=AF.Exp), r=["TRv"], w=["RB"])
            for j2 in range(0, 18, 2):
                bk, bkeys = bank_aux()
                for jj in range(2):
                    j = j2 + jj
                    for di, d in enumerate(("f", "b")):
                        n = ORDER[d][j]
                        sl = jj * 2 + di
                        P.pe(lambda e, bk=bk, sl=sl, n=n, d=d: e.matmul(bk[:, sl * 128:(sl + 1) * 128], lhsT=KT[d][:, n, :], rhs=vT[:, n, :], start=True, stop=True),
                             r=["Tk%s%d" % (d, n // 4), "Tv%d" % (n // 4)], w=bkeys)
                for jj in range(2):
                    j = j2 + jj
                    for di, d in enumerate(("f", "b")):
                        n = ORDER[d][j]
                        sl = jj * 2 + di
                        tcur = Tst[d][j % 2]
                        tprev = Tst[d][(j + 1) % 2]
                        tk, tpk = "Tst%s%d" % (d, j % 2), "Tst%s%d" % (d, (j + 1) % 2)
                        skey = "Sbf%s%d" % (d, n)
                        if j == 0:
                            P.pool(lambda e, d=d, n=n: e.memset(Sbf[d][:, n, :], 0.0), w=[skey])
                            P.dve(lambda e, bk=bk, sl=sl, tcur=tcur: e.tensor_copy(out=tcur, in_=bk[:, sl * 128:(sl + 1) * 128]), r=bkeys, w=[tk])
                        else:
                            rcol = Rv[d][:, n:n + 1]
                            rk = "RF" if d == "f" else "RB"
                            P.pool(lambda e, d=d, n=n, tprev=tprev, rcol=rcol: e.tensor_scalar(out=Sbf[d][:, n, :], in0=tprev, scalar1=rcol, scalar2=1.0, op0=ALU.mult, op1=ALU.mult),
                                   r=[tpk, rk], w=[skey])
                            P.dve(lambda e, bk=bk, sl=sl, tcur=tcur, tprev=tprev, rcol=rcol: e.scalar_tensor_tensor(
                                out=tcur, in0=tprev, scalar=rcol, in1=bk[:, sl * 128:(sl + 1) * 128], op0=ALU.mult, op1=ALU.add),
                                r=[tpk, rk] + bkeys, w=[tk])
            if HEAD_BARRIER:
                P.barrier()

        for bi in range(5):
            emit_phaseB(NH - 1, bi)
        if NH <= 8:
            if NH < 8:
                P.pool(lambda e: e.memset(Oall[:, NH:8, :], 0.0), w=["O%d_%d" % (hh_, b_) for hh_ in range(NH, 8) for b_ in range(5)])
            out_pass(0, False, "a")
        else:
            out_pass(8, True, "b")

        final_keys = []
        if stop_after is None:
            A.at(0)
            hT1 = A.bf16(8 * NT).rearrange("p (k t) -> p k t", t=NT)
            O1 = A.bf16(16 * 2048).rearrange("p (h t) -> p h t", t=2048)
            l1_off = A.off
            bct, mkeys = phase_mod(1, 9216, 26624)
            phase_norm(1, bct, mkeys, hT1,
                       lambda j: ((ctx1_d[j * 128:(j + 1) * 128, :], ["ctx1_%d" % j]) if j < 2
                                  else (x1_d[(j - 2) * 128:(j - 1) * 128, :], ["x1_%d" % (j - 2)])),
                       permute_lat=True)
            P.barrier()
            A.at(l1_off)
            xr_c = A.f32(260)
            xr_l = A.f32(2052)
            xc = A.f32(2 * NT).rearrange("p (a t) -> p a t", t=NT)
            xcb = A.bf16(2 * NT).rearrange("p (a t) -> p a t", t=NT)
            gsl = A.bf16(2 * 2048).rearrange("p (a t) -> p a t", t=2048)
            hf = A.f32(2048)
            hctx = A.f32(256)
            rgwb = [A.bf16(2 * 8 * 256).rearrange("p (a k n) -> p a k n", k=8, n=256)]
            axb = [A.bf16(8 * 256).rearrange("p (q n) -> p q n", n=256)]
            tr_ = A.f32(512); tig = A.f32(512); ta = A.f32(512); ta2 = A.f32(512); tth = A.f32(512)
            tu = A.f32(512); thb = A.f32(512)
            s2all = A.f32(NT)
            l1_end = A.off
            P.pool(lambda e: e.memset(xr_c[:, :], 0.0), w=["xr_c"])
            P.pool(lambda e: e.memset(xr_l[:, :], 0.0), w=["xr_l"])
            LB = [(0, 256), (256, 512), (768, 512), (1280, 512), (1792, 512)]
            for hh in range(8):
                wb = rgwb[0]
                wk = "rgwb0"
                ab = axb[0]
                ak = "axb0"
                if hh == 0:
                    P.dma(lambda e, wb=wb, hh=hh: e.dma_start(out=wb, in_=rgw_d[hh]), w=[wk], q="pool")
                if hh == 0:
                    P.dma(lambda e, ab=ab, hh=hh: e.dma_start(out=ab, in_=rgax_d[hh]), w=[ak], q="pool")
                for a in range(2):
                    ct = 2 * hh + a
                    cw = rgv[:, ct, 0:4]
                    cb = rgv[:, ct, 4:5]
                    for bi, (s0, sz) in enumerate(LB):
                        hk = hT_keys(1, s0, s0 + sz, True)
                        bx, kx = newbank()
                        bg, kg = newbank()
                        for (bk, bkey, co) in ((bx, kx, 0), (bg, kg, 128)):
                            if co == 128 and bi == 0:
                                continue
                            for kc in range(8):
                                P.pe(lambda e, bk=bk, wb=wb, a=a, kc=kc, co=co, s0=s0, sz=sz: e.matmul(
                                    bk[:, 0:sz], lhsT=wb[:, a, kc, co:co + 128], rhs=hT1[:, kc, s0:s0 + sz],
                                    start=(kc == 0), stop=(kc == 7)), r=[wk] + hk, w=bkey)
                        if bi == 0:
                            P.act(lambda e, bx=bx: e.copy(out=xr_c[:, 2:258], in_=bx[:, 0:256]), r=kx, w=["xr_c"])
                        else:
                            l0 = s0 - 256
                            P.act(lambda e, bx=bx, l0=l0: e.copy(out=xr_l[:, 2 + l0:2 + l0 + 512], in_=bx[:, 0:512]), r=kx, w=["xr_l"])
                            P.act(lambda e, bg=bg: e.activation(out=tth[:, 0:512], in_=bg[:, 0:512], func=AF.Tanh, scale=0.5), r=kg, w=["tth"])
                            P.dve(lambda e, bg=bg, a=a, l0=l0: e.scalar_tensor_tensor(out=gsl[:, a, l0:l0 + 512], in0=tth[:, 0:512], scalar=1.0, in1=bg[:, 0:512],
                                                                                    op0=ALU.add, op1=ALU.mult), r=kg + ["tth"], w=["gsl%d" % a])
                    for (xr, xk, c0, ln) in ((xr_c, "xr_c", 0, 256), (xr_l, "xr_l", 256, 2048)):
                        dst = xc[:, a, c0:c0 + ln]
                        ck = "xc%d_%d" % (a, 0 if c0 == 0 else 1)
                        P.dve(lambda e, xr=xr, dst=dst, ln=ln, cw=cw, cb=cb: e.tensor_scalar(out=dst, in0=xr[:, 2:2 + ln], scalar1=cw[:, 2:3], scalar2=cb, op0=ALU.mult, op1=ALU.add),
                              r=[xk, "rgv"], w=[ck])
                        for tap, off in ((0, 0), (1, 1), (3, 3)):
                            P.dve(lambda e, xr=xr, dst=dst, ln=ln, cw=cw, tap=tap, off=off: e.scalar_tensor_tensor(
                                out=dst, in0=xr[:, off:off + ln], scalar=cw[:, tap:tap + 1], in1=dst, op0=ALU.mult, op1=ALU.add),
                                r=[xk, "rgv", ck], w=[ck])
                        P.pool(lambda e, dst=dst, a=a, c0=c0, ln=ln: e.tensor_copy(out=xcb[:, a, c0:c0 + ln], in_=dst), r=[ck], w=["xcb%d_%d" % (a, 0 if c0 == 0 else 1)])
                if hh + 1 < 8:
                    P.dma(lambda e, wb=wb, hh=hh: e.dma_start(out=wb, in_=rgw_d[hh + 1]), w=[wk], q="pool")
                xcbk = lambda bi: ["xcb0_%d" % (0 if bi == 0 else 1), "xcb1_%d" % (0 if bi == 0 else 1)]
                for ao in range(2):
                    ct = 2 * hh + ao
                    for d in (0, 1):
                        order = list(range(5)) if d == 0 else [0, 4, 3, 2, 1]
                        prev_last = None
                        hb_a = rgvh[:, ct, d:d + 1]
                        hb_x = rgvh[:, ct, 2 + d:3 + d]
                        cl = clv[:, ct, d:d + 1]
                        clq = clqv[:, ct, d:d + 1]
                        chh = chhv[:, ct, d:d + 1]

                        def emit_coef(bi, both, ao=ao, d=d, ab=ab, ak=ak):
                            s0, sz = LB[bi]
                            res = []
                            for axi in ((0, 1) if both else (0,)):
                                bk, bkey = newbank()
                                for ic in range(2):
                                    q = (axi * 2 + d) * 2 + ic
                                    P.pe(lambda e, bk=bk, ab=ab, q=q, ao=ao, ic=ic, s0=s0, sz=sz: e.matmul(
                                        bk[:, 0:sz], lhsT=ab[:, q, ao * 128:(ao + 1) * 128], rhs=xcb[:, ic, s0:s0 + sz],
                                        start=(ic == 0), stop=(ic == 1)), r=[ak] + xcbk(bi), w=bkey)
                                res.append((bk, bkey))
                            return res

                        nxt = emit_coef(0, False)
                        for bi in range(5):
                            s0, sz = LB[bi]
                            (ba, ka), = nxt
                            P.act(lambda e, ba=ba, sz=sz, hb_a=hb_a: e.activation(out=tr_[:, 0:sz], in_=ba[:, 0:sz], func=AF.Tanh, bias=hb_a, scale=0.5), r=ka + ["rgvh"], w=["tr"])
                            if bi + 1 < 5:
                                nxt = emit_coef(bi + 1, False)
                            P.act(lambda e, sz=sz, chh=chh: e.activation(out=tth[:, 0:sz], in_=tr_[:, 0:sz], func=AF.Tanh, bias=chh, scale=chh), r=["tr", "chhv"], w=["tth"])
                            P.act(lambda e, sz=sz, cl=cl: e.activation(out=ta2[:, 0:sz], in_=tr_[:, 0:sz], func=AF.Exp, bias=cl, scale=cl), r=["tr", "clv"], w=["ta2"])
                            P.dve(lambda e, sz=sz, s0=s0: e.scalar_tensor_tensor(out=s2all[:, s0:s0 + sz], in0=ta2[:, 0:sz], scalar=1.0, in1=tth[:, 0:sz], op0=ALU.add, op1=ALU.mult),
                                  r=["ta2", "tth"], w=["s2_%d" % bi])
                        s2k = ["s2_%d" % b_ for b_ in range(5)]
                        P.act(lambda e: e.activation(out=s2all[:, :], in_=s2all[:, :], func=AF.Sqrt), r=s2k, w=s2k)
                        nxt = emit_coef(order[0], True)
                        for step, bi in enumerate(order):
                            s0, sz = LB[bi]
                            (ba, ka), (bx, kx) = nxt
                            P.act(lambda e, ba=ba, sz=sz, hb_a=hb_a: e.activation(out=tr_[:, 0:sz], in_=ba[:, 0:sz], func=AF.Tanh, bias=hb_a, scale=0.5), r=ka + ["rgvh"], w=["tr"])
                            P.act(lambda e, bx=bx, sz=sz, hb_x=hb_x: e.activation(out=tig[:, 0:sz], in_=bx[:, 0:sz], func=AF.Tanh, bias=hb_x, scale=0.5), r=kx + ["rgvh"], w=["tig"])
                            if step + 1 < len(order):
                                nxt = emit_coef(order[step + 1], True)
                            P.act(lambda e, sz=sz, clq=clq: e.activation(out=ta[:, 0:sz], in_=tr_[:, 0:sz], func=AF.Exp, bias=clq, scale=clq), r=["tr", "clqv"], w=["ta"])
                            P.dve(lambda e, sz=sz, ao=ao, s0=s0: e.scalar_tensor_tensor(out=tu[:, 0:sz], in0=tig[:, 0:sz], scalar=1.0, in1=xc[:, ao, s0:s0 + sz],
                                                                                       op0=ALU.add, op1=ALU.mult),
                                  r=["tig", "xc%d_%d" % (ao, 0 if bi == 0 else 1)], w=["tu"])
                            P.pool(lambda e, sz=sz, s0=s0: e.tensor_tensor(out=tu[:, 0:sz], in0=tu[:, 0:sz], in1=s2all[:, s0:s0 + sz], op=ALU.mult), r=["tu", "s2_%d" % bi], w=["tu"])
                            if d == 0:
                                dst = hctx[:, 0:256] if bi == 0 else hf[:, s0 - 256:s0 - 256 + sz]
                                dkey = "hctx" if bi == 0 else "hf%d" % bi
                                init = 0.0 if step == 0 else prev_last
                                P.dve(lambda e, dst=dst, sz=sz, init=init: e.tensor_tensor_scan(out=dst, data0=ta[:, 0:sz], data1=tu[:, 0:sz], initial=init,
                                                                                             op0=ALU.mult, op1=ALU.add),
                                      r=["ta", "tu"] + ([] if step == 0 else [pkey_prev]), w=[dkey])
                                prev_last = dst[:, sz - 1:sz]
                                pkey_prev = dkey
                            else:
                                dst = hctx[:, 0:256] if bi == 0 else thb[:, 0:sz]
                                dkey = "hctx" if bi == 0 else "thb"
                                init = 0.0 if step == 0 else prev_last
                                rk = ["ta", "tu"] + ([] if step == 0 else ["carry"])
                                P.dve(lambda e, dst=dst, sz=sz, init=init: e.tensor_tensor_scan(out=dst[:, ::-1], data0=ta[:, 0:sz][:, ::-1], data1=tu[:, 0:sz][:, ::-1],
                                                                                             initial=init, op0=ALU.mult, op1=ALU.add), r=rk, w=[dkey])
                                cc = carry[:, 0:1]
                                P.pool(lambda e, dst=dst, cc=cc: e.tensor_copy(out=cc, in_=dst[:, 0:1]), r=[dkey], w=["carry"])
                                prev_last = cc
                                if bi > 0:
                                    l0 = s0 - 256
                                    P.pool(lambda e, sz=sz, l0=l0: e.tensor_tensor(out=thb[:, 0:sz], in0=thb[:, 0:sz], in1=hf[:, l0:l0 + sz], op=ALU.add),
                                           r=["thb", "carry", "hf%d" % bi], w=["thb"])
                                    w0 = l0 // 32
                                    dsto = O1[:, ct, :].rearrange("p (r w) -> p w r", w=64)[:, w0:w0 + 16, :]
                                    srcs = thb[:, 0:512].rearrange("p (w r) -> p w r", r=32)
                                    srcg = gsl[:, ao, l0:l0 + 512].rearrange("p (w r) -> p w r", r=32)
                                    P.dve(lambda e, dsto=dsto, srcs=srcs, srcg=srcg: e.scalar_tensor_tensor(out=dsto, in0=srcs, scalar=0.25, in1=srcg, op0=ALU.mult, op1=ALU.mult),
                                          r=["thb", "gsl%d" % ao], w=["O1_%d" % ct])
                if hh + 1 < 8:
                    P.dma(lambda e, ab=ab, hh=hh: e.dma_start(out=ab, in_=rgax_d[hh + 1]), w=[ak], q="pool")
            P.barrier()
            A.at(l1_off)
            phase_out(1, O1, lambda j, ec: ["O1_%d" % ec], rgwo_d, final=True)
            final_keys = ["out_%d" % j for j in range(16)]
        else:
            final_keys = ["x1_%d" % j for j in range(16)] + ["ctx1_0", "ctx1_1"]
        P.finalize(final_keys)
        print("ops:", P.n_ops, "arena L0 end:", l0_end, "of", AW)
    return nc


def prep_inputs(inputs):
    f = lambda a: np.ascontiguousarray(np.asarray(a, dtype=np.float32))
    x, c, ctx, c_ctx = f(inputs["x"]), f(inputs["c"]), f(inputs["ctx"]), f(inputs["c_ctx"])
    ada_w, ada_b = f(inputs["ada_w"]), f(inputs["ada_b"])
    adaw = np.stack([ada_w[i].reshape(8, 128, 6, 512).transpose(2, 1, 0, 3) for i in range(2)], 0)
    rows = np.concatenate([ada_b[0], ada_b[1], f(inputs["norm_g"])[0], f(inputs["norm_g"])[1], f(inputs["final_norm_g"])])[None, :]
    hw = f(inputs["hg_w_in"])[0].reshape(8, 128, 5, 16, 128)
    hgw = hw.transpose(3, 1, 0, 2, 4).reshape(16, 128, 8, 640)
    hgwo = f(inputs["hg_w_out"])[0].reshape(16, 128, D).transpose(1, 0, 2)
    lbr = f(inputs["hg_lower_bounds"]).reshape(3, 16, 128)
    hgn = f(inputs["hg_norm_g"])[0].reshape(1, 16, 128)
    hgvec = np.concatenate([lbr, hgn], 0).transpose(2, 0, 1)
    rw = f(inputs["rg_w_in"])[0].reshape(8, 128, 2, 8, 2, 128)
    rgw = rw.transpose(3, 1, 4, 0, 2, 5).reshape(8, 128, 2, 8, 256)
    wa, wx = f(inputs["rg_w_a"])[0], f(inputs["rg_w_x"])[0]
    wax = np.stack([wa, wx], 0).reshape(2, 2, 8, 2, 128, 256)
    rgax = wax.transpose(2, 4, 0, 1, 3, 5).reshape(8, 128, 8, 256)
    rgwo = f(inputs["rg_w_out"])[0].reshape(16, 128, D).transpose(1, 0, 2)
    cw = f(inputs["rg_conv_w"])[0].reshape(4, 16, 128)
    cb = f(inputs["rg_conv_b"])[0].reshape(1, 16, 128)
    ba = f(inputs["rg_b_a"])[0].reshape(2, 16, 128)
    bx = f(inputs["rg_b_x"])[0].reshape(2, 16, 128)
    lam = f(inputs["rg_lambda"])[0].reshape(2, 16, 128)
    rgvec = np.concatenate([cw, cb, ba, bx, lam], 0).transpose(2, 1, 0)
    shared = dict(adaw=f(adaw), rows=f(rows), hgw=f(hgw), hgwo=f(hgwo), hgvec=f(hgvec), rgw=f(rgw), rgax=f(rgax),
                  rgwo=f(rgwo), rgvec=f(rgvec))
    maps = []
    for b in range(8):
        c2 = np.concatenate([c[b].reshape(8, 128).T, c_ctx.reshape(8, 128).T], 1)
        m = dict(shared)
        m.update(x=f(x[b]), ctx=f(ctx[b]), c2=f(c2))
        maps.append(m)
    return maps


def kernel(**inputs):
    maps = prep_inputs(inputs)
    nc = build_nc()
    res = run_bass_kernel_spmd(nc, maps, core_ids=list(range(8)))
    return np.stack([np.asarray(r["out"], dtype=np.float32) for r in res.results], 0)
```
